# Optimizing a Trainium2 kernel written in Bass

```python
import math
import jax, jax.numpy as jnp
from jax import lax
import numpy as np

D_MODEL = 1024
BATCH = 16
SEQ = 2048
DEPTH = 2
DEC_BATCH = 128
DEC_SEQ = 4
PAST_LEN = 16384
PAGE_SIZE = 128

N_A_LAYERS = DEPTH // 2
N_B_LAYERS = DEPTH - N_A_LAYERS
GROUP_SIZE = 16
N_GROUPS = D_MODEL // GROUP_SIZE
STATE_N = 64
HEAD_DIM = 64
N_HEADS = D_MODEL // HEAD_DIM
N_KV_HEADS = 4
Q_PER_KV = N_HEADS // N_KV_HEADS
WINDOW = 128
D_FF = 4 * D_MODEL
EPS = 1e-6
DT_MIN = 1e-3
DT_MAX = 1e-1
F32 = jnp.float32

kernel_name = "yoco_s5_swa_sink_decoder_step"


def rmsnorm(x, g):
    xf = x.astype(F32)
    y = xf * lax.rsqrt(jnp.mean(xf * xf, axis=-1, keepdims=True) + EPS)
    return (y * g.astype(F32)).astype(x.dtype)


def alibi_slopes():
    return 2.0 ** (-8.0 * jnp.arange(1, N_HEADS + 1, dtype=F32) / N_HEADS)


def s5_discretize(a_re, a_im, log_dt, b_re, b_im):
    dt = jnp.exp(log_dt.astype(F32))[:, None]
    a_re = a_re.astype(F32)
    a_im = a_im.astype(F32)
    mag = jnp.exp(dt * a_re)
    ab_re = mag * jnp.cos(dt * a_im)
    ab_im = mag * jnp.sin(dt * a_im)
    den = a_re * a_re + a_im * a_im
    nr = ab_re - 1.0
    ni = ab_im
    f_re = (nr * a_re + ni * a_im) / den
    f_im = (ni * a_re - nr * a_im) / den
    b_re = b_re.astype(F32)
    b_im = b_im.astype(F32)
    bb_re = f_re[..., None] * b_re - f_im[..., None] * b_im
    bb_im = f_re[..., None] * b_im + f_im[..., None] * b_re
    return ab_re, ab_im, bb_re, bb_im


def _ssm_combine(e1, e2):
    a1r, a1i, b1r, b1i = e1
    a2r, a2i, b2r, b2i = e2
    ar = a2r * a1r - a2i * a1i
    ai = a2r * a1i + a2i * a1r
    br = a2r * b1r - a2i * b1i + b2r
    bi = a2r * b1i + a2i * b1r + b2i
    return (ar, ai, br, bi)


def s5_mixer(u, h0_re, h0_im, a_re, a_im, log_dt, b_re, b_im, c_re, c_im, d_skip, w_glu):
    bsz, length, _ = u.shape
    uf = u.astype(F32).reshape(bsz, length, N_GROUPS, GROUP_SIZE)
    ab_re, ab_im, bb_re, bb_im = s5_discretize(a_re, a_im, log_dt, b_re, b_im)
    bu_re = jnp.einsum('blgc,gnc->blgn', uf, bb_re)
    bu_im = jnp.einsum('blgc,gnc->blgn', uf, bb_im)
    h0r = h0_re.astype(F32)
    h0i = h0_im.astype(F32)
    bu_re = bu_re.at[:, 0].add(ab_re * h0r - ab_im * h0i)
    bu_im = bu_im.at[:, 0].add(ab_re * h0i + ab_im * h0r)
    a_seq_re = jnp.broadcast_to(ab_re, (1, length, N_GROUPS, STATE_N))
    a_seq_im = jnp.broadcast_to(ab_im, (1, length, N_GROUPS, STATE_N))
    _, _, xr, xi = lax.associative_scan(_ssm_combine, (a_seq_re, a_seq_im, bu_re, bu_im), axis=1)
    y = (jnp.einsum('blgn,gcn->blgc', xr, c_re.astype(F32))
         - jnp.einsum('blgn,gcn->blgc', xi, c_im.astype(F32)))
    y = y.reshape(bsz, length, D_MODEL) + d_skip.astype(F32) * u.astype(F32)
    z = jax.nn.gelu(y).astype(u.dtype) @ w_glu
    gate_in, gate = jnp.split(z, 2, axis=-1)
    out = gate_in * jax.nn.sigmoid(gate)
    return out, xr[:, -1].astype(h0_re.dtype), xi[:, -1].astype(h0_im.dtype)


def sqrelu_mlp(x, w_up, w_down):
    return jnp.square(jax.nn.relu(x @ w_up)) @ w_down


def shared_kv(h, kv_norm, w_kv, k_norm):
    bsz, length, _ = h.shape
    kv = rmsnorm(h, kv_norm) @ w_kv
    k, v = jnp.split(kv, 2, axis=-1)
    k = rmsnorm(k.reshape(bsz, length, N_KV_HEADS, HEAD_DIM), k_norm)
    v = v.reshape(bsz, length, N_KV_HEADS, HEAD_DIM)
    return k, v


def gqa_sink_attention(q, k, v, dist, valid, sinks):
    qg = q.reshape(q.shape[:-2] + (N_KV_HEADS, Q_PER_KV, HEAD_DIM))
    s = jnp.einsum('...qhgd,...khd->...hgqk', qg, k, preferred_element_type=F32) * (HEAD_DIM ** -0.5)
    slopes = alibi_slopes().reshape(N_KV_HEADS, Q_PER_KV, 1, 1)
    s = s - slopes * dist.astype(F32)
    s = jnp.where(valid, s, -jnp.inf)
    sink = jnp.broadcast_to(sinks.astype(F32).reshape(N_KV_HEADS, Q_PER_KV, 1, 1), s.shape[:-1] + (1,))
    p = jax.nn.softmax(jnp.concatenate([s, sink], axis=-1), axis=-1)[..., :-1]
    o = jnp.einsum('...hgqk,...khd->...qhgd', p.astype(v.dtype), v)
    return o.reshape(q.shape)


def banded_prompt_attention(q, k, v, sinks):
    bsz, length = q.shape[:2]
    nb = length // WINDOW
    qb = q.reshape(bsz, nb, WINDOW, N_HEADS, HEAD_DIM)
    kb = k.reshape(bsz, nb, WINDOW, N_KV_HEADS, HEAD_DIM)
    vb = v.reshape(bsz, nb, WINDOW, N_KV_HEADS, HEAD_DIM)
    pad = ((0, 0), (1, 0), (0, 0), (0, 0), (0, 0))
    kk = jnp.concatenate([jnp.pad(kb, pad)[:, :-1], kb], axis=2)
    vv = jnp.concatenate([jnp.pad(vb, pad)[:, :-1], vb], axis=2)
    qi = jnp.arange(WINDOW)[:, None]
    kj = jnp.arange(2 * WINDOW)[None, :]
    dist = qi - kj + WINDOW
    blk = jnp.arange(nb)[:, None, None]
    valid = (dist >= 0) & (dist < WINDOW) & ((blk > 0) | (kj >= WINDOW))
    valid = valid[:, None, None]
    o = gqa_sink_attention(qb, kk, vv, dist, valid, sinks)
    return o.reshape(bsz, length, N_HEADS, HEAD_DIM)


def window_cache_attention(q, k_all, v_all, sinks):
    t_new = q.shape[1]
    qi = jnp.arange(t_new)[:, None]
    kj = jnp.arange(k_all.shape[1])[None, :]
    dist = qi + WINDOW - kj
    valid = (dist >= 0) & (dist < WINDOW)
    return gqa_sink_attention(q, k_all, v_all, dist, valid, sinks)


def _trunk(x, h0_re, h0_im, cache_k, cache_v, norm_mix, norm_ffn, ssm_a_re, ssm_a_im, ssm_log_dt,
           ssm_b_re, ssm_b_im, ssm_c_re, ssm_c_im, ssm_d, ssm_w_glu, kv_norm, w_kv, k_norm,
           w_q, q_norm, attn_sinks, w_o, w_up, w_down):
    is_prompt = cache_k is None
    bsz, length, _ = x.shape
    new_re, new_im = [], []
    k_all = v_all = k_buf = v_buf = None
    for layer in range(DEPTH):
        hn = rmsnorm(x, norm_mix[layer])
        if layer < N_A_LAYERS:
            y, hr, hi = s5_mixer(hn, h0_re[layer], h0_im[layer], ssm_a_re[layer], ssm_a_im[layer],
                                 ssm_log_dt[layer], ssm_b_re[layer], ssm_b_im[layer], ssm_c_re[layer],
                                 ssm_c_im[layer], ssm_d[layer], ssm_w_glu[layer])
            new_re.append(hr)
            new_im.append(hi)
        else:
            b = layer - N_A_LAYERS
            q = rmsnorm((hn @ w_q[b]).reshape(bsz, length, N_HEADS, HEAD_DIM), q_norm[b])
            if is_prompt:
                o = banded_prompt_attention(q, k_all, v_all, attn_sinks[b])
            else:
                o = window_cache_attention(q, k_all, v_all, attn_sinks[b])
            y = o.reshape(bsz, length, N_HEADS * HEAD_DIM) @ w_o[b]
        x = x + y
        x = x + sqrelu_mlp(rmsnorm(x, norm_ffn[layer]), w_up[layer], w_down[layer])
        if layer == N_A_LAYERS - 1:
            k_sh, v_sh = shared_kv(x, kv_norm, w_kv, k_norm)
            if is_prompt:
                k_all, v_all = k_sh, v_sh
            else:
                k_all = jnp.concatenate([cache_k, k_sh.astype(cache_k.dtype)], axis=1)
                v_all = jnp.concatenate([cache_v, v_sh.astype(cache_v.dtype)], axis=1)
            k_buf = k_all[:, -WINDOW:]
            v_buf = v_all[:, -WINDOW:]
    return x, jnp.stack(new_re), jnp.stack(new_im), k_buf, v_buf


def setup_inputs(seed: int = 0) -> dict:
    key = jax.random.key(seed)
    ks = jax.random.split(key, 32)

    def nrm(k, shape, scale):
        return jax.random.normal(k, shape, F32) * scale

    n_idx = jnp.arange(STATE_N, dtype=F32)
    a_shape = (N_A_LAYERS, N_GROUPS, STATE_N)
    return {
        "x_prompt": nrm(ks[0], (BATCH, SEQ, D_MODEL), 1.0),
        "x_sample": nrm(ks[1], (DEC_BATCH, DEC_SEQ, D_MODEL), 1.0),
        "state_ssm_re": nrm(ks[2], (N_A_LAYERS, DEC_BATCH, N_GROUPS, STATE_N), 0.5),
        "state_ssm_im": nrm(ks[3], (N_A_LAYERS, DEC_BATCH, N_GROUPS, STATE_N), 0.5),
        "cache_k": nrm(ks[4], (DEC_BATCH, WINDOW, N_KV_HEADS, HEAD_DIM), 1.0),
        "cache_v": nrm(ks[5], (DEC_BATCH, WINDOW, N_KV_HEADS, HEAD_DIM), 1.0),
        "norm_mix": 1.0 + nrm(ks[6], (DEPTH, D_MODEL), 0.02),
        "norm_ffn": 1.0 + nrm(ks[7], (DEPTH, D_MODEL), 0.02),
        "ssm_a_re": -0.5 + nrm(ks[8], a_shape, 0.01),
        "ssm_a_im": math.pi * n_idx + nrm(ks[9], a_shape, 0.01),
        "ssm_log_dt": jax.random.uniform(ks[10], (N_A_LAYERS, N_GROUPS), F32,
                                         minval=math.log(DT_MIN), maxval=math.log(DT_MAX)),
        "ssm_b_re": nrm(ks[11], (N_A_LAYERS, N_GROUPS, STATE_N, GROUP_SIZE), (2 * GROUP_SIZE) ** -0.5),
        "ssm_b_im": nrm(ks[12], (N_A_LAYERS, N_GROUPS, STATE_N, GROUP_SIZE), (2 * GROUP_SIZE) ** -0.5),
        "ssm_c_re": nrm(ks[13], (N_A_LAYERS, N_GROUPS, GROUP_SIZE, STATE_N), (2 * STATE_N) ** -0.5),
        "ssm_c_im": nrm(ks[14], (N_A_LAYERS, N_GROUPS, GROUP_SIZE, STATE_N), (2 * STATE_N) ** -0.5),
        "ssm_d": nrm(ks[15], (N_A_LAYERS, D_MODEL), 0.5),
        "ssm_w_glu": nrm(ks[16], (N_A_LAYERS, D_MODEL, 2 * D_MODEL), D_MODEL ** -0.5),
        "kv_norm": 1.0 + nrm(ks[17], (D_MODEL,), 0.02),
        "w_kv": nrm(ks[18], (D_MODEL, 2 * N_KV_HEADS * HEAD_DIM), D_MODEL ** -0.5),
        "k_norm": 1.0 + nrm(ks[19], (HEAD_DIM,), 0.02),
        "w_q": nrm(ks[20], (N_B_LAYERS, D_MODEL, N_HEADS * HEAD_DIM), D_MODEL ** -0.5),
        "q_norm": 1.0 + nrm(ks[21], (N_B_LAYERS, HEAD_DIM), 0.02),
        "attn_sinks": nrm(ks[22], (N_B_LAYERS, N_HEADS), 0.5),
        "w_o": nrm(ks[23], (N_B_LAYERS, N_HEADS * HEAD_DIM, D_MODEL), (N_HEADS * HEAD_DIM) ** -0.5),
        "w_up": nrm(ks[24], (DEPTH, D_MODEL, D_FF), D_MODEL ** -0.5),
        "w_down": nrm(ks[25], (DEPTH, D_FF, D_MODEL), D_FF ** -0.5),
    }


def reference(x_prompt, x_sample, state_ssm_re, state_ssm_im, cache_k, cache_v, norm_mix, norm_ffn,
              ssm_a_re, ssm_a_im, ssm_log_dt, ssm_b_re, ssm_b_im, ssm_c_re, ssm_c_im, ssm_d, ssm_w_glu,
              kv_norm, w_kv, k_norm, w_q, q_norm, attn_sinks, w_o, w_up, w_down):
    weights = (norm_mix, norm_ffn, ssm_a_re, ssm_a_im, ssm_log_dt, ssm_b_re, ssm_b_im, ssm_c_re,
               ssm_c_im, ssm_d, ssm_w_glu, kv_norm, w_kv, k_norm, w_q, q_norm, attn_sinks, w_o,
               w_up, w_down)
    zeros_state = jnp.zeros((N_A_LAYERS, x_prompt.shape[0], N_GROUPS, STATE_N), state_ssm_re.dtype)
    y_prompt, re_p, im_p, k_p, v_p = _trunk(x_prompt, zeros_state, zeros_state, None, None, *weights)
    y_sample, re_s, im_s, k_s, v_s = _trunk(x_sample, state_ssm_re, state_ssm_im, cache_k, cache_v, *weights)
    return (y_prompt, y_sample, re_p, im_p, k_p, v_p, re_s, im_s, k_s, v_s)
```

```python
import contextlib
import math
import numpy as np
import concourse.bass as bass
import concourse.mybir as mybir
from concourse.bass_utils import run_bass_kernel_spmd

F32 = mybir.dt.float32
BF16 = mybir.dt.bfloat16
I32 = mybir.dt.int32
ALU = mybir.AluOpType
AF = mybir.ActivationFunctionType

NCORES = 8
D = 1024
NG = 64
EPS = 1e-6
TWO_PI_SAFE = 6.283185
INV_2PI = 1.0 / (2.0 * math.pi)


class Res:
    __slots__ = ("name", "w", "r")

    def __init__(self, name=""):
        self.name = name
        self.w = None
        self.r = []


class Op:
    __slots__ = ("eng", "fn", "deps", "dma", "sig", "tok", "dsem", "exempt")

    def __init__(self, eng, fn, dma):
        self.eng = eng
        self.fn = fn
        self.deps = []
        self.dma = dma
        self.sig = False
        self.tok = None
        self.dsem = None
        self.exempt = False


class Prog:
    ENGS = ("pe", "act", "dve", "pool", "sp")
    NDSEM = 8
    NDSEM_E = {"pool": 56}

    def __init__(self):
        self.q = {e: [] for e in self.ENGS}
        self.dma_hist = {e: [None] * self.NDSEM_E.get(e, self.NDSEM) for e in self.ENGS}
        self.dma_cnt = {e: 0 for e in self.ENGS}
        self.dma_semcount = {}
        self.pending = {e: [] for e in self.ENGS}

    def barrier(self):
        last = []
        for e in self.ENGS:
            for op in reversed(self.q[e]):
                if not op.dma:
                    last.append(op)
                    break
            for op in self.dma_hist[e]:
                if op is not None and not getattr(op, "exempt", False):
                    last.append(op)
        for e in self.ENGS:
            self.pending[e] = list(last)

    def add(self, eng, fn, reads=(), writes=(), dma=False):
        op = Op(eng, fn, dma)
        deps = list(self.pending[eng])
        self.pending[eng] = []
        for r in reads:
            if r.w is not None:
                deps.append(r.w)
        for r in writes:
            if r.w is not None:
                deps.append(r.w)
            deps.extend(r.r)
        if dma:
            k = self.dma_cnt[eng] % self.NDSEM_E.get(eng, self.NDSEM)
            self.dma_cnt[eng] += 1
            prev = self.dma_hist[eng][k]
            if prev is not None:
                deps.append(prev)
            self.dma_hist[eng][k] = op
            key = (eng, k)
            self.dma_semcount[key] = self.dma_semcount.get(key, 0) + 1
            op.dsem = key
            op.tok = (key, 16 * self.dma_semcount[key])
        seen = set()
        for d in deps:
            if id(d) in seen or d is op:
                continue
            seen.add(id(d))
            if d.eng == "pe" and eng == "pe" and not d.dma and not dma:
                continue
            d.sig = True
            op.deps.append(d)
        for r in reads:
            if not dma:
                r.r = [x for x in r.r if x.dma or x.eng != eng]
            r.r.append(op)
        for r in writes:
            r.w = op
            r.r = []
        self.q[eng].append(op)
        return op

    def emit(self, nc):
        for e in self.ENGS:
            c = 0
            for op in self.q[e]:
                if not op.dma and op.sig:
                    c += 1
                    op.tok = (e, c)
        with contextlib.ExitStack() as st:
            sems = {}
            for e in self.ENGS:
                sems[e] = st.enter_context(nc.semaphore("c_" + e))
            for key in self.dma_semcount:
                sems[key] = st.enter_context(nc.semaphore("d_%s%d" % key))
            block = st.enter_context(nc.Block())

            def run(ename, eobj):
                known = {}
                for op in self.q[ename]:
                    for d in op.deps:
                        s, v = d.tok
                        if known.get(s, 0) < v:
                            eobj.wait_ge(sems[s], v)
                            known[s] = v
                    ins = op.fn(eobj)
                    if op.dma:
                        ins.then_inc(sems[op.dsem], 16)
                    elif op.sig:
                        ins.then_inc(sems[ename], 1)
                if ename == "sp":
                    for key, cnt in self.dma_semcount.items():
                        if known.get(key, 0) < 16 * cnt:
                            eobj.wait_ge(sems[key], 16 * cnt)
                    for e in self.ENGS:
                        c = sum(1 for op in self.q[e] if (not op.dma and op.sig))
                        if c and e != ename and known.get(e, 0) < c:
                            eobj.wait_ge(sems[e], c)

            @block.tensor
            def _(e):
                run("pe", e)

            @block.scalar
            def _(e):
                run("act", e)

            @block.vector
            def _(e):
                run("dve", e)

            @block.gpsimd
            def _(e):
                run("pool", e)

            @block.sync
            def _(e):
                run("sp", e)


class T:
    def __init__(self, t, name):
        self.t = t
        self.res = Res(name)

    def __getitem__(self, k):
        return self.t[k]


def ins_b(a, pos, count):
    l = [list(x) for x in a.ap]
    l.insert(pos, [0, count])
    return bass.AP(a.tensor, a.offset, l)


class StopBuild(Exception):
    pass


class Builder:
    def ck(self, name):
        if self.dbg.get('sstop') == name:
            raise StopBuild()

    def __init__(self, dbg=None):
        self.dbg = dbg or {}
        self.nc = bass.Bass("TRN2", target_bir_lowering=False)
        self.P = Prog()
        self.sb_top = 0
        self.sb_base = None
        self.n_t = 0
        self.psum_i = 0
        self.bank_ctr = {}
        self.dram = {}
        try:
            self.build()
        except StopBuild:
            self.P.barrier()
            o = self.dout('dbg_S', [128, 128])
            self.dma('sp', o, self.identf[:], r=[self.identf])

    def din(self, name, shape, dt=F32):
        a = self.nc.dram_tensor(name, list(shape), dt, kind="ExternalInput").ap()
        self.dram[name] = a
        return a

    def dout(self, name, shape, dt=F32):
        a = self.nc.dram_tensor(name, list(shape), dt, kind="ExternalOutput").ap()
        self.dram[name] = a
        return a

    def dscr(self, name, shape, dt):
        t = T(self.nc.dram_tensor(name, list(shape), dt).ap(), name)
        return t

    def sb(self, name, shape, dt):
        esz = {F32: 4, BF16: 2, I32: 4}[dt]
        n = 1
        for s in shape[1:]:
            n *= s
        nbytes = (n * esz + 31) // 32 * 32
        off = self.sb_top
        self.sb_top += nbytes
        assert self.sb_top <= self.sb_cap, (name, self.sb_top, self.sb_cap)
        self.n_t += 1
        t = self.nc.alloc_sbuf_tensor_at("%s_%d" % (name, self.n_t), list(shape), dt, offset=self.sb_base + off)
        return T(t, name)

    def op(self, eng, fn, r=(), w=()):
        rr = [x.res if isinstance(x, T) else x for x in r]
        ww = [x.res if isinstance(x, T) else x for x in w]
        ex = [x for x in rr if x.name.startswith("bank")]
        if ex:
            rr = [x for x in rr if not x.name.startswith("bank")]
            ww = ww + [x for x in ex if x not in ww]
        return self.P.add(eng, fn, rr, ww)

    def dma(self, eng, out, in_, r=(), w=()):
        return self.P.add(eng, lambda e: e.dma_start(out=out, in_=in_),
                          [x.res if isinstance(x, T) else x for x in r],
                          [x.res if isinstance(x, T) else x for x in w], dma=True)

    def emit_casts(self, deps):
        for (t, i, rpd, src) in self.cast_list:
            o_ = self.dma("pool", t[i:i + rpd, :], src[i:i + rpd, :], r=deps, w=t.blk[i // 128:(i + rpd) // 128])
            o_.exempt = True
        self.cast_list = []

    def bankp(self, name, idxs):
        c = self.bank_ctr.get(name, 0)
        self.bank_ctr[name] = c + 1
        return self.banks[idxs[c % len(idxs)]]

    def bank(self):
        pool = getattr(self, "bank_pool", None) or list(range(8))
        b = self.banks[pool[self.psum_i % len(pool)]]
        self.psum_i += 1
        return b

    def tt(self, eng, out, in0, in1, alu, r, w):
        return self.op(eng, lambda e: e.tensor_tensor(out=out, in0=in0, in1=in1, op=alu), r, w)

    def ts(self, eng, out, in0, s1, op0, r, w, s2=None, op1=None):
        if op1 is None:
            return self.op(eng, lambda e: e.tensor_scalar(out=out, in0=in0, scalar1=s1, scalar2=None, op0=op0), r, w)
        return self.op(eng, lambda e: e.tensor_scalar(out=out, in0=in0, scalar1=s1, scalar2=s2, op0=op0, op1=op1), r, w)

    def stt(self, out, in0, scalar, in1, op0, op1, r, w):
        return self.op("dve", lambda e: e.scalar_tensor_tensor(out=out, in0=in0, scalar=scalar, in1=in1, op0=op0, op1=op1), r, w)

    def act(self, out, in_, func, r, w, **kw):
        return self.op("act", lambda e: e.activation(out=out, in_=in_, func=func, **kw), r, w)

    def cp(self, eng, out, in_, r, w):
        if eng == "act":
            return self.op("act", lambda e: e.copy(out=out, in_=in_), r, w)
        return self.op(eng, lambda e: e.tensor_copy(out=out, in_=in_), r, w)

    def cpow(self, X, TH, re, im, shape, tmp, rX, wR, neg_mag_im=None):
        def v(t):
            n = 1
            for s_ in shape[1:]:
                n *= s_
            a = t.t[:, 0:n]
            if len(shape) == 3:
                a = a.rearrange("p (a b) -> p a b", a=shape[1])
            elif len(shape) == 4:
                a = a.rearrange("p (a b c) -> p a b c", a=shape[1], b=shape[2])
            return a
        f1, f2, f3, m = [v(t) for t in tmp[:4]]
        ii = v(tmp[4])
        tr = list(tmp)
        self.op("dve", lambda e: e.tensor_copy(out=ii, in_=TH), r=rX, w=[tmp[4]])
        self.op("dve", lambda e: e.tensor_copy(out=f1, in_=ii), r=[tmp[4]], w=[tmp[0]])
        self.op("dve", lambda e: e.tensor_tensor(out=f1, in0=TH, in1=f1, op=ALU.subtract), r=list(rX) + [tmp[0]], w=[tmp[0]])
        self.op("dve", lambda e: e.tensor_scalar(out=f2, in0=TH, scalar1=0.25, scalar2=None, op0=ALU.add), r=rX, w=[tmp[1]])
        self.op("dve", lambda e: e.tensor_copy(out=ii, in_=f2), r=[tmp[1]], w=[tmp[4]])
        self.op("dve", lambda e: e.tensor_copy(out=f3, in_=ii), r=[tmp[4]], w=[tmp[2]])
        self.op("dve", lambda e: e.tensor_tensor(out=f2, in0=f2, in1=f3, op=ALU.subtract), r=[tmp[1], tmp[2]], w=[tmp[1]])
        self.op("act", lambda e: e.activation(out=f1, in_=f1, func=AF.Sin, scale=TWO_PI_SAFE), r=[tmp[0]], w=[tmp[0]])
        self.op("act", lambda e: e.activation(out=f2, in_=f2, func=AF.Sin, scale=TWO_PI_SAFE), r=[tmp[1]], w=[tmp[1]])
        self.op("act", lambda e: e.activation(out=m, in_=X, func=AF.Exp), r=rX, w=[tmp[3]])
        self.op("dve", lambda e: e.tensor_tensor(out=re, in0=m, in1=f2, op=ALU.mult), r=[tmp[3], tmp[1]], w=wR)
        self.op("dve", lambda e: e.tensor_tensor(out=im, in0=m, in1=f1, op=ALU.mult), r=[tmp[3], tmp[0]], w=wR)

    def build(self):
        nc = self.nc
        dbg = self.dbg
        xp = self.din("xp", [2, 2048, D])
        a_re = self.din("ssm_a_re", [1, NG, 64])
        a_im = self.din("ssm_a_im", [1, NG, 64])
        log_dt = self.din("ssm_log_dt", [1, NG])
        b_re = self.din("ssm_b_re", [1, NG, 64, 16])
        b_im = self.din("ssm_b_im", [1, NG, 64, 16])
        c_re = self.din("ssm_c_re", [1, NG, 16, 64])
        c_im = self.din("ssm_c_im", [1, NG, 16, 64])
        ssm_d = self.din("ssm_d", [1, D])
        norm_mix = self.din("norm_mix", [2, D])
        c_ident = self.din("c_ident", [128, 128])
        c_tri = self.din("c_tri", [128, 129])
        c_mask2 = self.din("c_mask2", [128, 128])
        c_kcol = self.din("c_kcol", [128, 1])
        c_ramp = self.din("c_ramp", [128, 129])

        sPr = self.dscr("sPr", [128, NG, 64], F32)
        sPi = self.dscr("sPi", [128, NG, 64], F32)
        sQr = self.dscr("sQr", [128, NG, 129], F32)
        sQB = self.dscr("sQB", [128, NG, 129], F32)
        sW1 = self.dscr("sW1", [128, NG, 256], BF16)
        sW2 = self.dscr("sW2", [128, NG, 128], BF16)
        sW3 = self.dscr("sW3", [128, NG, 128], BF16)

        self.sb_cap = (nc.sbuf_bytes_remaining - 64) // 32 * 32
        arena = nc.alloc_sbuf_tensor("arena", [128, self.sb_cap], mybir.dt.uint8)
        self.sb_base = nc.lookup_mloc(arena).addr
        assert self.sb_base % 32 == 0, self.sb_base
        self.banks = [T(nc.alloc_psum_tensor("bank%d" % i, [128, 512], F32), "bank%d" % i) for i in range(8)]

        identf = self.sb("identf", [128, 128], F32)
        identb = self.sb("identb", [128, 128], BF16)
        trib = self.sb("trib", [128, 129], BF16)
        mask2 = self.sb("mask2", [128, 128], F32)
        kcol = self.sb("kcol", [128, 1], F32)
        ramp = self.sb("ramp", [128, 129], F32)
        self.dma("sp", identf[:], c_ident, w=[identf])
        self.dma("sp", mask2[:], c_mask2, w=[mask2])
        self.dma("sp", kcol[:], c_kcol, w=[kcol])
        self.dma("sp", ramp[:], c_ramp, w=[ramp])
        trif = self.sb("trif", [128, 129], F32)
        self.dma("sp", trif[:], c_tri, w=[trif])
        self.op("dve", lambda e: e.tensor_copy(out=identb[:], in_=identf[:]), r=[identf], w=[identb])
        self.op("dve", lambda e: e.tensor_copy(out=trib[:], in_=trif[:]), r=[trif], w=[trib])
        self.identf, self.identb, self.trib = identf, identb, trib
        w_glu = self.din("ssm_w_glu", [1, D, 2 * D])
        w_up = self.din("w_up", [2, D, 4 * D])
        w_down = self.din("w_down", [2, 4 * D, D])
        w_kv = self.din("w_kv", [D, 512])
        w_q = self.din("w_q", [1, D, D])
        w_o = self.din("w_o", [1, D, D])
        norm_ffn = self.din("norm_ffn", [2, D])
        kv_norm = self.din("kv_norm", [D])
        self.wb = {}
        self.cast_list = []
        def castw(name, src, rows, cols):
            t = self.dscr(name, [rows, cols], BF16)
            t.blk = [Res("%s_%d" % (name, i)) for i in range(rows // 128)]
            rpd = max(128, (2 * 1024 * 1024) // (cols * 4))
            for i in range(0, rows, rpd):
                self.cast_list.append((t, i, rpd, src))
            self.wb[name] = t
        if self.dbg.get('nocast'):
            castw = lambda *a: None
        castw("wglu", w_glu[0], D, 2 * D)
        castw("wup0", w_up[0], D, 4 * D)
        castw("wdn0", w_down[0], 4 * D, D)
        castw("wkv", w_kv, D, 512)
        castw("wq", w_q[0], D, D)
        castw("wo", w_o[0], D, D)
        castw("wup1", w_up[1], D, 4 * D)
        castw("wdn1", w_down[1], 4 * D, D)
        self.io = dict(
            k_norm=self.din("k_norm", [64]), q_norm=self.din("q_norm", [1, 64]), sinks=self.din("attn_sinks", [1, 16]),
            c_abias=self.din("c_abias", [128, 2, 2, 16, 128], BF16),
            xs=self.din("xs", [16, 4, D]), st_re=self.din("st_re", [16, NG, 64]), st_im=self.din("st_im", [16, NG, 64]),
            ck=self.din("ck", [16, 128, 4, 64]), cv=self.din("cv", [16, 128, 4, 64]),
            c_sbc=self.din("c_sbc", [128, 16, 4]), c_sbn=self.din("c_sbn", [64, 4, 16, 4, 4]),
            ys=self.dout("ys", [16, 4, D]), sre_s=self.dout("sre_s", [16, NG, 64]), sim_s=self.dout("sim_s", [16, NG, 64]),
            kb_s=self.dout("kb_s", [16, 128, 4, 64]), vb_s=self.dout("vb_s", [16, 128, 4, 64]),
            yp=self.dout("yp", [2, 2048, D]), sre_p=self.dout("sre_p", [2, NG, 64]), sim_p=self.dout("sim_p", [2, NG, 64]),
            kb_p=self.dout("kb_p", [2, 128, 4, 64]), vb_p=self.dout("vb_p", [2, 128, 4, 64]))
        self.ck('casts')
        self.setup_s5(locals())
        self.ck('s5a')
        self.setup_s5b(locals())
        self.ck('s5b')
        if self.dbg.get('main', True):
            self.main(locals())

    def setup_s5(self, L):
        nc = self.nc
        g = lambda k: L[k]
        identf, mask2, kcol, ramp = g("identf"), g("mask2"), g("kcol"), g("ramp")
        a_re, a_im, log_dt = g("a_re"), g("a_im"), g("log_dt")
        b_re, b_im, c_re, c_im = g("b_re"), g("b_im"), g("c_re"), g("c_im")
        op, dma, sb = self.op, self.dma, self.sb
        a2 = sb("a2", [64, 2, 128], F32)
        for h, src in enumerate((a_re, a_im)):
            for c in range(2):
                dma("sp", a2[:, h, c * 64:(c + 1) * 64], src[0], w=[a2])
        AR = sb("AR", [128, NG], F32)
        AI = sb("AI", [128, NG], F32)
        for h, dst in enumerate((AR, AI)):
            bk = self.bank()
            op("pe", lambda e, h=h, bk=bk: e.transpose(out=bk[:, 0:64], in_=a2[:, h, :], identity=identf[0:64, 0:64]), r=[a2, identf], w=[bk])
            op("act", lambda e, dst=dst, bk=bk: e.copy(out=dst[:], in_=bk[:, 0:64]), r=[bk], w=[dst])
        DT = sb("DT", [128, NG], F32)
        dma("sp", DT[:], log_dt[0].partition_broadcast(128), w=[DT])
        op("act", lambda e: e.activation(out=DT[:], in_=DT[:], func=AF.Exp), r=[DT], w=[DT])
        self.ck('ar')
        XR = sb("XR", [128, NG], F32)
        TH = sb("TH", [128, NG], F32)
        op("dve", lambda e: e.tensor_tensor(out=XR[:], in0=DT[:], in1=AR[:], op=ALU.mult), r=[DT, AR], w=[XR])
        op("dve", lambda e: e.scalar_tensor_tensor(out=TH[:], in0=DT[:], scalar=INV_2PI, in1=AI[:], op0=ALU.mult, op1=ALU.mult), r=[DT, AI], w=[TH])
        self.XR, self.TH = XR, TH
        A1r = sb("A1r", [128, NG], F32)
        A1i = sb("A1i", [128, NG], F32)
        self.AR, self.AI, self.DT = AR, AI, DT
        self.A3r = sb("A3r", [128, NG], F32)
        self.A3B = sb("A3B", [128, NG], F32)
        self.A4r = sb("A4r", [128, NG], F32)
        self.A4B = sb("A4B", [128, NG], F32)
        self.A1B = sb("A1B", [128, NG], F32)
        self.mark = self.sb_top
        big = 512
        tmp = [sb("cp%d" % i, [128, big], F32) for i in range(4)] + [sb("cpi", [128, big], I32)]
        self.cpow(XR[:], TH[:], A1r[:], A1i[:], [128, NG], tmp, [XR, TH], [A1r, A1i])
        self.A1r, self.A1i = A1r, A1i
        self.ck('a1')
        fr = sb("fr", [128, NG], F32)
        fi = sb("fi", [128, NG], F32)
        t0 = sb("t0", [128, NG], F32)
        t1 = sb("t1", [128, NG], F32)
        t2 = sb("t2", [128, NG], F32)
        tt, ts, stt, act, cp = self.tt, self.ts, self.stt, self.act, self.cp
        ts("dve", t0[:], A1r[:], -1.0, ALU.add, [A1r], [t0])
        tt("dve", t1[:], AR[:], AR[:], ALU.mult, [AR], [t1])
        tt("dve", t2[:], AI[:], AI[:], ALU.mult, [AI], [t2])
        tt("dve", t1[:], t1[:], t2[:], ALU.add, [t1, t2], [t1])
        self.op("dve", lambda e: e.reciprocal(out=t1[:], in_=t1[:]), [t1], [t1])
        tt("dve", fr[:], t0[:], AR[:], ALU.mult, [t0, AR], [fr])
        tt("dve", t2[:], A1i[:], AI[:], ALU.mult, [A1i, AI], [t2])
        tt("dve", fr[:], fr[:], t2[:], ALU.add, [fr, t2], [fr])
        tt("dve", fr[:], fr[:], t1[:], ALU.mult, [fr, t1], [fr])
        tt("dve", fi[:], A1i[:], AR[:], ALU.mult, [A1i, AR], [fi])
        tt("dve", t2[:], t0[:], AI[:], ALU.mult, [t0, AI], [t2])
        tt("dve", fi[:], fi[:], t2[:], ALU.subtract, [fi, t2], [fi])
        tt("dve", fi[:], fi[:], t1[:], ALU.mult, [fi, t1], [fi])

        X8 = sb("X8", [128, NG, 8], F32)
        T8 = sb("T8", [128, NG, 8], F32)
        r8 = ins_b(ramp[:, 0:8], 1, NG)
        tt("dve", X8[:], ins_b(XR[:], 2, 8), r8, ALU.mult, [XR, ramp], [X8])
        tt("dve", T8[:], ins_b(TH[:], 2, 8), r8, ALU.mult, [TH, ramp], [T8])
        P8r = sb("P8r", [128, NG, 8], F32)
        P8i = sb("P8i", [128, NG, 8], F32)
        self.cpow(X8[:], T8[:], P8r[:], P8i[:], [128, NG, 8], tmp, [X8, T8], [P8r, P8i])
        for (tr_, tb_, ti_) in ((self.A3r, self.A3B, 3), (self.A4r, self.A4B, 4), (None, self.A1B, 1)):
            if tr_ is not None:
                cp("dve", tr_[:], P8r[:, :, ti_], [P8r], [tr_])
            ts("dve", tb_[0:64], P8i[0:64, :, ti_], -1.0, ALU.mult, [P8i], [tb_])
            cp("dve", tb_[64:128], P8i[64:128, :, ti_], [P8i, tb_], [tb_])
        N8r = sb("N8r", [128, NG, 8], F32)
        N8i = sb("N8i", [128, NG, 8], F32)
        m2 = sb("m2", [128, NG, 8], F32)
        act(m2[:], X8[:], AF.Exp, [X8], [m2], scale=-2.0)
        tt("dve", N8r[:], P8r[:], m2[:], ALU.mult, [P8r, m2], [N8r])
        stt(N8i[:], P8i[:], -1.0, m2[:], ALU.mult, ALU.mult, [P8i, m2], [N8i])
        rr = sb("rr", [128, NG, 8], F32)
        ri = sb("ri", [128, NG, 8], F32)
        u8 = sb("u8", [128, NG, 8], F32)
        frb = ins_b(fr[:], 2, 8)
        fib = ins_b(fi[:], 2, 8)
        tt("dve", rr[:], N8r[:], frb, ALU.mult, [N8r, fr], [rr])
        tt("dve", u8[:], N8i[:], fib, ALU.mult, [N8i, fi], [u8])
        tt("dve", rr[:], rr[:], u8[:], ALU.subtract, [rr, u8], [rr])
        tt("dve", ri[:], N8r[:], fib, ALU.mult, [N8r, fi], [ri])
        tt("dve", u8[:], N8i[:], frb, ALU.mult, [N8i, fr], [u8])
        tt("dve", ri[:], ri[:], u8[:], ALU.add, [ri, u8], [ri])
        c2L = sb("c2L", [128, NG, 8], F32)
        ts("dve", c2L[0:64], ri[0:64], -1.0, ALU.mult, [ri], [c2L])
        cp("dve", c2L[64:128], ri[64:128], [ri, c2L], [c2L])
        c1R = sb("c1R", [128, NG, 8], F32)
        c2R = sb("c2R", [128, NG, 8], F32)
        cp("dve", c1R[0:64], P8r[0:64], [P8r], [c1R])
        ts("dve", c1R[64:128], P8r[64:128], -1.0, ALU.mult, [P8r, c1R], [c1R])
        ts("dve", c2R[:], P8i[:], -1.0, ALU.mult, [P8i], [c2R])

        self.ck('rho')
        Ba = sb("Ba", [128, NG, 16], F32)
        Bb = sb("Bb", [128, NG, 16], F32)
        Bc1 = sb("Bc1", [64, 2, 1024], F32)
        Bc2 = sb("Bc2", [64, 2, 1024], F32)
        bre_v = b_re[0].rearrange("g n c -> g (n c)")
        bim_v = b_im[0].rearrange("g n c -> g (n c)")
        dma("sp", Bc1[:, 0, :], bre_v, w=[Bc1])
        dma("sp", Bc1[:, 1, :], bim_v, w=[Bc1])
        dma("sp", Bc2[:, 0, :], bim_v, w=[Bc2])
        dma("sp", Bc2[:, 1, :], bre_v, w=[Bc2])
        nb_ = 0
        for src, dst in ((Bc1, Ba), (Bc2, Bb)):
            flat = src[:].rearrange("p h f -> p (h f)")
            for c4 in range(4):
                bk = self.bank()
                def btf(e, bk=bk, flat=flat, c4=c4):
                    ins = None
                    for cc in range(4):
                        ci = c4 * 4 + cc
                        in_ = bass.AP(flat.tensor, flat.offset + ci, [list(flat.ap[0]), [16, 128]])
                        ins = e.transpose(out=bk[:, cc * 64:(cc + 1) * 64], in_=in_, identity=identf[0:64, 0:64])
                    return ins
                op("pe", btf, r=[src, identf], w=[bk])
                cp("act" if nb_ % 2 == 0 else "dve", dst[:, :, c4 * 4:(c4 + 1) * 4].rearrange("p g c -> p c g"),
                   bk[:, 0:256].rearrange("p (c g) -> p c g", c=4), [bk], [dst])
                nb_ += 1
        Cin = sb("Cin", [128, 8, 128], F32)
        Cin2 = sb("Cin2", [128, 8, 128], F32)
        cre_v = c_re[0].rearrange("(gh gl) c n -> (gl c) gh n", gl=8)
        cim_v = c_im[0].rearrange("(gh gl) c n -> (gl c) gh n", gl=8)
        dma("sp", Cin[:, :, 0:64], cre_v, w=[Cin])
        dma("sp", Cin[:, :, 64:128], cim_v, w=[Cin])
        dma("sp", Cin2[:, :, 0:64], cim_v, w=[Cin2])
        dma("sp", Cin2[:, :, 64:128], cre_v, w=[Cin2])
        self.emit_casts([a2, DT, Bc1, Bc2, Cin, Cin2])
        Ca = sb("Ca", [128, NG, 16], F32)
        Cb = sb("Cb", [128, NG, 16], F32)
        for src, dst in ((Cin, Ca), (Cin2, Cb)):
            for gh in range(8):
                bk = self.bank()
                op("pe", lambda e, bk=bk, src=src, gh=gh: e.transpose(out=bk[:, 0:128], in_=src[:, gh, :], identity=identf[:]), r=[src, identf], w=[bk])
                cp("act", dst[:, gh * 8:(gh + 1) * 8, :], bk[:, 0:128], [bk], [dst])

        self.ck('bc')
        Lt = sb("Lt", [128, NG, 8, 16], F32)
        R0 = sb("R0", [128, NG, 8, 16], F32)
        self.tmpL_off = self.sb_top
        tmpL = sb("tmpL", [128, 32, 8, 16], F32)
        for h in range(2):
            gs = slice(h * 32, (h + 1) * 32)
            def b4(t):
                return ins_b(t[:, gs, :], 2, 8)
            def c4(t):
                return ins_b(t[:, gs, :], 3, 16)
            tt("dve", Lt[:, gs], b4(Ba), c4(rr), ALU.mult, [Ba, rr], [Lt])
            tt("dve", tmpL[:], b4(Bb), c4(c2L), ALU.mult, [Bb, c2L], [tmpL])
            tt("dve", Lt[:, gs], Lt[:, gs], tmpL[:], ALU.add, [Lt, tmpL], [Lt])
            tt("dve", R0[:, gs], b4(Ca), c4(c1R), ALU.mult, [Ca, c1R], [R0])
            tt("dve", tmpL[:], b4(Cb), c4(c2R), ALU.mult, [Cb, c2R], [tmpL])
            tt("dve", R0[:, gs], R0[:, gs], tmpL[:], ALU.add, [R0, tmpL], [R0])

        self.ck('lr')
        sW1, sW2, sW3 = g("sW1"), g("sW2"), g("sW3")
        self.P.barrier()
        self.sb_top = self.tmpL_off
        NQ = 16
        W1s = sb("W1s", [128, NQ, 256], BF16)
        W2s = sb("W2s", [128, NQ, 128], BF16)
        W3s = sb("W3s", [128, NQ, 128], BF16)
        Lh = sb("Lh", [128, NQ, 128], BF16)
        Ll = sb("Ll", [128, NQ, 128], BF16)
        Rh = sb("Rh", [128, NQ, 128], BF16)
        Rl = sb("Rl", [128, NQ, 128], BF16)
        tf = sb("tf", [128, NQ, 128], F32)
        dbgW = "tables" in self.dbg
        if dbgW:
            o3 = self.dout("dbg_W2", [128, NG, 128], BF16)
            o4 = self.dout("dbg_W1", [128, NG, 256], BF16)
        for h in range(NG // NQ):
            gs = slice(h * NQ, (h + 1) * NQ)
            Lv = Lt[:, gs].rearrange("p g t c -> p g (t c)")
            Rv = R0[:, gs].rearrange("p g t c -> p g (t c)")
            cp("dve", W3s[:], Rv, [R0], [W3s])
            for src, hi, lo in ((Lv, Lh, Ll), (Rv, Rh, Rl)):
                cp("dve", hi[:], src, [Lt, R0], [hi])
                tt("dve", lo[:], src, hi[:], ALU.subtract, [Lt, R0, hi], [lo])
            for gl in range(NQ):
                if self.dbg.get("nowpe"):
                    break
                gi = h * NQ + gl
                bk = self.bank()
                Lg = Lt[:, gi].rearrange("p s c -> p (s c)")
                identb = self.identb
                op("pe", lambda e, bk=bk, gl=gl: e.matmul(bk[:, 0:128], lhsT=Lh[:, gl, :], rhs=identb[:], start=True, stop=True), r=[Lh, identb], w=[bk])
                def w2f(e, bk=bk, gl=gl):
                    e.matmul(bk[:, 128:256], lhsT=Lh[:, gl, :], rhs=Rh[:, gl, :], start=True, stop=False)
                    e.matmul(bk[:, 128:256], lhsT=Lh[:, gl, :], rhs=Rl[:, gl, :], start=False, stop=False)
                    return e.matmul(bk[:, 128:256], lhsT=Ll[:, gl, :], rhs=Rh[:, gl, :], start=False, stop=True)
                op("pe", w2f, r=[Lh, Ll, Rh, Rl], w=[bk])
                ev = self.dbg.get("evac", "ad")
                if "a" in ev:
                    cp("act", W1s[:, gl, 0:128], bk[:, 0:128], [bk], [W1s])
                    cp("act", W1s[:, gl, 128:256], bk[:, 0:128], [bk, W1s], [W1s])
                if "d" in ev:
                    tt("dve", W2s[:, gl], bk[:, 128:256], mask2[:], ALU.mult, [bk, mask2], [W2s])
            if not self.dbg.get("nowdma"):
                dma("sp", sW1[:, gs], W1s[:], r=[W1s], w=[sW1])
                dma("sp", sW2[:, gs], W2s[:], r=[W2s], w=[sW2])
                dma("sp", sW3[:, gs], W3s[:], r=[W3s], w=[sW3])
            if dbgW:
                dma("sp", o3[:, gs], W2s[:], r=[W2s])
                dma("sp", o4[:, gs], W1s[:], r=[W1s])
        if "tables" in self.dbg:
            o1 = self.dout("dbg_L", [128, NG, 128])
            o2 = self.dout("dbg_R0", [128, NG, 128])
            dma("sp", o1, Lt[:].rearrange("p g s c -> p g (s c)"), r=[Lt])
            dma("sp", o2, R0[:].rearrange("p g s c -> p g (s c)"), r=[R0])


    def setup_s5b(self, L):
        g = lambda k: L[k]
        kcol, ramp = g("kcol"), g("ramp")
        a_re, a_im = g("a_re"), g("a_im")
        sPr, sPi, sQr, sQB = g("sPr"), g("sPi"), g("sQr"), g("sQB")
        op, dma, sb = self.op, self.dma, self.sb
        tt, ts, stt, act, cp = self.tt, self.ts, self.stt, self.act, self.cp
        XR, TH, DT = self.XR, self.TH, self.DT
        self.P.barrier()
        self.sb_top = self.mark
        big = 32 * 129
        tmp = [sb("cq%d" % i, [128, big], F32) for i in range(4)] + [sb("cqi", [128, big], I32)]
        mark2 = self.sb_top
        ARb = sb("ARb", [128, NG, 64], F32)
        AIb = sb("AIb", [128, NG, 64], F32)
        dma("sp", ARb[:].rearrange("p g n -> p (g n)"), a_re[0].rearrange("g n -> (g n)").partition_broadcast(128), w=[ARb])
        dma("sp", AIb[:].rearrange("p g n -> p (g n)"), a_im[0].rearrange("g n -> (g n)").partition_broadcast(128), w=[AIb])
        XP = sb("XP", [128, 32, 64], F32)
        TP = sb("TP", [128, 32, 64], F32)
        Prh = sb("Prh", [128, 32, 64], F32)
        Pih = sb("Pih", [128, 32, 64], F32)
        for h in range(2):
            gs = slice(h * 32, (h + 1) * 32)
            dtb = ins_b(DT[:, gs], 2, 64)
            tt("dve", XP[:], ARb[:, gs], dtb, ALU.mult, [ARb, DT], [XP])
            ts("dve", XP[:], XP[:], kcol[:], ALU.mult, [XP, kcol], [XP], s2=-8.0, op1=ALU.mult)
            tt("dve", TP[:], AIb[:, gs], dtb, ALU.mult, [AIb, DT], [TP])
            ts("dve", TP[:], TP[:], kcol[:], ALU.mult, [TP, kcol], [TP], s2=-8.0 * INV_2PI, op1=ALU.mult)
            self.cpow(XP[:], TP[:], Prh[:], Pih[:], [128, 32, 64], tmp, [XP, TP], [Prh, Pih])
            dma("sp", sPr[:, gs], Prh[:], r=[Prh], w=[sPr])
            dma("sp", sPi[:, gs], Pih[:], r=[Pih], w=[sPi])
        self.P.barrier()
        self.sb_top = mark2
        XQ = sb("XQ", [128, 32, 129], F32)
        TQ = sb("TQ", [128, 32, 129], F32)
        Qrh = sb("Qrh", [128, 32, 129], F32)
        Qih = sb("Qih", [128, 32, 129], F32)
        QBh = sb("QBh", [128, 32, 129], F32)
        for h in range(2):
            gs = slice(h * 32, (h + 1) * 32)
            rb = ins_b(ramp[:], 1, 32)
            stt(XQ[:], ins_b(XR[:, gs], 2, 129), 8.0, rb, ALU.mult, ALU.mult, [XR, ramp], [XQ])
            stt(TQ[:], ins_b(TH[:, gs], 2, 129), 8.0, rb, ALU.mult, ALU.mult, [TH, ramp], [TQ])
            self.cpow(XQ[:], TQ[:], Qrh[:], Qih[:], [128, 32, 129], tmp, [XQ, TQ], [Qrh, Qih])
            ts("dve", QBh[0:64], Qih[0:64], -1.0, ALU.mult, [Qih], [QBh])
            cp("dve", QBh[64:128], Qih[64:128], [Qih, QBh], [QBh])
            dma("sp", sQr[:, gs], Qrh[:], r=[Qrh], w=[sQr])
            dma("sp", sQB[:, gs], QBh[:], r=[QBh], w=[sQB])
        self.P.barrier()
        self.sb_top = self.mark

    def main(self, L):
        nc = self.nc
        g = lambda k: L[k]
        op, dma, sb = self.op, self.dma, self.sb
        tt, ts, stt, act, cp = self.tt, self.ts, self.stt, self.act, self.cp
        xp = g("xp")
        identb, trib = self.identb, self.trib
        Rr = [T(None, "R%d" % r) for r in range(8)]
        Rt = sb("Rt", [128, 8, D], F32)
        HN = sb("HN", [128, 8, D], BF16)
        HNr = [Res("HN%d" % r) for r in range(8)]
        XT = sb("XT", [128, 8, D], BF16)
        XTd = [Res("XT%d" % i) for i in range(8)]
        G = sb("G", [128, 5, D], F32)
        Dv = sb("Dv", [128, D], F32)
        dma("sp", G[:, 0:2, :], g("norm_mix").partition_broadcast(128), w=[G])
        dma("sp", Dv[:], g("ssm_d")[0].partition_broadcast(128), w=[Dv])
        zc = [sb("zc%d" % i, [128, NG], F32) for i in range(2)]
        zcs = [sb("zcs%d" % i, [128, NG], F32) for i in range(2)]
        ssq = sb("ssq", [128, 8], F32)
        rstd = sb("rstd", [128, 8], F32)
        epsT = sb("epsT", [128, 1], F32)
        op("pool", lambda e: e.memset(epsT[:], EPS), w=[epsT])
        KT = sb("KT", [64, 4, 9, 128], BF16)
        KTres = [Res("KT%d" % i) for i in range(9)]
        VP = sb("VP", [128, 9, 4, 65], BF16)
        VPres = [Res("VP%d" % i) for i in range(9)]
        op("pool", lambda e: e.memset(VP[:], 1.0), w=VPres)
        esink = sb("esink", [128, 16], F32)
        gq = sb("gq", [128, 64], F32)
        gk = sb("gk", [128, 64], F32)
        io = self.io
        dma("sp", esink[:], io["sinks"][0].partition_broadcast(128), w=[esink])
        act(esink[:], esink[:], AF.Exp, [esink], [esink])
        dma("sp", gq[:], io["q_norm"][0].partition_broadcast(128), w=[gq])
        ts("dve", gq[:], gq[:], 0.125, ALU.mult, [gq], [gq])
        dma("sp", gk[:], io["k_norm"].partition_broadcast(128), w=[gk])
        gq8 = sb("gq8", [64, 1], F32)
        gqk = sb("gqk", [64, 1], F32)
        dma("sp", gq8[:], io["q_norm"][0].rearrange("(d o) -> d o", o=1), w=[gq8])
        dma("sp", gqk[:], io["k_norm"].rearrange("(d o) -> d o", o=1), w=[gqk])
        ts("dve", gq8[:], gq8[:], 0.125, ALU.mult, [gq8], [gq8])
        tt("dve", gqk[:], gqk[:], gq8[:], ALU.mult, [gqk, gq8], [gqk])
        Hfin = sb("Hfin", [128, NG], F32)
        AinvR = sb("AinvR", [128, NG], F32)
        AinvB = sb("AinvB", [128, NG], F32)
        Ht1 = sb("Ht1", [128, NG], F32)
        Ht2 = sb("Ht2", [128, NG], F32)
        A1r, A1i = self.A1r, self.A1i
        tt("dve", Ht1[:], A1r[:], A1r[:], ALU.mult, [A1r], [Ht1])
        tt("dve", Ht2[:], A1i[:], A1i[:], ALU.mult, [A1i], [Ht2])
        tt("dve", Ht1[:], Ht1[:], Ht2[:], ALU.add, [Ht1, Ht2], [Ht1])
        op("dve", lambda e: e.reciprocal(out=Ht1[:], in_=Ht1[:]), [Ht1], [Ht1])
        tt("dve", AinvR[:], A1r[:], Ht1[:], ALU.mult, [A1r, Ht1], [AinvR])
        tt("dve", AinvB[0:64], A1i[0:64], Ht1[0:64], ALU.mult, [A1i, Ht1], [AinvB])
        stt(AinvB[64:128], A1i[64:128], -1.0, Ht1[64:128], ALU.mult, ALU.mult, [A1i, Ht1, AinvB], [AinvB])
        self.main_mark = self.sb_top
        Rres = [Res("Rres%d" % r) for r in range(8)]

        HNg = HN[:].rearrange("p r d -> p (r d)").rearrange("p (g s c) -> p g s c", g=NG, s=8)

        cx = {"M": 128, "NT": 8}

        def rmsnorm(gi, gmajor=False):
            M, NT = cx["M"], cx["NT"]
            for r in range(NT):
                act(XT[0:M, 0, :], Rt[0:M, r, :], AF.Square, [Rres[r]], [XTd[0], ssq], accum_out=ssq[0:M, r:r + 1])
            act(rstd[0:M, 0:NT], ssq[0:M, 0:NT], AF.Ln, [ssq, epsT], [rstd], scale=1.0 / D, bias=epsT[0:M])
            act(rstd[0:M, 0:NT], rstd[0:M, 0:NT], AF.Exp, [rstd], [rstd], scale=-0.5)
            for r in range(NT):
                if gmajor:
                    stt(HNg[:, :, r, :], Rt[:, r, :].rearrange("p (g c) -> p g c", c=16), rstd[:, r:r + 1],
                        G[:, gi, :].rearrange("p (g c) -> p g c", c=16), ALU.mult, ALU.mult, [Rres[r], rstd, G], [HNr[r]])
                else:
                    stt(HN[0:M, r, :], Rt[0:M, r, :], rstd[0:M, r:r + 1], G[0:M, gi, :], ALU.mult, ALU.mult, [Rres[r], rstd, G], [HNr[r]])

        sW1, sW2, sW3, sPr, sPi, sQr, sQB = [g(k) for k in ("sW1", "sW2", "sW3", "sPr", "sPi", "sQr", "sQB")]
        tb = []
        for i in range(2):
            tb.append(dict(
                W1=sb("W1b%d" % i, [128, 8, 256], BF16), W2=sb("W2b%d" % i, [128, 8, 128], BF16),
                W3=sb("W3b%d" % i, [128, 8, 128], BF16), Pr=sb("Prb%d" % i, [128, 8, 64], F32),
                Pi=sb("Pib%d" % i, [128, 8, 64], F32), Qr=sb("Qrb%d" % i, [128, 8, 129], F32),
                QB=sb("QBb%d" % i, [128, 8, 129], F32)))
        UTb = [sb("UTb%d" % i, [128, 8, 128], BF16) for i in range(2)]
        Zb = [sb("Zb%d" % i, [128, 8, 192], BF16) for i in range(2)]
        tA = [sb("tA%d" % i, [128, 2, 192], F32) for i in range(2)]
        tB = [sb("tB%d" % i, [128, 2, 192], F32) for i in range(2)]
        t1b = sb("t1b", [128, 8, 129], F32)
        t2b = sb("t2b", [128, 8, 129], F32)
        HTb = [sb("HTb%d" % i, [128, 8, 128], BF16) for i in range(2)]
        du = [sb("du0", [128, 8, 128], F32)] * 2
        ytmp = [sb("ytmp0", [128, 8, 128], F32)] * 2
        zsw = sb("zsw", [64, 128], F32)
        hres = [[Res("htb%d_%d" % (i, j)) for j in range(2)] for i in range(2)]
        utres = [[Res("ut%d_%d" % (i, j)) for j in range(2)] for i in range(2)]
        t1res = [Res("t1b%d" % i) for i in range(8)]
        t2res = [Res("t2b%d" % i) for i in range(8)]
        s5_top = self.sb_top
        self.sb_top = self.sb_cap - 16 * 1024 - 64
        assert self.sb_top >= s5_top, (self.sb_top, s5_top)
        GL = sb("GL", [128, 8, D], BF16)
        arena_lim = self.sb_cap - 16 * 1024 - 64
        self.sb_top = self.main_mark
        Wg4 = [sb("Wg4_%d" % i, [128, 8, 512], BF16) for i in range(4)]
        sg = [sb("sg%d" % i, [128, 512], F32) for i in range(2)]
        pr = [sb("pr%d" % i, [128, 512], F32) for i in range(2)]
        assert self.sb_top <= arena_lim
        self.sb_top = self.main_mark
        Hh = sb("Hh", [128, 32, 512], BF16)
        Hres = [Res("Hh%d" % i) for i in range(32)]
        NWB = 3
        Wu = [sb("Wu%d" % i, [128, 8, 512], BF16) for i in range(NWB)]
        Wd = [sb("Wd%d" % i, [128, 4, D], BF16) for i in range(NWB)]
        assert self.sb_top <= arena_lim
        self.sb_top = self.main_mark
        Wkv = sb("Wkv", [128, 8, 512], BF16)
        Wq = [sb("Wq%d" % i, [128, 8, 512], BF16) for i in range(2)]
        Wo = Wq
        sqf2 = [sb("sqf%d" % i, [128, D], F32) for i in range(2)]
        qss2 = [sb("qss%d" % i, [128, 16], F32) for i in range(2)]
        Kf2 = [sb("Kf%d" % i, [128, 256], F32) for i in range(2)]
        Vf2 = [sb("Vf0", [128, 256], F32)] * 2
        Kb2 = [sb("Kb%d" % i, [128, 256], BF16) for i in range(2)]
        Qb2 = [sb("Qb%d" % i, [128, D], BF16) for i in range(2)]
        sqf, qss, Kf, Vf, Kb, Qb = sqf2[0], qss2[0], Kf2[0], Vf2[0], Kb2[0], Qb2[0]
        scf4 = [sb("scf%d" % i, [128, 512], F32) for i in range(4)]
        scf = scf4
        att_shared = self.sb_top
        OTt = sb("OTt", [128, 8, D], BF16)
        OTd = [Res("OT%d" % i) for i in range(8)]
        abias = sb("abias", [128, 2, 2, 16, 128], BF16)
        QT = [sb("QT%d" % i, [64, 16, 128], BF16) for i in range(2)]
        exb = [sb("exb%d" % i, [128, 512], BF16) for i in range(4)]
        den2 = [sb("den%d" % i, [128, 4], F32) for i in range(2)]
        Otok = [sb("Otok%d" % i, [128, D], BF16) for i in range(2)]
        assert self.sb_top <= self.sb_cap, self.sb_top
        self.sb_top = att_shared
        att_top = self.sb_top
        WoS = sb("WoS", [64, 16, 512], BF16)
        sbc = sb("sbc", [128, 64], F32)
        sbn = sb("sbn", [64, 4, 256], F32)
        KTn = sb("KTn", [64, 4, 64], BF16)
        VPn = sb("VPn", [64, 4, 64], BF16)
        QTs = sb("QTs", [64, 16, 64], BF16)
        QTs2 = sb("QTs2", [64, 4, 256], BF16)
        exn = sb("exn", [64, 4, 256], BF16)
        Kc = [sb("Kc%d" % i, [128, 256], F32) for i in range(2)]
        Vc = [sb("Vc%d" % i, [128, 256], F32) for i in range(2)]
        Kcb2 = [sb("Kcb%d" % i, [128, 256], BF16) for i in range(2)]
        VPc = [sb("VPc%d" % i, [128, 4, 64], BF16) for i in range(2)]
        KTc = [sb("KTc%d" % i, [64, 4, 128], BF16) for i in range(2)]
        exc = [sb("exc%d" % i, [128, 64], BF16) for i in range(2)]
        OTs = sb("OTs", [64, 16, 64], BF16)
        dn = sb("dn", [64, 512], F32)
        onesb = sb("onesb", [128, 64], BF16)
        assert self.sb_top <= self.sb_cap, self.sb_top
        self.sb_top = self.main_mark
        stb = [dict(W1=sb("sW1b%d" % i, [128, 8, 256], BF16), W2=sb("sW2b%d" % i, [128, 8, 128], BF16),
                    W3=sb("sW3b%d" % i, [128, 8, 128], BF16)) for i in range(2)]
        HNs = sb("HNs", [16, NG, 4, 16], BF16)
        GLs = sb("GLs", [16, 4, D], BF16)
        H0k = [sb("H0k%d" % i, [16, 8, 128], F32) for i in range(2)]
        H0k2 = [sb("H0k2%d" % i, [16, 8, 128], F32) for i in range(2)]
        H0T2 = [sb("H0T%d" % i, [128, 8, 16], F32) for i in range(2)]
        H0Ts2 = [sb("H0Ts%d" % i, [128, 8, 16], F32) for i in range(2)]
        za1 = sb("za1", [128, 8, 16], F32)
        za2 = sb("za2", [128, 8, 16], F32)
        zt1 = sb("zt1", [128, 8, 16], F32)
        zt2 = sb("zt2", [128, 8, 16], F32)
        Fbs2 = [sb("Fbs%d" % i, [128, 8, 16], BF16) for i in range(2)]
        UTs2 = [sb("UTs%d" % i, [64, 8, 16], BF16) for i in range(2)]
        Hs_ = sb("Hs_", [128, 8, 16], F32)
        Hout = sb("Hout", [16, 8, 128], F32)
        yt16 = sb("yt16", [16, 4, 128], F32)
        du16 = sb("du16", [16, 4, 128], F32)
        assert self.sb_top <= arena_lim, self.sb_top
        wb = self.wb
        dma("sp", G[:, 2:4, :], g("norm_ffn").partition_broadcast(128), w=[G])
        dma("sp", G[:, 4, :], g("kv_norm").partition_broadcast(128), w=[G])

        def to_xt(src, src_res, perm=False):
            M, NT = cx["M"], cx["NT"]
            for r in range(NT):
                bk = self.bank()
                bkb = bk[:].bitcast(BF16)
                def tf(e, bkb=bkb, r=r):
                    ins = None
                    for dt_ in range(8):
                        ins = e.transpose(out=bkb[:, dt_ * 128:dt_ * 128 + M], in_=src[0:M, r, dt_ * 128:(dt_ + 1) * 128], identity=identb[0:M, 0:M])
                    return ins
                rr_ = [src_res[r]] if len(src_res) == NT and NT > 1 else list(src_res)
                op("pe", tf, r=rr_ + [identb], w=[bk])
                eng = "act" if r % 2 == 0 else "dve"
                if perm:
                    o_ = XT[:].rearrange("p t (q r l) -> p t q r l", q=8, r=8)[:, :, :, r, :]
                    i_ = bkb[:, 0:1024].rearrange("p (t q l) -> p t q l", t=8, q=8)
                    cp(eng, o_, i_, [bk], XTd)
                elif NT == 8:
                    cp(eng, XT[:, :, r * 128:(r + 1) * 128], bkb[:, 0:1024].rearrange("p (t k) -> p t k", t=8), [bk], XTd)
                else:
                    cp(eng, XT[:, :, 0:M], bkb[:, 0:1024].rearrange("p (t k) -> p t k", t=8)[:, :, 0:M], [bk], XTd)

        def glu_phase():
            M, NT = cx["M"], cx["NT"]
            self.P.barrier()
            to_xt(GL, [GLres])
            wg = wb["wglu"]
            wv = wg[:].rearrange("(kt p) f -> p kt f", p=128)
            for j in range(2):
                Wa, Wgt = Wg4[2 * (j % 2)], Wg4[2 * (j % 2) + 1]
                dma("sp", Wa[:], wv[:, :, j * 512:(j + 1) * 512], r=wg.blk, w=[Wa])
                dma("sp", Wgt[:], wv[:, :, D + j * 512:D + (j + 1) * 512], r=wg.blk, w=[Wgt])
                for r in range(NT):
                    bA, bB = self.bank(), self.bank()
                    def mm(e, bk, W, r=r):
                        ins = None
                        for kt in range(8):
                            ins = e.matmul(bk[0:M, 0:512], lhsT=XT[:, kt, r * 128:r * 128 + M], rhs=W[:, kt, :], start=(kt == 0), stop=(kt == 7))
                        return ins
                    op("pe", lambda e, bA=bA, Wa=Wa, mm=mm: mm(e, bA, Wa), r=XTd + [Wa], w=[bA])
                    op("pe", lambda e, bB=bB, Wgt=Wgt, mm=mm: mm(e, bB, Wgt), r=XTd + [Wgt], w=[bB])
                    sg_, pr_ = sg[r % 2], pr[r % 2]
                    act(sg_[0:M], bB[0:M, 0:512], AF.Sigmoid, [bB], [sg_])
                    tt("dve", pr_[0:M], bA[0:M, 0:512], sg_[0:M], ALU.mult, [bA, sg_], [pr_])
                    tt("pool", Rt[0:M, r, j * 512:(j + 1) * 512], Rt[0:M, r, j * 512:(j + 1) * 512], pr_[0:M], ALU.add, [Rres[r], pr_], [Rres[r]])

        def mlp_phase(layer):
            M, NT = cx["M"], cx["NT"]
            rmsnorm(2 + layer)
            to_xt(HN, HNr)
            self.P.barrier()
            wu, wd = wb["wup%d" % layer], wb["wdn%d" % layer]
            wuv = wu[:].rearrange("(kt p) f -> p kt f", p=128)
            wdv = wd[:].rearrange("(ft p) d -> p ft d", p=128)
            nld = [0, 0]
            nhalf = 2 if NT == 8 else 1
            tph = NT // nhalf
            ncol = 512 if NT == 8 else M
            for hf in range(nhalf):
                cols = slice(hf * 512, hf * 512 + ncol)
                for fb in range(8):
                    W = Wu[nld[0] % NWB]
                    nld[0] += 1
                    dma("sp", W[:], wuv[:, :, fb * 512:(fb + 1) * 512], r=wu.blk, w=[W])
                    for f4 in range(4):
                        ft = fb * 4 + f4
                        bk = self.bank()
                        def mm(e, bk=bk, W=W, f4=f4, cols=cols):
                            ins = None
                            for kt in range(8):
                                ins = e.matmul(bk[:, 0:ncol], lhsT=W[:, kt, f4 * 128:(f4 + 1) * 128], rhs=XT[:, kt, cols], start=(kt == 0), stop=(kt == 7))
                            return ins
                        op("pe", mm, r=XTd + [W], w=[bk])
                        act(Hh[:, ft, 0:ncol], bk[:, 0:ncol], AF.Relu, [bk], [Hres[ft]])
                        tt("pool", Hh[:, ft, 0:ncol], Hh[:, ft, 0:ncol], Hh[:, ft, 0:ncol], ALU.mult, [Hres[ft]], [Hres[ft]])
                for wbk in range(8):
                    W = Wd[nld[1] % NWB]
                    nld[1] += 1
                    dma("sp", W[:], wdv[:, wbk * 4:(wbk + 1) * 4, :], r=wd.blk[wbk * 4:(wbk + 1) * 4], w=[W])
                    for rl in range(tph):
                        def mm(e, W=W, rl=rl, wbk=wbk):
                            ins = None
                            for f4 in range(4):
                                ft = wbk * 4 + f4
                                for dh in range(2):
                                    ins = e.matmul(self.banks[rl * 2 + dh][0:M, 0:512], lhsT=Hh[:, ft, rl * 128:rl * 128 + M],
                                                   rhs=W[:, f4, dh * 512:(dh + 1) * 512], start=(ft == 0), stop=(ft == 31))
                            return ins
                        op("pe", mm, r=Hres[wbk * 4:(wbk + 1) * 4] + [W], w=[self.banks[rl * 2], self.banks[rl * 2 + 1]])
                for rl in range(tph):
                    r = hf * tph + rl
                    for dh in range(2):
                        bk = self.banks[rl * 2 + dh]
                        tt("dve", Rt[0:M, r, dh * 512:(dh + 1) * 512], bk[0:M, 0:512], Rt[0:M, r, dh * 512:(dh + 1) * 512], ALU.add, [bk, Rres[r]], [Rres[r]])
        GLres = Res("GL")
        dbgY = self.dout("dbg_Y", [2, 128, 8, D]) if "s5" in self.dbg else None
        dbgF = self.dout("dbg_F", [2, 128, NG]) if "s5" in self.dbg else None

        def headnorm(ps_list, nh, outb, par=0, outf=None):
            M = cx["M"]
            sqf, qss = sqf2[par], qss2[par]
            c0 = 0
            for bk, ncol in ps_list:
                act(sqf[0:M, c0:c0 + ncol], bk[0:M, 0:ncol], AF.Square, [bk], [sqf])
                c0 += ncol
            op("dve", lambda e: e.tensor_reduce(out=qss[0:M, 0:nh], in_=sqf[0:M, 0:nh * 64].rearrange("p (h d) -> p h d", d=64),
                                                axis=mybir.AxisListType.X, op=ALU.add), [sqf], [qss])
            act(qss[0:M, 0:nh], qss[0:M, 0:nh], AF.Ln, [qss, epsT], [qss], scale=1.0 / 64, bias=epsT[0:M])
            act(qss[0:M, 0:nh], qss[0:M, 0:nh], AF.Exp, [qss], [qss], scale=-0.5)
            c0 = 0
            for bk, ncol in ps_list:
                h0_, hn_ = c0 // 64, ncol // 64
                v3 = lambda a: a.rearrange("p (h d) -> p h d", d=64)
                tt("dve", v3(outb[0:M, c0:c0 + ncol]), v3(bk[0:M, 0:ncol]), ins_b(qss[0:M, h0_:h0_ + hn_], 2, 64), ALU.mult, [bk, qss], [outb])
                if outf is not None:
                    tt("dve", v3(outf[0:M, c0:c0 + ncol]), v3(bk[0:M, 0:ncol]), ins_b(qss[0:M, h0_:h0_ + hn_], 2, 64), ALU.mult, [bk, qss], [outf])
                    tt("pool", v3(outf[0:M, c0:c0 + ncol]), v3(outf[0:M, c0:c0 + ncol]), ins_b(gk[0:M], 1, hn_), ALU.mult, [outf, gk], [outf])
                c0 += ncol

        def attn_phase(b_, c_):
            self.P.barrier()
            dma("sp", abias[:], io["c_abias"], w=[abias])
            wkv, wq, wo = wb["wkv"], wb["wq"], wb["wo"]
            dma("sp", Wkv[:], wkv[:].rearrange("(kt p) f -> p kt f", p=128), r=wkv.blk, w=[Wkv])
            for j in range(2):
                dma("sp", Wq[j][:], wq[:].rearrange("(kt p) f -> p kt f", p=128)[:, :, j * 512:(j + 1) * 512], r=wq.blk, w=[Wq[j]])
            rmsnorm(4)
            to_xt(HN, HNr, perm=True)
            if c_ == 1:
                cp("pool", KT[:, :, 0, :], KT[:, :, 8, :], [KTres[8]], [KTres[0]])
                cp("pool", VP[:, 0], VP[:, 8], [VPres[8]], [VPres[0]])
            kvb = {}

            def kv_proj(qb):
                bk = self.bankp("kvp", [0, 1, 2])
                def kvf(e, bk=bk, qb=qb):
                    ins = None
                    for kt in range(8):
                        ins = e.matmul(bk[:, 0:512], lhsT=XT[:, kt, qb * 128:(qb + 1) * 128], rhs=Wkv[:, kt, :], start=(kt == 0), stop=(kt == 7))
                    return ins
                op("pe", kvf, r=XTd + [Wkv], w=[bk])
                kvb[qb] = bk

            def kv_epi(qb):
                bk = kvb.pop(qb)
                Kf_, Vf_, Kb_ = Kf2[qb % 2], Vf2[qb % 2], Kb2[qb % 2]
                last = (c_ == 1 and qb == 7)
                cp("act", Vf_[:], bk[:, 256:512], [bk], [Vf_])
                cp("pool", VP[:, qb + 1, :, 0:64], Vf_[:].rearrange("p (h d) -> p h d", d=64), [Vf_], [VPres[qb + 1]])
                headnorm([(bk, 256)], 4, Kb_, qb % 2, outf=(Kf_ if last else None))
                bk2 = self.bankp("kvt", [3, 4])
                bk2b = bk2[:].bitcast(BF16)
                def ktf(e, bk2b=bk2b, Kb_=Kb_):
                    ins = None
                    for hk in range(4):
                        ins = e.transpose(out=bk2b[0:64, hk * 128:(hk + 1) * 128], in_=Kb_[:, hk * 64:(hk + 1) * 64], identity=identb[:])
                    return ins
                op("pe", ktf, r=[Kb_, identb], w=[bk2])
                act(KT[:, :, qb + 1, :], bk2b[0:64, 0:512].rearrange("p (h k) -> p h k", h=4), AF.Copy, [bk2, gqk], [KTres[qb + 1]], scale=gqk[:])
                if last:
                    kv_ = io["kb_p"][b_].rearrange("(kl r) h d -> r kl (h d)", r=8)
                    vv_ = io["vb_p"][b_].rearrange("(kl r) h d -> r kl (h d)", r=8)
                    for r in range(8):
                        dma("sp", kv_[r], Kf_[r * 16:(r + 1) * 16, :], r=[Kf_])
                        dma("sp", vv_[r], Vf_[r * 16:(r + 1) * 16, :], r=[Vf_])

            kv_proj(0)
            for qb in range(8):
                if qb + 1 < 8:
                    kv_proj(qb + 1)
                kv_epi(qb)
            rmsnorm(1)
            to_xt(HN, HNr, perm=True)

            def stage_a(qb):
                qt = QT[qb % 2]
                Qb_ = Qb2[qb % 2]
                bq = [self.bankp("qp", [0, 1]), self.bankp("qp", [0, 1])]
                for j in range(2):
                    def qf(e, bk=bq[j], j=j, qb=qb):
                        ins = None
                        for kt in range(8):
                            ins = e.matmul(bk[:, 0:512], lhsT=XT[:, kt, qb * 128:(qb + 1) * 128], rhs=Wq[j][:, kt, :], start=(kt == 0), stop=(kt == 7))
                        return ins
                    op("pe", qf, r=XTd + [Wq[j]], w=[bq[j]])
                headnorm([(bq[0], 512), (bq[1], 512)], 16, Qb_, qb % 2)

            def stage_a2(qb):
                qt = QT[qb % 2]
                Qb_ = Qb2[qb % 2]
                for half in range(2):
                    bk2 = self.bankp("qt", [2])
                    bk2b = bk2[:].bitcast(BF16)
                    def qtf(e, bk2b=bk2b, half=half, Qb_=Qb_):
                        ins = None
                        for hh in range(8):
                            h = half * 8 + hh
                            ins = e.transpose(out=bk2b[0:64, hh * 128:(hh + 1) * 128], in_=Qb_[:, h * 64:(h + 1) * 64], identity=identb[:])
                        return ins
                    op("pe", qtf, r=[Qb_, identb], w=[bk2])
                    cp("act" if half == 0 else "dve", qt[:, half * 8:(half + 1) * 8, :], bk2b[0:64, 0:1024].rearrange("p (h k) -> p h k", h=8), [bk2], [qt])

            def stage_b(qb):
                qt = QT[qb % 2]
                kbs = [1] if (c_ == 0 and qb == 0) else [0, 1]
                ot = Otok[qb % 2]
                sbanks = {}

                def scores(hk):
                    lst = []
                    for kb in kbs:
                        slot = qb + kb
                        bk = self.bankp("sc", [3, 4, 5, 6])
                        def scf_(e, bk=bk, hk=hk, slot=slot, kb=kb):
                            e.matmul(bk[:, 0:512], lhsT=KT[:, hk, slot, :], rhs=qt[:, 4 * hk:4 * hk + 4, :].rearrange("p h k -> p (h k)"), start=True, stop=False)
                            e.matmul(bk[:, 0:512], lhsT=identb[:], rhs=abias[:, 0, kb, 4 * hk:4 * hk + 4, :].rearrange("p h k -> p (h k)"), start=False, stop=False)
                            return e.matmul(bk[:, 0:512], lhsT=identb[:], rhs=abias[:, 1, kb, 4 * hk:4 * hk + 4, :].rearrange("p h k -> p (h k)"), start=False, stop=True)
                        op("pe", scf_, r=[KTres[slot], qt, abias, identb], w=[bk])
                        lst.append((bk, kb, slot))
                    sbanks[hk] = lst

                exd = {}

                def rest1(hk):
                    exs = []
                    for bk, kb, slot in sbanks.pop(hk):
                        ex = exb[(hk % 2) * 2 + kb]
                        act(ex[:], bk[:, 0:512], AF.Exp, [bk], [ex])
                        exs.append((ex, slot))
                    exd[hk] = exs

                def rest2(hk):
                    exs = exd.pop(hk)
                    bo = self.bankp("pv", [7])
                    def pvf(e, bo=bo, exs=exs, hk=hk):
                        ins = None
                        for hl in range(4):
                            for i, (ex, slot) in enumerate(exs):
                                ins = e.matmul(bo[:, hl * 65:(hl + 1) * 65], lhsT=ex[:, hl * 128:(hl + 1) * 128], rhs=VP[:, slot, hk, :],
                                               start=(i == 0), stop=(i == len(exs) - 1))
                        return ins
                    op("pe", pvf, r=[e_ for e_, _ in exs] + [VPres[sl] for _, sl in exs], w=[bo])
                    bov = bo[:, 0:260].rearrange("p (h c) -> p h c", c=65)
                    dn_ = den2[hk % 2]
                    tt("dve", dn_[:], bov[:, :, 64], esink[:, 4 * hk:4 * hk + 4], ALU.add, [bo, esink], [dn_])
                    op("dve", lambda e, dn_=dn_: e.reciprocal(out=dn_[:], in_=dn_[:]), [dn_], [dn_])
                    tt("dve", ot[:, hk * 256:(hk + 1) * 256].rearrange("p (h d) -> p h d", d=64), bov[:, :, 0:64], ins_b(dn_[:], 2, 64), ALU.mult, [bo, dn_], [ot])

                scores(0)
                rest1(0)
                for hk in range(4):
                    if hk + 1 < 4:
                        scores(hk + 1)
                        rest1(hk + 1)
                    rest2(hk)
                bk3 = self.bankp("qt", [2])
                bk3b = bk3[:].bitcast(BF16)
                def otf(e, bk3b=bk3b, ot=ot):
                    ins = None
                    for dt_ in range(8):
                        ins = e.transpose(out=bk3b[:, dt_ * 128:(dt_ + 1) * 128], in_=ot[:, dt_ * 128:(dt_ + 1) * 128], identity=identb[:])
                    return ins
                op("pe", otf, r=[ot, identb], w=[bk3])
                o_ = OTt[:].rearrange("p t (r k) -> p t r k", r=8)[:, :, :, qb * 16:(qb + 1) * 16]
                i_ = bk3b[:, 0:1024].rearrange("p (t r l) -> p t r l", t=8, r=8)
                cp("act", o_, i_, [bk3], OTd)

            stage_a(0)
            stage_a2(0)
            for qb in range(8):
                if qb + 1 < 8:
                    stage_a(qb + 1)
                stage_b(qb)
                if qb + 1 < 8:
                    stage_a2(qb + 1)
            for j in range(2):
                dma("sp", Wo[j][:], wo[:].rearrange("(kt p) f -> p kt f", p=128)[:, :, j * 512:(j + 1) * 512], r=wo.blk, w=[Wo[j]])
            for r in range(8):
                for j in range(2):
                    bk = self.bank()
                    def of(e, bk=bk, r=r, j=j):
                        ins = None
                        for kt in range(8):
                            ins = e.matmul(bk[:, 0:512], lhsT=OTt[:, kt, r * 128:(r + 1) * 128], rhs=Wo[j][:, kt, :], start=(kt == 0), stop=(kt == 7))
                        return ins
                    op("pe", of, r=OTd + [Wo[j]], w=[bk])
                    tt("dve", Rt[:, r, j * 512:(j + 1) * 512], bk[:, 0:512], Rt[:, r, j * 512:(j + 1) * 512], ALU.add, [bk, Rres[r]], [Rres[r]])

        identf = self.identf
        A1r = self.A1r

        def sample_s5():
            self.P.barrier()
            for t in range(4):
                dma("sp", Rt[t * 16:(t + 1) * 16, 0, :], io["xs"][:, t, :], w=[Rres[0]])
            rmsnorm(0)
            for t in range(4):
                dma("sp", HNs[:, :, t, :], HN[t * 16:(t + 1) * 16, 0, :].rearrange("p (g c) -> p g c", c=16), r=[HNr[0]], w=[HNs])
            def ss1(gb):
                gsl = slice(gb * 8, (gb + 1) * 8)
                csl = slice(gb * 128, (gb + 1) * 128)
                t_ = stb[gb % 2]
                H0T, H0Ts, Fbs, UTs = H0T2[gb % 2], H0Ts2[gb % 2], Fbs2[gb % 2], UTs2[gb % 2]
                for key, src in (("W1", sW1), ("W2", sW2), ("W3", sW3)):
                    dma("sp", t_[key][:], src[:, gsl], r=[src], w=[t_[key]])
                h0, h02 = H0k[gb % 2], H0k2[gb % 2]
                dma("sp", h0[:, :, 0:64], io["st_re"][:, gsl, :], w=[h0])
                dma("sp", h0[:, :, 64:128], io["st_im"][:, gsl, :], w=[h0])
                dma("sp", h02[:, :, 0:64], io["st_im"][:, gsl, :], w=[h02])
                dma("sp", h02[:, :, 64:128], io["st_re"][:, gsl, :], w=[h02])
                for src, dst, eng in ((h0, H0T, "act"), (h02, H0Ts, "dve")):
                    bk = self.bank()
                    def hf_(e, bk=bk, src=src):
                        ins = None
                        for gl in range(8):
                            ins = e.transpose(out=bk[:, gl * 16:(gl + 1) * 16], in_=src[:, gl, :], identity=identf[0:16, 0:16])
                        return ins
                    op("pe", hf_, r=[src, identf], w=[bk])
                    cp(eng, dst[:].rearrange("p g s -> p (g s)"), bk[:, 0:128], [bk], [dst])
                tt("dve", za1[:], H0T[:], ins_b(A1r[:, gsl], 2, 16), ALU.mult, [H0T, A1r], [za1])
                tt("dve", za2[:], H0Ts[:], ins_b(self.A1B[:, gsl], 2, 16), ALU.mult, [H0Ts, self.A1B], [za2])
                tt("pool", Fbs[:], za1[:], za2[:], ALU.add, [za1, za2], [Fbs])
                bk = self.bank()
                bkb = bk[:].bitcast(BF16)
                def uf_(e, bkb=bkb, gb=gb):
                    ins = None
                    for gl in range(8):
                        ins = e.transpose(out=bkb[0:64, gl * 16:(gl + 1) * 16], in_=HNs[:, gb * 8 + gl].rearrange("p t c -> p (t c)"), identity=identb[0:16, 0:16])
                    return ins
                op("pe", uf_, r=[HNs, identb], w=[bk])
                cp("act", UTs[:].rearrange("p g s -> p (g s)"), bkb[0:64, 0:128], [bk], [UTs])
            def ss2(gb):
                gsl = slice(gb * 8, (gb + 1) * 8)
                csl = slice(gb * 128, (gb + 1) * 128)
                t_ = stb[gb % 2]
                H0T, H0Ts, Fbs, UTs = H0T2[gb % 2], H0Ts2[gb % 2], Fbs2[gb % 2], UTs2[gb % 2]
                bk = self.bank()
                def yf_(e, bk=bk, t_=t_):
                    ins = None
                    for gl in range(8):
                        e.matmul(bk[0:16, gl * 64:(gl + 1) * 64], lhsT=Fbs[:, gl, :], rhs=t_["W3"][:, gl, 0:64], start=True, stop=False)
                        ins = e.matmul(bk[0:16, gl * 64:(gl + 1) * 64], lhsT=UTs[:, gl, :], rhs=t_["W2"][0:64, gl, 0:64], start=False, stop=True)
                    return ins
                op("pe", yf_, r=[Fbs, UTs, t_["W3"], t_["W2"]], w=[bk])
                tt("pool", du16[:].rearrange("p t (g c) -> p g t c", g=8), HNs[:, gsl, :, :],
                   ins_b(Dv[0:16, csl].rearrange("p (g c) -> p g c", g=8), 2, 4), ALU.mult, [HNs, Dv], [du16])
                tt("dve", yt16[:].rearrange("p t (g c) -> p g t c", g=8), bk[0:16, 0:512].rearrange("p (g t c) -> p g t c", g=8, t=4),
                   du16[:].rearrange("p t (g c) -> p g t c", g=8), ALU.add, [bk, du16], [yt16])
                act(GLs[:, :, csl], yt16[:], AF.Gelu_apprx_tanh, [yt16], [GLs])
                bx, bxs = self.bank(), self.bank()
                for bk_, c0 in ((bx, 0), (bxs, 64)):
                    def xf_(e, bk_=bk_, c0=c0, t_=t_):
                        ins = None
                        for gl in range(8):
                            ins = e.matmul(bk_[:, gl * 16:(gl + 1) * 16], lhsT=t_["W1"][0:64, gl, c0:c0 + 128], rhs=UTs[:, gl, :], start=True, stop=True)
                        return ins
                    op("pe", xf_, r=[t_["W1"], UTs], w=[bk_])
                bc = lambda t__: ins_b(t__[:, gsl], 2, 16)
                v3 = lambda a: a.rearrange("p (g s) -> p g s", g=8)
                tt("dve", zt1[:], v3(bx[:, 0:128]), bc(self.A3r), ALU.mult, [bx, self.A3r], [zt1])
                tt("dve", zt2[:], v3(bxs[:, 0:128]), bc(self.A3B), ALU.mult, [bxs, self.A3B], [zt2])
                tt("pool", zt1[:], zt1[:], zt2[:], ALU.add, [zt1, zt2], [zt1])
                tt("dve", zt2[:], H0T[:], bc(self.A4r), ALU.mult, [H0T, self.A4r], [zt2])
                tt("pool", zt1[:], zt1[:], zt2[:], ALU.add, [zt1, zt2], [zt1])
                tt("dve", zt2[:], H0Ts[:], bc(self.A4B), ALU.mult, [H0Ts, self.A4B], [zt2])
                tt("pool", Hs_[:], zt1[:], zt2[:], ALU.add, [zt1, zt2], [Hs_])
                for q in range(2):
                    bk = self.bank()
                    def tf_(e, bk=bk, q=q):
                        ins = None
                        for gq in range(4):
                            ins = e.transpose(out=bk[0:16, gq * 128:(gq + 1) * 128], in_=Hs_[:, 4 * q + gq, :], identity=identf[:])
                        return ins
                    op("pe", tf_, r=[Hs_, identf], w=[bk])
                    cp("act", Hout[:, 4 * q:4 * q + 4, :].rearrange("p g n -> p (g n)"), bk[0:16, 0:512], [bk], [Hout])
                dma("sp", io["sre_s"][:, gsl, :], Hout[:, :, 0:64], r=[Hout])
                dma("sp", io["sim_s"][:, gsl, :], Hout[:, :, 64:128], r=[Hout])
            ss1(0)
            for gb in range(8):
                if gb + 1 < 8:
                    ss1(gb + 1)
                ss2(gb)
            for t in range(4):
                dma("sp", GL[t * 16:(t + 1) * 16, 0, :], GLs[:, t, :], r=[GLs], w=[GLres])

        def sample_attn():
            M = 64
            self.P.barrier()
            self.bank_pool = [0, 1, 2, 3]
            wkv, wq, wo = wb["wkv"], wb["wq"], wb["wo"]
            dma("sp", Wkv[:], wkv[:].rearrange("(kt p) f -> p kt f", p=128), r=wkv.blk, w=[Wkv])
            for j in range(2):
                dma("sp", Wq[j][:], wq[:].rearrange("(kt p) f -> p kt f", p=128)[:, :, j * 512:(j + 1) * 512], r=wq.blk, w=[Wq[j]])
            dma("sp", sbc[:], io["c_sbc"].rearrange("p h t -> p (h t)"), w=[sbc])
            dma("sp", sbn[:], io["c_sbn"].rearrange("p k b l t -> p k (b l t)"), w=[sbn])
            op("pool", lambda e: e.memset(onesb[:], 1.0), w=[onesb])
            rmsnorm(4)
            to_xt(HN, HNr)
            bk = self.bank()
            def kvf(e, bk=bk):
                ins = None
                for kt in range(8):
                    ins = e.matmul(bk[0:M, 0:512], lhsT=XT[:, kt, 0:M], rhs=Wkv[:, kt, :], start=(kt == 0), stop=(kt == 7))
                return ins
            op("pe", kvf, r=XTd + [Wkv], w=[bk])
            cp("act", Vf[0:M], bk[0:M, 256:512], [bk], [Vf])
            cp("pool", VPn[:], Vf[0:M].rearrange("p (h d) -> p h d", d=64), [Vf], [VPn])
            headnorm([(bk, 256)], 4, Kb, 0, outf=Kf)
            dma("sp", io["kb_s"][:, 0:124], io["ck"][:, 4:128])
            dma("sp", io["vb_s"][:, 0:124], io["cv"][:, 4:128])
            for t in range(4):
                dma("sp", io["kb_s"][:, 124 + t].rearrange("b h d -> b (h d)"), Kf[t * 16:(t + 1) * 16, :], r=[Kf])
                dma("sp", io["vb_s"][:, 124 + t].rearrange("b h d -> b (h d)"), Vf[t * 16:(t + 1) * 16, :], r=[Vf])
            bk2 = self.bank()
            bk2b = bk2[:].bitcast(BF16)
            def ktf(e, bk2b=bk2b):
                ins = None
                for hk in range(4):
                    ins = e.transpose(out=bk2b[0:64, hk * 64:(hk + 1) * 64], in_=Kb[0:M, hk * 64:(hk + 1) * 64], identity=identb[0:M, 0:M])
                return ins
            op("pe", ktf, r=[Kb, identb], w=[bk2])
            act(KTn[:].rearrange("p h k -> p (h k)"), bk2b[0:64, 0:256], AF.Copy, [bk2, gqk], [KTn], scale=gqk[:])
            rmsnorm(1)
            to_xt(HN, HNr)
            bq = [self.bank(), self.bank()]
            for j in range(2):
                def qf(e, bk=bq[j], j=j):
                    ins = None
                    for kt in range(8):
                        ins = e.matmul(bk[0:M, 0:512], lhsT=XT[:, kt, 0:M], rhs=Wq[j][:, kt, :], start=(kt == 0), stop=(kt == 7))
                    return ins
                op("pe", qf, r=XTd + [Wq[j]], w=[bq[j]])
            headnorm([(bq[0], 512), (bq[1], 512)], 16, Qb, 0)
            bk2 = self.bank()
            bk2b = bk2[:].bitcast(BF16)
            def qtf(e, bk2b=bk2b):
                ins = None
                for h in range(16):
                    ins = e.transpose(out=bk2b[0:64, h * 64:(h + 1) * 64], in_=Qb[0:M, h * 64:(h + 1) * 64], identity=identb[0:M, 0:M])
                return ins
            op("pe", qtf, r=[Qb, identb], w=[bk2])
            cp("act", QTs[:].rearrange("p h k -> p (h k)"), bk2b[0:64, 0:1024], [bk2], [QTs])
            for hk in range(4):
                cp("pool", QTs2[:, hk, :].rearrange("p (b l t) -> p b l t", b=16, l=4),
                   QTs[:, 4 * hk:4 * hk + 4, :].rearrange("p l (t b) -> p b l t", b=16), [QTs], [QTs2])
            for hp in range(2):
                bk = self.bank()
                def snf(e, bk=bk, hp=hp):
                    ins = None
                    for q in range(2):
                        hk = 2 * hp + q
                        ins = e.matmul(bk[0:64, q * 256:(q + 1) * 256], lhsT=KTn[:, hk, :], rhs=QTs2[:, hk, :], start=True, stop=True)
                    return ins
                op("pe", snf, r=[KTn, QTs2], w=[bk])
                tt("dve", dn[:, :], bk[0:64, 0:512], sbn[:, 2 * hp:2 * hp + 2, :].rearrange("p k c -> p (k c)"), ALU.add, [bk, sbn], [dn])
                act(exn[:, 2 * hp:2 * hp + 2, :].rearrange("p k c -> p (k c)"), dn[:, :], AF.Exp, [dn], [exn])
            NUM = [self.banks[4], self.banks[5]]
            DEN = [self.banks[6], self.banks[7]]
            def sp_stage(b):
                kc, vc, vpc, ktc, ex_ = Kc[b % 2], Vc[b % 2], VPc[b % 2], KTc[b % 2], exc[b % 2]
                kcb_ = Kcb2[b % 2]
                dma("sp", kc[:], io["ck"][b].rearrange("j h d -> j (h d)"), w=[kc])
                dma("sp", vc[:], io["cv"][b].rearrange("j h d -> j (h d)"), w=[vc])
                cp("pool", kcb_[:], kc[:], [kc], [kcb_])
                cp("pool", vpc[:].rearrange("p h d -> p (h d)"), vc[:], [vc], [vpc])
                bk2 = self.bank()
                bk2b = bk2[:].bitcast(BF16)
                def kcf(e, bk2b=bk2b, kcb_=kcb_):
                    ins = None
                    for hk in range(4):
                        ins = e.transpose(out=bk2b[0:64, hk * 128:(hk + 1) * 128], in_=kcb_[:, hk * 64:(hk + 1) * 64], identity=identb[:])
                    return ins
                op("pe", kcf, r=[kcb_, identb], w=[bk2])
                act(ktc[:].rearrange("p h k -> p (h k)"), bk2b[0:64, 0:512], AF.Copy, [bk2, gq8], [ktc], scale=gq8[:])
                bk = self.bank()
                def scf_(e, bk=bk, ktc=ktc, b=b):
                    ins = None
                    for hk in range(4):
                        ins = e.matmul(bk[:, hk * 16:(hk + 1) * 16], lhsT=ktc[:, hk, :], rhs=QTs2[:, hk, b * 16:(b + 1) * 16], start=True, stop=True)
                    return ins
                op("pe", scf_, r=[ktc, QTs2], w=[bk])
                sc_ = scf[b % 2]
                tt("dve", sc_[:, 0:64], bk[:, 0:64], sbc[:], ALU.add, [bk, sbc], [sc_])
                act(ex_[:], sc_[:, 0:64], AF.Exp, [sc_], [ex_])

            def sv_stage(b):
                vpc, ex_ = VPc[b % 2], exc[b % 2]
                nb_, db_ = NUM[b // 8], DEN[b // 8]
                def pvf(e, nb_=nb_, db_=db_, vpc=vpc, ex_=ex_, b=b):
                    ins = None
                    for hk in range(4):
                        c0 = (b % 8) * 64 + hk * 16
                        e.matmul(nb_[0:64, c0:c0 + 16], lhsT=vpc[:, hk, :], rhs=ex_[:, hk * 16:(hk + 1) * 16], start=True, stop=False)
                        e.matmul(nb_[0:64, c0:c0 + 16], lhsT=VPn[:, hk, :], rhs=exn[:, hk, b * 16:(b + 1) * 16], start=False, stop=True)
                        e.matmul(db_[0:64, c0:c0 + 16], lhsT=onesb[:, 0:64], rhs=ex_[:, hk * 16:(hk + 1) * 16], start=True, stop=False)
                        ins = e.matmul(db_[0:64, c0:c0 + 16], lhsT=onesb[0:64, 0:64], rhs=exn[:, hk, b * 16:(b + 1) * 16], start=False, stop=True)
                    return ins
                op("pe", pvf, r=[vpc, ex_, VPn, exn, onesb], w=[nb_, db_])

            sp_stage(0)
            for b in range(16):
                if b + 1 < 16:
                    sp_stage(b + 1)
                sv_stage(b)
            for half in range(2):
                nb_, db_ = NUM[half], DEN[half]
                dn4 = dn[:, :].rearrange("p (b h t) -> p b h t", b=8, h=16)
                es4 = ins_b(ins_b(esink[0:64, 0:16], 1, 8), 3, 4)
                tt("dve", dn4, db_[0:64, 0:512].rearrange("p (b h t) -> p b h t", b=8, h=16), es4, ALU.add, [db_, esink], [dn])
                op("dve", lambda e: e.reciprocal(out=dn[:, :], in_=dn[:, :]), [dn], [dn])
                o4 = OTs[:].rearrange("p h (t b) -> p b h t", b=16)[:, half * 8:(half + 1) * 8]
                tt("dve", o4, nb_[0:64, 0:512].rearrange("p (b h t) -> p b h t", b=8, h=16), dn4, ALU.mult, [nb_, dn], [OTs])
            wov = wo[:].rearrange("(h p) f -> p h f", p=64)
            for j in range(2):
                dma("sp", WoS[:], wov[:, :, j * 512:(j + 1) * 512], r=wo.blk, w=[WoS])
                bk = self.bank()
                def of(e, bk=bk):
                    ins = None
                    for h in range(16):
                        ins = e.matmul(bk[0:M, 0:512], lhsT=OTs[:, h, :], rhs=WoS[:, h, :], start=(h == 0), stop=(h == 15))
                    return ins
                op("pe", of, r=[OTs, WoS], w=[bk])
                tt("dve", Rt[0:M, 0, j * 512:(j + 1) * 512], bk[0:M, 0:512], Rt[0:M, 0, j * 512:(j + 1) * 512], ALU.add, [bk, Rres[0]], [Rres[0]])
            self.bank_pool = None

        nchunks = self.dbg.get("nchunks", 4)
        if "endprobe" in self.dbg:
            self.dbgE = self.dout("dbg_E", [128, 64])
        if "l0" in self.dbg:
            self.dbgR = self.dout("dbg_R", [2, 128, 8, D])
        for ch in range(nchunks):
            b_, c_ = ch // 2, ch % 2
            xv = xp[b_, c_ * 1024:(c_ + 1) * 1024, :].rearrange("(k r) d -> k r d", r=8)
            for r in range(8):
                dma("sp", Rt[:, r, :], xv[:, r, :], w=[Rres[r]])
            rmsnorm(0, gmajor=True)
            zin, zsin = zc[ch % 2], zcs[ch % 2]
            zout, zsout = zc[(ch + 1) % 2], zcs[(ch + 1) % 2]
            if c_ == 0:
                op("pool", lambda e, z=zin: e.memset(z[:], 0.0), w=[zin])
                op("pool", lambda e, z=zsin: e.memset(z[:], 0.0), w=[zsin])
            def s1(gb):
                gsl = slice(gb * 8, (gb + 1) * 8)
                t_ = tb[gb % 2]
                for key, src in (("W1", sW1), ("W2", sW2), ("W3", sW3), ("Pr", sPr), ("Pi", sPi), ("Qr", sQr), ("QB", sQB)):
                    dma("sp", t_[key][:], src[:, gsl], r=[src], w=[t_[key]])
                ut, zb, htb = UTb[gb % 2], Zb[gb % 2], HTb[gb % 2]
                bk = self.bankp("s5ut", [0])
                bkb = bk[:].bitcast(BF16)
                def utf(e, bkb=bkb, gb=gb):
                    ins = None
                    for gl in range(8):
                        gi = gb * 8 + gl
                        ins = e.transpose(out=bkb[:, gl * 128:(gl + 1) * 128], in_=HNg[:, gi].rearrange("p s c -> p (s c)"), identity=identb[:])
                    return ins
                op("pe", utf, r=HNr + [identb], w=[bk])
                cp("act", ut[:, 0:4, :].rearrange("p g k -> p (g k)"), bkb[:, 0:512], [bk], [utres[gb % 2][0]])
                cp("dve", ut[:, 4:8, :].rearrange("p g k -> p (g k)"), bkb[:, 512:1024], [bk], [utres[gb % 2][1]])
                for j in range(4):
                    bk = self.bankp("s5x", [1, 2])
                    def xf(e, bk=bk, j=j, ut=ut, t_=t_):
                        ins = None
                        for q in range(2):
                            gl = 2 * j + q
                            ins = e.matmul(bk[:, q * 256:(q + 1) * 256], lhsT=ut[:, gl, :], rhs=t_["W1"][:, gl, :], start=True, stop=True)
                        return ins
                    op("pe", xf, r=[utres[gb % 2][j // 2], t_["W1"]], w=[bk])
                    Xv = bk[:].rearrange("p (g c) -> p g c", g=2)
                    ta, tb_ = tA[j % 2], tB[j % 2]
                    x0 = Xv[:, :, 0:192].rearrange("p g (a n) -> p g a n", a=3)
                    x1 = Xv[:, :, 64:256].rearrange("p g (a n) -> p g a n", a=3)
                    prb = ins_b(t_["Pr"][:, 2 * j:2 * j + 2, :], 2, 3)
                    pib = ins_b(t_["Pi"][:, 2 * j:2 * j + 2, :], 2, 3)
                    ta4 = ta[:].rearrange("p g (a n) -> p g a n", a=3)
                    tb4 = tb_[:].rearrange("p g (a n) -> p g a n", a=3)
                    tt("dve", ta4, x0, prb, ALU.mult, [bk, t_["Pr"]], [ta])
                    tt("dve", tb4, x1, pib, ALU.mult, [bk, t_["Pi"]], [tb_])
                    z4 = zb[:, 2 * j:2 * j + 2, :].rearrange("p g (a n) -> p g a n", a=3)
                    def sl(v, lo, step, cnt):
                        return bass.AP(v.tensor, v.offset + lo * v.ap[2][0], [list(v.ap[0]), list(v.ap[1]), [step * v.ap[2][0], cnt], list(v.ap[3])])
                    tt("pool", sl(z4, 0, 2, 2), sl(ta4, 0, 2, 2), sl(tb4, 0, 2, 2), ALU.subtract, [ta, tb_], [zb])
                    tt("pool", sl(z4, 1, 1, 1), sl(ta4, 1, 1, 1), sl(tb4, 1, 1, 1), ALU.add, [ta, tb_, zb], [zb])
            def s23(gb):
                gsl = slice(gb * 8, (gb + 1) * 8)
                t_ = tb[gb % 2]
                ut, zb, htb = UTb[gb % 2], Zb[gb % 2], HTb[gb % 2]
                for gl in range(8):
                    gi = gb * 8 + gl
                    bk = self.bankp("s5c", [3, 4, 5])
                    def cf(e, bk=bk, gl=gl, zb=zb):
                        e.matmul(bk[:, 0:129], lhsT=zb[:, gl, 0:128], rhs=trib[:, 0:129], start=True, stop=True)
                        return e.matmul(bk[:, 256:385], lhsT=zb[:, gl, 64:192], rhs=trib[:, 0:129], start=True, stop=True)
                    op("pe", cf, r=[zb, trib], w=[bk])
                    stt(t1b[:, gl, :], bk[:, 0:129], zin[:, gi:gi + 1], t_["Qr"][:, gl, :], ALU.add, ALU.mult, [bk, zin, t_["Qr"]], [t1res[gl]])
                    stt(t2b[:, gl, :], bk[:, 256:385], zsin[:, gi:gi + 1], t_["QB"][:, gl, :], ALU.add, ALU.mult, [bk, zsin, t_["QB"]], [t2res[gl]])
                tt("dve", htb[:, 0:4, :], t1b[:, 0:4, 0:128], t2b[:, 0:4, 0:128], ALU.add, t1res[0:4] + t2res[0:4], [hres[gb % 2][0]])
                tt("pool", htb[:, 4:8, :], t1b[:, 4:8, 0:128], t2b[:, 4:8, 0:128], ALU.add, t1res[4:8] + t2res[4:8], [hres[gb % 2][1]])
                tt("pool", zout[:, gsl], t1b[:, :, 128], t2b[:, :, 128], ALU.add, t1res + t2res, [zout])
                du_, yt = du[gb % 2], ytmp[gb % 2]
                csl = slice(gb * 128, (gb + 1) * 128)
                tt("pool", du_[:].rearrange("p t (g c) -> p g t c", g=8), HNg[:, gsl, :, :],
                   ins_b(Dv[:, csl].rearrange("p (g c) -> p g c", g=8), 2, 8), ALU.mult, HNr + [Dv], [du_])
                for q in range(2):
                    bk = self.bankp("s5y", [6, 7])
                    def yf(e, bk=bk, q=q, htb=htb, ut=ut, t_=t_):
                        ins = None
                        for gq in range(4):
                            gl = 4 * q + gq
                            e.matmul(bk[:, gq * 128:(gq + 1) * 128], lhsT=htb[:, gl, :], rhs=t_["W3"][:, gl, :], start=True, stop=False)
                            ins = e.matmul(bk[:, gq * 128:(gq + 1) * 128], lhsT=ut[:, gl, :], rhs=t_["W2"][:, gl, :], start=False, stop=True)
                        return ins
                    op("pe", yf, r=[hres[gb % 2][q], utres[gb % 2][q], t_["W3"], t_["W2"]], w=[bk])
                    bv = bk[:].rearrange("p (g t c) -> p g t c", g=4, t=8)
                    yv = yt[:, :, q * 64:(q + 1) * 64].rearrange("p t (g c) -> p g t c", g=4)
                    dv = du_[:, :, q * 64:(q + 1) * 64].rearrange("p t (g c) -> p g t c", g=4)
                    tt("dve", yv, bv, dv, ALU.add, [bk, du_], [yt])
                if dbgY is not None and ch < 2:
                    dma("sp", dbgY[ch, :, :, csl], yt[:], r=[yt])
                act(GL[:, :, csl], yt[:], AF.Gelu_apprx_tanh, [yt], [GLres])
            s1(0)
            for gb in range(8):
                if gb + 1 < 8:
                    s1(gb + 1)
                s23(gb)
            bkT = self.bankp("s5ut", [0])
            op("pe", lambda e, bkT=bkT, zout=zout: e.transpose(out=bkT[0:64, 0:128], in_=zout[:], identity=identf[:]), r=[zout, identf], w=[bkT])
            cp("act", zsw[0:64, 0:64], bkT[0:64, 64:128], [bkT], [zsw])
            cp("act", zsw[0:64, 64:128], bkT[0:64, 0:64], [bkT, zsw], [zsw])
            bkT2 = self.bankp("s5x", [1, 2])
            op("pe", lambda e, bkT2=bkT2: e.transpose(out=bkT2[:, 0:64], in_=zsw[0:64, :], identity=identf[0:64, 0:64]), r=[zsw, identf], w=[bkT2])
            cp("act", zsout[:], bkT2[:, 0:64], [bkT2], [zsout])
            if c_ == 1:
                tt("dve", Hfin[:], zout[:], AinvR[:], ALU.mult, [zout, AinvR], [Hfin])
                tt("dve", Ht1[:], zsout[:], AinvB[:], ALU.mult, [zsout, AinvB], [Ht1])
                tt("dve", Hfin[:], Hfin[:], Ht1[:], ALU.add, [Hfin, Ht1], [Hfin])
                bk = self.bank()
                op("pe", lambda e, bk=bk: e.transpose(out=bk[0:64, 0:128], in_=Hfin[:], identity=self.identf[:]), r=[Hfin, self.identf], w=[bk])
                cp("act", Ht2[0:64, :], bk[0:64, 0:64], [bk], [Ht2])
                cp("act", Ht1[0:64, :], bk[0:64, 64:128], [bk, Ht1], [Ht1])
                dma("sp", io["sre_p"][b_], Ht2[0:64, :], r=[Ht2])
                dma("sp", io["sim_p"][b_], Ht1[0:64, :], r=[Ht1])
            if dbgF is not None and ch < 2:
                dma("sp", dbgF[ch], zout[:], r=[zout])
            stop = self.dbg.get("stop")
            if "endprobe" in self.dbg and ch == nchunks - 1 and stop is not None:
                self.P.barrier()
                dma("sp", self.dbgE[:], Rt[:, 0, 0:64], r=[Rres[0]])
            if stop == "s5":
                continue
            glu_phase()
            if stop == "glu":
                continue
            mlp_phase(0)
            if "l0" in self.dbg and ch < 2:
                for r in range(8):
                    dma("sp", self.dbgR[ch, :, r, :], Rt[:, r, :], r=[Rres[r]])
            if stop == "l0":
                self.P.barrier()
                continue
            attn_phase(b_, c_)
            mlp_phase(1)
            yv = io["yp"][b_, c_ * 1024:(c_ + 1) * 1024, :].rearrange("(k r) d -> k r d", r=8)
            for r in range(8):
                dma("sp", yv[:, r, :], Rt[:, r, :], r=[Rres[r]])
            self.P.barrier()
        if self.dbg.get("sample", True):
            cx["M"], cx["NT"] = 64, 1
            sample_s5()
            glu_phase()
            mlp_phase(0)
            sample_attn()
            mlp_phase(1)
            for t in range(4):
                dma("sp", io["ys"][:, t, :], Rt[t * 16:(t + 1) * 16, 0, :], r=[Rres[0]])

def make_consts():
    c = {}
    c["c_ident"] = np.eye(128, dtype=np.float32)
    j = np.arange(128)[:, None]
    k = np.arange(129)[None, :]
    c["c_tri"] = (j < k).astype(np.float32)
    p = np.arange(128)
    s_ = p // 16
    c["c_mask2"] = (s_[None, :] >= s_[:, None]).astype(np.float32)
    c["c_kcol"] = np.arange(128, dtype=np.float32)[:, None]
    c["c_ramp"] = np.tile(np.arange(129, dtype=np.float32)[None, :], (128, 1))
    tok = 8 * (p % 16) + (p // 16)
    slopes = 2.0 ** (-8.0 * np.arange(1, 17, dtype=np.float64) / 16)
    ab = np.zeros((128, 2, 16, 128), np.float32)
    for kb in range(2):
        dist = tok[None, :] - tok[:, None] + (128 if kb == 0 else 0)
        valid = (dist >= 0) & (dist < 128)
        for h in range(16):
            ab[:, kb, h, :] = np.where(valid, -slopes[h] * dist, -30000.0)
    import ml_dtypes
    ab_hi = ab.astype(ml_dtypes.bfloat16)
    ab_lo = (ab - ab_hi.astype(np.float32)).astype(ml_dtypes.bfloat16)
    c["c_abias"] = np.stack([ab_hi, ab_lo], axis=1)
    sbc = np.zeros((128, 16, 4), np.float32)
    jj = np.arange(128)
    for t in range(4):
        dist = t + 128 - jj
        valid = (dist >= 0) & (dist < 128)
        for h in range(16):
            sbc[:, h, t] = np.where(valid, -slopes[h] * dist, -30000.0)
    c["c_sbc"] = sbc
    sbn = np.full((64, 4, 16, 4, 4), -30000.0, np.float32)
    for tp in range(4):
        for sp_ in range(16):
            for hk in range(4):
                for hl in range(4):
                    for t in range(tp, 4):
                        sbn[tp * 16 + sp_, hk, sp_, hl, t] = -slopes[4 * hk + hl] * (t - tp)
    c["c_sbn"] = sbn
    return c


def build_program(dbg=None):
    b = Builder(dbg)
    b.P.emit(b.nc)
    return b


_CACHE = {}


def kernel(**inputs):
    if "b" not in _CACHE:
        _CACHE["b"] = build_program()
    b = _CACHE["b"]
    consts = make_consts()
    f32 = lambda a: np.ascontiguousarray(np.asarray(a, dtype=np.float32))
    in_maps = []
    for c in range(NCORES):
        m = dict(consts)
        for k, v in inputs.items():
            m[k] = f32(v)
        m["xp"] = f32(inputs["x_prompt"][2 * c:2 * c + 2])
        m["xs"] = f32(inputs["x_sample"][16 * c:16 * c + 16])
        m["st_re"] = f32(inputs["state_ssm_re"][0, 16 * c:16 * c + 16])
        m["st_im"] = f32(inputs["state_ssm_im"][0, 16 * c:16 * c + 16])
        m["ck"] = f32(inputs["cache_k"][16 * c:16 * c + 16])
        m["cv"] = f32(inputs["cache_v"][16 * c:16 * c + 16])
        in_maps.append({k: v for k, v in m.items() if k in b.dram})
    res = run_bass_kernel_spmd(b.nc, in_maps, core_ids=list(range(NCORES)))
    rs = res.results

    def cat(name, shape):
        if name in rs[0]:
            return np.concatenate([np.asarray(r[name], dtype=np.float32) for r in rs], axis=0)
        return np.zeros(shape, np.float32)
    y_prompt = cat("yp", (16, 2048, D))
    y_sample = cat("ys", (128, 4, D))
    re_p = cat("sre_p", (16, NG, 64))[None]
    im_p = cat("sim_p", (16, NG, 64))[None]
    k_p = cat("kb_p", (16, 128, 4, 64))
    v_p = cat("vb_p", (16, 128, 4, 64))
    re_s = cat("sre_s", (128, NG, 64))[None]
    im_s = cat("sim_s", (128, NG, 64))[None]
    k_s = cat("kb_s", (128, 128, 4, 64))
    v_s = cat("vb_s", (128, 128, 4, 64))
    return (y_prompt, y_sample, re_p, im_p, k_p, v_p, re_s, im_s, k_s, v_s)
```

```python
import contextlib
import math
import numpy as np
import concourse.bass as bass
import concourse.mybir as mybir
from concourse.bass_utils import run_bass_kernel_spmd

F32 = mybir.dt.float32
BF16 = mybir.dt.bfloat16
I32 = mybir.dt.int32
ALU = mybir.AluOpType
AF = mybir.ActivationFunctionType

NCORES = 8
D = 1024
NG = 64
EPS = 1e-6
TWO_PI_SAFE = 6.283185
INV_2PI = 1.0 / (2.0 * math.pi)


class Res:
    __slots__ = ("name", "w", "r")

    def __init__(self, name=""):
        self.name = name
        self.w = None
        self.r = []


class Op:
    __slots__ = ("eng", "fn", "deps", "dma", "sig", "tok", "dsem", "exempt")

    def __init__(self, eng, fn, dma):
        self.eng = eng
        self.fn = fn
        self.deps = []
        self.dma = dma
        self.sig = False
        self.tok = None
        self.dsem = None
        self.exempt = False


class Prog:
    ENGS = ("pe", "act", "dve", "pool", "sp")
    NDSEM = 8
    NDSEM_E = {"pool": 56}

    def __init__(self):
        self.q = {e: [] for e in self.ENGS}
        self.dma_hist = {e: [None] * self.NDSEM_E.get(e, self.NDSEM) for e in self.ENGS}
        self.dma_cnt = {e: 0 for e in self.ENGS}
        self.dma_semcount = {}
        self.pending = {e: [] for e in self.ENGS}

    def barrier(self):
        last = []
        for e in self.ENGS:
            for op in reversed(self.q[e]):
                if not op.dma:
                    last.append(op)
                    break
            for op in self.dma_hist[e]:
                if op is not None and not getattr(op, "exempt", False):
                    last.append(op)
        for e in self.ENGS:
            self.pending[e] = list(last)

    def add(self, eng, fn, reads=(), writes=(), dma=False):
        op = Op(eng, fn, dma)
        deps = list(self.pending[eng])
        self.pending[eng] = []
        for r in reads:
            if r.w is not None:
                deps.append(r.w)
        for r in writes:
            if r.w is not None:
                deps.append(r.w)
            deps.extend(r.r)
        if dma:
            k = self.dma_cnt[eng] % self.NDSEM_E.get(eng, self.NDSEM)
            self.dma_cnt[eng] += 1
            prev = self.dma_hist[eng][k]
            if prev is not None:
                deps.append(prev)
            self.dma_hist[eng][k] = op
            key = (eng, k)
            self.dma_semcount[key] = self.dma_semcount.get(key, 0) + 1
            op.dsem = key
            op.tok = (key, 16 * self.dma_semcount[key])
        seen = set()
        for d in deps:
            if id(d) in seen or d is op:
                continue
            seen.add(id(d))
            if d.eng == "pe" and eng == "pe" and not d.dma and not dma:
                continue
            d.sig = True
            op.deps.append(d)
        for r in reads:
            if not dma:
                r.r = [x for x in r.r if x.dma or x.eng != eng]
            r.r.append(op)
        for r in writes:
            r.w = op
            r.r = []
        self.q[eng].append(op)
        return op

    def emit(self, nc):
        for e in self.ENGS:
            c = 0
            for op in self.q[e]:
                if not op.dma and op.sig:
                    c += 1
                    op.tok = (e, c)
        with contextlib.ExitStack() as st:
            sems = {}
            for e in self.ENGS:
                sems[e] = st.enter_context(nc.semaphore("c_" + e))
            for key in self.dma_semcount:
                sems[key] = st.enter_context(nc.semaphore("d_%s%d" % key))
            block = st.enter_context(nc.Block())

            def run(ename, eobj):
                known = {}
                for op in self.q[ename]:
                    for d in op.deps:
                        s, v = d.tok
                        if known.get(s, 0) < v:
                            eobj.wait_ge(sems[s], v)
                            known[s] = v
                    ins = op.fn(eobj)
                    if op.dma:
                        ins.then_inc(sems[op.dsem], 16)
                    elif op.sig:
                        ins.then_inc(sems[ename], 1)
                if ename == "sp":
                    for key, cnt in self.dma_semcount.items():
                        if known.get(key, 0) < 16 * cnt:
                            eobj.wait_ge(sems[key], 16 * cnt)
                    for e in self.ENGS:
                        c = sum(1 for op in self.q[e] if (not op.dma and op.sig))
                        if c and e != ename and known.get(e, 0) < c:
                            eobj.wait_ge(sems[e], c)

            @block.tensor
            def _(e):
                run("pe", e)

            @block.scalar
            def _(e):
                run("act", e)

            @block.vector
            def _(e):
                run("dve", e)

            @block.gpsimd
            def _(e):
                run("pool", e)

            @block.sync
            def _(e):
                run("sp", e)


class T:
    def __init__(self, t, name):
        self.t = t
        self.res = Res(name)

    def __getitem__(self, k):
        return self.t[k]


def ins_b(a, pos, count):
    l = [list(x) for x in a.ap]
    l.insert(pos, [0, count])
    return bass.AP(a.tensor, a.offset, l)


class StopBuild(Exception):
    pass


class Builder:
    def ck(self, name):
        if self.dbg.get('sstop') == name:
            raise StopBuild()

    def __init__(self, dbg=None):
        self.dbg = dbg or {}
        self.nc = bass.Bass("TRN2", target_bir_lowering=False)
        self.P = Prog()
        self.sb_top = 0
        self.sb_base = None
        self.n_t = 0
        self.psum_i = 0
        self.bank_ctr = {}
        self.dram = {}
        try:
            self.build()
        except StopBuild:
            self.P.barrier()
            o = self.dout('dbg_S', [128, 128])
            self.dma('sp', o, self.identf[:], r=[self.identf])

    def din(self, name, shape, dt=F32):
        a = self.nc.dram_tensor(name, list(shape), dt, kind="ExternalInput").ap()
        self.dram[name] = a
        return a

    def dout(self, name, shape, dt=F32):
        a = self.nc.dram_tensor(name, list(shape), dt, kind="ExternalOutput").ap()
        self.dram[name] = a
        return a

    def dscr(self, name, shape, dt):
        t = T(self.nc.dram_tensor(name, list(shape), dt).ap(), name)
        return t

    def sb(self, name, shape, dt):
        esz = {F32: 4, BF16: 2, I32: 4}[dt]
        n = 1
        for s in shape[1:]:
            n *= s
        nbytes = (n * esz + 31) // 32 * 32
        off = self.sb_top
        self.sb_top += nbytes
        assert self.sb_top <= self.sb_cap, (name, self.sb_top, self.sb_cap)
        self.n_t += 1
        t = self.nc.alloc_sbuf_tensor_at("%s_%d" % (name, self.n_t), list(shape), dt, offset=self.sb_base + off)
        return T(t, name)

    def op(self, eng, fn, r=(), w=()):
        rr = [x.res if isinstance(x, T) else x for x in r]
        ww = [x.res if isinstance(x, T) else x for x in w]
        ex = [x for x in rr if x.name.startswith("bank")]
        if ex:
            rr = [x for x in rr if not x.name.startswith("bank")]
            ww = ww + [x for x in ex if x not in ww]
        return self.P.add(eng, fn, rr, ww)

    def dma(self, eng, out, in_, r=(), w=()):
        return self.P.add(eng, lambda e: e.dma_start(out=out, in_=in_),
                          [x.res if isinstance(x, T) else x for x in r],
                          [x.res if isinstance(x, T) else x for x in w], dma=True)

    def emit_casts(self, deps):
        for (t, i, rpd, src) in self.cast_list:
            o_ = self.dma("pool", t[i:i + rpd, :], src[i:i + rpd, :], r=deps, w=t.blk[i // 128:(i + rpd) // 128])
            o_.exempt = True
        self.cast_list = []

    def bankp(self, name, idxs):
        c = self.bank_ctr.get(name, 0)
        self.bank_ctr[name] = c + 1
        return self.banks[idxs[c % len(idxs)]]

    def bank(self):
        pool = getattr(self, "bank_pool", None) or list(range(8))
        b = self.banks[pool[self.psum_i % len(pool)]]
        self.psum_i += 1
        return b

    def tt(self, eng, out, in0, in1, alu, r, w):
        return self.op(eng, lambda e: e.tensor_tensor(out=out, in0=in0, in1=in1, op=alu), r, w)

    def ts(self, eng, out, in0, s1, op0, r, w, s2=None, op1=None):
        if op1 is None:
            return self.op(eng, lambda e: e.tensor_scalar(out=out, in0=in0, scalar1=s1, scalar2=None, op0=op0), r, w)
        return self.op(eng, lambda e: e.tensor_scalar(out=out, in0=in0, scalar1=s1, scalar2=s2, op0=op0, op1=op1), r, w)

    def stt(self, out, in0, scalar, in1, op0, op1, r, w):
        return self.op("dve", lambda e: e.scalar_tensor_tensor(out=out, in0=in0, scalar=scalar, in1=in1, op0=op0, op1=op1), r, w)

    def act(self, out, in_, func, r, w, **kw):
        return self.op("act", lambda e: e.activation(out=out, in_=in_, func=func, **kw), r, w)

    def cp(self, eng, out, in_, r, w):
        if eng == "act":
            return self.op("act", lambda e: e.copy(out=out, in_=in_), r, w)
        return self.op(eng, lambda e: e.tensor_copy(out=out, in_=in_), r, w)

    def cpow(self, X, TH, re, im, shape, tmp, rX, wR, neg_mag_im=None):
        def v(t):
            n = 1
            for s_ in shape[1:]:
                n *= s_
            a = t.t[:, 0:n]
            if len(shape) == 3:
                a = a.rearrange("p (a b) -> p a b", a=shape[1])
            elif len(shape) == 4:
                a = a.rearrange("p (a b c) -> p a b c", a=shape[1], b=shape[2])
            return a
        f1, f2, f3, m = [v(t) for t in tmp[:4]]
        ii = v(tmp[4])
        tr = list(tmp)
        self.op("dve", lambda e: e.tensor_copy(out=ii, in_=TH), r=rX, w=[tmp[4]])
        self.op("dve", lambda e: e.tensor_copy(out=f1, in_=ii), r=[tmp[4]], w=[tmp[0]])
        self.op("dve", lambda e: e.tensor_tensor(out=f1, in0=TH, in1=f1, op=ALU.subtract), r=list(rX) + [tmp[0]], w=[tmp[0]])
        self.op("dve", lambda e: e.tensor_scalar(out=f2, in0=TH, scalar1=0.25, scalar2=None, op0=ALU.add), r=rX, w=[tmp[1]])
        self.op("dve", lambda e: e.tensor_copy(out=ii, in_=f2), r=[tmp[1]], w=[tmp[4]])
        self.op("dve", lambda e: e.tensor_copy(out=f3, in_=ii), r=[tmp[4]], w=[tmp[2]])
        self.op("dve", lambda e: e.tensor_tensor(out=f2, in0=f2, in1=f3, op=ALU.subtract), r=[tmp[1], tmp[2]], w=[tmp[1]])
        self.op("act", lambda e: e.activation(out=f1, in_=f1, func=AF.Sin, scale=TWO_PI_SAFE), r=[tmp[0]], w=[tmp[0]])
        self.op("act", lambda e: e.activation(out=f2, in_=f2, func=AF.Sin, scale=TWO_PI_SAFE), r=[tmp[1]], w=[tmp[1]])
        self.op("act", lambda e: e.activation(out=m, in_=X, func=AF.Exp), r=rX, w=[tmp[3]])
        self.op("dve", lambda e: e.tensor_tensor(out=re, in0=m, in1=f2, op=ALU.mult), r=[tmp[3], tmp[1]], w=wR)
        self.op("dve", lambda e: e.tensor_tensor(out=im, in0=m, in1=f1, op=ALU.mult), r=[tmp[3], tmp[0]], w=wR)

    def build(self):
        nc = self.nc
        dbg = self.dbg
        xp = self.din("xp", [2, 2048, D])
        a_re = self.din("ssm_a_re", [1, NG, 64])
        a_im = self.din("ssm_a_im", [1, NG, 64])
        log_dt = self.din("ssm_log_dt", [1, NG])
        b_re = self.din("ssm_b_re", [1, NG, 64, 16])
        b_im = self.din("ssm_b_im", [1, NG, 64, 16])
        c_re = self.din("ssm_c_re", [1, NG, 16, 64])
        c_im = self.din("ssm_c_im", [1, NG, 16, 64])
        ssm_d = self.din("ssm_d", [1, D])
        norm_mix = self.din("norm_mix", [2, D])
        c_ident = self.din("c_ident", [128, 128])
        c_tri = self.din("c_tri", [128, 129])
        c_mask2 = self.din("c_mask2", [128, 128])
        c_kcol = self.din("c_kcol", [128, 1])
        c_ramp = self.din("c_ramp", [128, 129])

        sPr = self.dscr("sPr", [128, NG, 64], F32)
        sPi = self.dscr("sPi", [128, NG, 64], F32)
        sQr = self.dscr("sQr", [128, NG, 129], F32)
        sQB = self.dscr("sQB", [128, NG, 129], F32)
        sW1 = self.dscr("sW1", [128, NG, 256], BF16)
        sW2 = self.dscr("sW2", [128, NG, 128], BF16)
        sW3 = self.dscr("sW3", [128, NG, 128], BF16)

        self.sb_cap = (nc.sbuf_bytes_remaining - 64) // 32 * 32
        arena = nc.alloc_sbuf_tensor("arena", [128, self.sb_cap], mybir.dt.uint8)
        self.sb_base = nc.lookup_mloc(arena).addr
        assert self.sb_base % 32 == 0, self.sb_base
        self.banks = [T(nc.alloc_psum_tensor("bank%d" % i, [128, 512], F32), "bank%d" % i) for i in range(8)]

        identf = self.sb("identf", [128, 128], F32)
        identb = self.sb("identb", [128, 128], BF16)
        trib = self.sb("trib", [128, 129], BF16)
        mask2 = self.sb("mask2", [128, 128], F32)
        kcol = self.sb("kcol", [128, 1], F32)
        ramp = self.sb("ramp", [128, 129], F32)
        self.dma("sp", identf[:], c_ident, w=[identf])
        self.dma("sp", mask2[:], c_mask2, w=[mask2])
        self.dma("sp", kcol[:], c_kcol, w=[kcol])
        self.dma("sp", ramp[:], c_ramp, w=[ramp])
        trif = self.sb("trif", [128, 129], F32)
        self.dma("sp", trif[:], c_tri, w=[trif])
        self.op("dve", lambda e: e.tensor_copy(out=identb[:], in_=identf[:]), r=[identf], w=[identb])
        self.op("dve", lambda e: e.tensor_copy(out=trib[:], in_=trif[:]), r=[trif], w=[trib])
        self.identf, self.identb, self.trib = identf, identb, trib
        w_glu = self.din("ssm_w_glu", [1, D, 2 * D])
        w_up = self.din("w_up", [2, D, 4 * D])
        w_down = self.din("w_down", [2, 4 * D, D])
        w_kv = self.din("w_kv", [D, 512])
        w_q = self.din("w_q", [1, D, D])
        w_o = self.din("w_o", [1, D, D])
        norm_ffn = self.din("norm_ffn", [2, D])
        kv_norm = self.din("kv_norm", [D])
        self.wb = {}
        self.cast_list = []
        def castw(name, src, rows, cols):
            t = self.dscr(name, [rows, cols], BF16)
            t.blk = [Res("%s_%d" % (name, i)) for i in range(rows // 128)]
            rpd = max(128, (2 * 1024 * 1024) // (cols * 4))
            for i in range(0, rows, rpd):
                self.cast_list.append((t, i, rpd, src))
            self.wb[name] = t
        if self.dbg.get('nocast'):
            castw = lambda *a: None
        castw("wglu", w_glu[0], D, 2 * D)
        castw("wup0", w_up[0], D, 4 * D)
        castw("wdn0", w_down[0], 4 * D, D)
        castw("wkv", w_kv, D, 512)
        castw("wq", w_q[0], D, D)
        castw("wo", w_o[0], D, D)
        castw("wup1", w_up[1], D, 4 * D)
        castw("wdn1", w_down[1], 4 * D, D)
        self.io = dict(
            k_norm=self.din("k_norm", [64]), q_norm=self.din("q_norm", [1, 64]), sinks=self.din("attn_sinks", [1, 16]),
            c_abias=self.din("c_abias", [128, 2, 2, 16, 128], BF16),
            xs=self.din("xs", [16, 4, D]), st_re=self.din("st_re", [16, NG, 64]), st_im=self.din("st_im", [16, NG, 64]),
            ck=self.din("ck", [16, 128, 4, 64]), cv=self.din("cv", [16, 128, 4, 64]),
            c_sbc=self.din("c_sbc", [128, 16, 4]), c_sbn=self.din("c_sbn", [64, 4, 16, 4, 4]),
            ys=self.dout("ys", [16, 4, D]), sre_s=self.dout("sre_s", [16, NG, 64]), sim_s=self.dout("sim_s", [16, NG, 64]),
            kb_s=self.dout("kb_s", [16, 128, 4, 64]), vb_s=self.dout("vb_s", [16, 128, 4, 64]),
            yp=self.dout("yp", [2, 2048, D]), sre_p=self.dout("sre_p", [2, NG, 64]), sim_p=self.dout("sim_p", [2, NG, 64]),
            kb_p=self.dout("kb_p", [2, 128, 4, 64]), vb_p=self.dout("vb_p", [2, 128, 4, 64]))
        self.ck('casts')
        self.setup_s5(locals())
        self.ck('s5a')
        self.setup_s5b(locals())
        self.ck('s5b')
        if self.dbg.get('main', True):
            self.main(locals())

    def setup_s5(self, L):
        nc = self.nc
        g = lambda k: L[k]
        identf, mask2, kcol, ramp = g("identf"), g("mask2"), g("kcol"), g("ramp")
        a_re, a_im, log_dt = g("a_re"), g("a_im"), g("log_dt")
        b_re, b_im, c_re, c_im = g("b_re"), g("b_im"), g("c_re"), g("c_im")
        op, dma, sb = self.op, self.dma, self.sb
        a2 = sb("a2", [64, 2, 128], F32)
        for h, src in enumerate((a_re, a_im)):
            for c in range(2):
                dma("sp", a2[:, h, c * 64:(c + 1) * 64], src[0], w=[a2])
        AR = sb("AR", [128, NG], F32)
        AI = sb("AI", [128, NG], F32)
        for h, dst in enumerate((AR, AI)):
            bk = self.bank()
            op("pe", lambda e, h=h, bk=bk: e.transpose(out=bk[:, 0:64], in_=a2[:, h, :], identity=identf[0:64, 0:64]), r=[a2, identf], w=[bk])
            op("act", lambda e, dst=dst, bk=bk: e.copy(out=dst[:], in_=bk[:, 0:64]), r=[bk], w=[dst])
        DT = sb("DT", [128, NG], F32)
        dma("sp", DT[:], log_dt[0].partition_broadcast(128), w=[DT])
        op("act", lambda e: e.activation(out=DT[:], in_=DT[:], func=AF.Exp), r=[DT], w=[DT])
        self.ck('ar')
        XR = sb("XR", [128, NG], F32)
        TH = sb("TH", [128, NG], F32)
        op("dve", lambda e: e.tensor_tensor(out=XR[:], in0=DT[:], in1=AR[:], op=ALU.mult), r=[DT, AR], w=[XR])
        op("dve", lambda e: e.scalar_tensor_tensor(out=TH[:], in0=DT[:], scalar=INV_2PI, in1=AI[:], op0=ALU.mult, op1=ALU.mult), r=[DT, AI], w=[TH])
        self.XR, self.TH = XR, TH
        A1r = sb("A1r", [128, NG], F32)
        A1i = sb("A1i", [128, NG], F32)
        self.AR, self.AI, self.DT = AR, AI, DT
        self.A3r = sb("A3r", [128, NG], F32)
        self.A3B = sb("A3B", [128, NG], F32)
        self.A4r = sb("A4r", [128, NG], F32)
        self.A4B = sb("A4B", [128, NG], F32)
        self.A1B = sb("A1B", [128, NG], F32)
        self.mark = self.sb_top
        big = 512
        tmp = [sb("cp%d" % i, [128, big], F32) for i in range(4)] + [sb("cpi", [128, big], I32)]
        self.cpow(XR[:], TH[:], A1r[:], A1i[:], [128, NG], tmp, [XR, TH], [A1r, A1i])
        self.A1r, self.A1i = A1r, A1i
        self.ck('a1')
        fr = sb("fr", [128, NG], F32)
        fi = sb("fi", [128, NG], F32)
        t0 = sb("t0", [128, NG], F32)
        t1 = sb("t1", [128, NG], F32)
        t2 = sb("t2", [128, NG], F32)
        tt, ts, stt, act, cp = self.tt, self.ts, self.stt, self.act, self.cp
        ts("dve", t0[:], A1r[:], -1.0, ALU.add, [A1r], [t0])
        tt("dve", t1[:], AR[:], AR[:], ALU.mult, [AR], [t1])
        tt("dve", t2[:], AI[:], AI[:], ALU.mult, [AI], [t2])
        tt("dve", t1[:], t1[:], t2[:], ALU.add, [t1, t2], [t1])
        self.op("dve", lambda e: e.reciprocal(out=t1[:], in_=t1[:]), [t1], [t1])
        tt("dve", fr[:], t0[:], AR[:], ALU.mult, [t0, AR], [fr])
        tt("dve", t2[:], A1i[:], AI[:], ALU.mult, [A1i, AI], [t2])
        tt("dve", fr[:], fr[:], t2[:], ALU.add, [fr, t2], [fr])
        tt("dve", fr[:], fr[:], t1[:], ALU.mult, [fr, t1], [fr])
        tt("dve", fi[:], A1i[:], AR[:], ALU.mult, [A1i, AR], [fi])
        tt("dve", t2[:], t0[:], AI[:], ALU.mult, [t0, AI], [t2])
        tt("dve", fi[:], fi[:], t2[:], ALU.subtract, [fi, t2], [fi])
        tt("dve", fi[:], fi[:], t1[:], ALU.mult, [fi, t1], [fi])

        X8 = sb("X8", [128, NG, 8], F32)
        T8 = sb("T8", [128, NG, 8], F32)
        r8 = ins_b(ramp[:, 0:8], 1, NG)
        tt("dve", X8[:], ins_b(XR[:], 2, 8), r8, ALU.mult, [XR, ramp], [X8])
        tt("dve", T8[:], ins_b(TH[:], 2, 8), r8, ALU.mult, [TH, ramp], [T8])
        P8r = sb("P8r", [128, NG, 8], F32)
        P8i = sb("P8i", [128, NG, 8], F32)
        self.cpow(X8[:], T8[:], P8r[:], P8i[:], [128, NG, 8], tmp, [X8, T8], [P8r, P8i])
        for (tr_, tb_, ti_) in ((self.A3r, self.A3B, 3), (self.A4r, self.A4B, 4), (None, self.A1B, 1)):
            if tr_ is not None:
                cp("dve", tr_[:], P8r[:, :, ti_], [P8r], [tr_])
            ts("dve", tb_[0:64], P8i[0:64, :, ti_], -1.0, ALU.mult, [P8i], [tb_])
            cp("dve", tb_[64:128], P8i[64:128, :, ti_], [P8i, tb_], [tb_])
        N8r = sb("N8r", [128, NG, 8], F32)
        N8i = sb("N8i", [128, NG, 8], F32)
        m2 = sb("m2", [128, NG, 8], F32)
        act(m2[:], X8[:], AF.Exp, [X8], [m2], scale=-2.0)
        tt("dve", N8r[:], P8r[:], m2[:], ALU.mult, [P8r, m2], [N8r])
        stt(N8i[:], P8i[:], -1.0, m2[:], ALU.mult, ALU.mult, [P8i, m2], [N8i])
        rr = sb("rr", [128, NG, 8], F32)
        ri = sb("ri", [128, NG, 8], F32)
        u8 = sb("u8", [128, NG, 8], F32)
        frb = ins_b(fr[:], 2, 8)
        fib = ins_b(fi[:], 2, 8)
        tt("dve", rr[:], N8r[:], frb, ALU.mult, [N8r, fr], [rr])
        tt("dve", u8[:], N8i[:], fib, ALU.mult, [N8i, fi], [u8])
        tt("dve", rr[:], rr[:], u8[:], ALU.subtract, [rr, u8], [rr])
        tt("dve", ri[:], N8r[:], fib, ALU.mult, [N8r, fi], [ri])
        tt("dve", u8[:], N8i[:], frb, ALU.mult, [N8i, fr], [u8])
        tt("dve", ri[:], ri[:], u8[:], ALU.add, [ri, u8], [ri])
        c2L = sb("c2L", [128, NG, 8], F32)
        ts("dve", c2L[0:64], ri[0:64], -1.0, ALU.mult, [ri], [c2L])
        cp("dve", c2L[64:128], ri[64:128], [ri, c2L], [c2L])
        c1R = sb("c1R", [128, NG, 8], F32)
        c2R = sb("c2R", [128, NG, 8], F32)
        cp("dve", c1R[0:64], P8r[0:64], [P8r], [c1R])
        ts("dve", c1R[64:128], P8r[64:128], -1.0, ALU.mult, [P8r, c1R], [c1R])
        ts("dve", c2R[:], P8i[:], -1.0, ALU.mult, [P8i], [c2R])

        self.ck('rho')
        Ba = sb("Ba", [128, NG, 16], F32)
        Bb = sb("Bb", [128, NG, 16], F32)
        Bc1 = sb("Bc1", [64, 2, 1024], F32)
        Bc2 = sb("Bc2", [64, 2, 1024], F32)
        bre_v = b_re[0].rearrange("g n c -> g (n c)")
        bim_v = b_im[0].rearrange("g n c -> g (n c)")
        dma("sp", Bc1[:, 0, :], bre_v, w=[Bc1])
        dma("sp", Bc1[:, 1, :], bim_v, w=[Bc1])
        dma("sp", Bc2[:, 0, :], bim_v, w=[Bc2])
        dma("sp", Bc2[:, 1, :], bre_v, w=[Bc2])
        nb_ = 0
        for src, dst in ((Bc1, Ba), (Bc2, Bb)):
            flat = src[:].rearrange("p h f -> p (h f)")
            for c4 in range(4):
                bk = self.bank()
                def btf(e, bk=bk, flat=flat, c4=c4):
                    ins = None
                    for cc in range(4):
                        ci = c4 * 4 + cc
                        in_ = bass.AP(flat.tensor, flat.offset + ci, [list(flat.ap[0]), [16, 128]])
                        ins = e.transpose(out=bk[:, cc * 64:(cc + 1) * 64], in_=in_, identity=identf[0:64, 0:64])
                    return ins
                op("pe", btf, r=[src, identf], w=[bk])
                cp("act" if nb_ % 2 == 0 else "dve", dst[:, :, c4 * 4:(c4 + 1) * 4].rearrange("p g c -> p c g"),
                   bk[:, 0:256].rearrange("p (c g) -> p c g", c=4), [bk], [dst])
                nb_ += 1
        Cin = sb("Cin", [128, 8, 128], F32)
        Cin2 = sb("Cin2", [128, 8, 128], F32)
        cre_v = c_re[0].rearrange("(gh gl) c n -> (gl c) gh n", gl=8)
        cim_v = c_im[0].rearrange("(gh gl) c n -> (gl c) gh n", gl=8)
        dma("sp", Cin[:, :, 0:64], cre_v, w=[Cin])
        dma("sp", Cin[:, :, 64:128], cim_v, w=[Cin])
        dma("sp", Cin2[:, :, 0:64], cim_v, w=[Cin2])
        dma("sp", Cin2[:, :, 64:128], cre_v, w=[Cin2])
        self.emit_casts([a2, DT, Bc1, Bc2, Cin, Cin2])
        Ca = sb("Ca", [128, NG, 16], F32)
        Cb = sb("Cb", [128, NG, 16], F32)
        for src, dst in ((Cin, Ca), (Cin2, Cb)):
            for gh in range(8):
                bk = self.bank()
                op("pe", lambda e, bk=bk, src=src, gh=gh: e.transpose(out=bk[:, 0:128], in_=src[:, gh, :], identity=identf[:]), r=[src, identf], w=[bk])
                cp("act", dst[:, gh * 8:(gh + 1) * 8, :], bk[:, 0:128], [bk], [dst])

        self.ck('bc')
        Lt = sb("Lt", [128, NG, 8, 16], F32)
        R0 = sb("R0", [128, NG, 8, 16], F32)
        self.tmpL_off = self.sb_top
        tmpL = sb("tmpL", [128, 32, 8, 16], F32)
        for h in range(2):
            gs = slice(h * 32, (h + 1) * 32)
            def b4(t):
                return ins_b(t[:, gs, :], 2, 8)
            def c4(t):
                return ins_b(t[:, gs, :], 3, 16)
            tt("dve", Lt[:, gs], b4(Ba), c4(rr), ALU.mult, [Ba, rr], [Lt])
            tt("dve", tmpL[:], b4(Bb), c4(c2L), ALU.mult, [Bb, c2L], [tmpL])
            tt("dve", Lt[:, gs], Lt[:, gs], tmpL[:], ALU.add, [Lt, tmpL], [Lt])
            tt("dve", R0[:, gs], b4(Ca), c4(c1R), ALU.mult, [Ca, c1R], [R0])
            tt("dve", tmpL[:], b4(Cb), c4(c2R), ALU.mult, [Cb, c2R], [tmpL])
            tt("dve", R0[:, gs], R0[:, gs], tmpL[:], ALU.add, [R0, tmpL], [R0])

        self.ck('lr')
        sW1, sW2, sW3 = g("sW1"), g("sW2"), g("sW3")
        self.P.barrier()
        self.sb_top = self.tmpL_off
        NQ = 16
        W1s = sb("W1s", [128, NQ, 256], BF16)
        W2s = sb("W2s", [128, NQ, 128], BF16)
        W3s = sb("W3s", [128, NQ, 128], BF16)
        Lh = sb("Lh", [128, NQ, 128], BF16)
        Ll = sb("Ll", [128, NQ, 128], BF16)
        Rh = sb("Rh", [128, NQ, 128], BF16)
        Rl = sb("Rl", [128, NQ, 128], BF16)
        tf = sb("tf", [128, NQ, 128], F32)
        dbgW = "tables" in self.dbg
        if dbgW:
            o3 = self.dout("dbg_W2", [128, NG, 128], BF16)
            o4 = self.dout("dbg_W1", [128, NG, 256], BF16)
        for h in range(NG // NQ):
            gs = slice(h * NQ, (h + 1) * NQ)
            Lv = Lt[:, gs].rearrange("p g t c -> p g (t c)")
            Rv = R0[:, gs].rearrange("p g t c -> p g (t c)")
            cp("dve", W3s[:], Rv, [R0], [W3s])
            for src, hi, lo in ((Lv, Lh, Ll), (Rv, Rh, Rl)):
                cp("dve", hi[:], src, [Lt, R0], [hi])
                tt("dve", lo[:], src, hi[:], ALU.subtract, [Lt, R0, hi], [lo])
            for gl in range(NQ):
                if self.dbg.get("nowpe"):
                    break
                gi = h * NQ + gl
                bk = self.bank()
                Lg = Lt[:, gi].rearrange("p s c -> p (s c)")
                identb = self.identb
                op("pe", lambda e, bk=bk, gl=gl: e.matmul(bk[:, 0:128], lhsT=Lh[:, gl, :], rhs=identb[:], start=True, stop=True), r=[Lh, identb], w=[bk])
                def w2f(e, bk=bk, gl=gl):
                    e.matmul(bk[:, 128:256], lhsT=Lh[:, gl, :], rhs=Rh[:, gl, :], start=True, stop=False)
                    e.matmul(bk[:, 128:256], lhsT=Lh[:, gl, :], rhs=Rl[:, gl, :], start=False, stop=False)
                    return e.matmul(bk[:, 128:256], lhsT=Ll[:, gl, :], rhs=Rh[:, gl, :], start=False, stop=True)
                op("pe", w2f, r=[Lh, Ll, Rh, Rl], w=[bk])
                ev = self.dbg.get("evac", "ad")
                if "a" in ev:
                    cp("act", W1s[:, gl, 0:128], bk[:, 0:128], [bk], [W1s])
                    cp("act", W1s[:, gl, 128:256], bk[:, 0:128], [bk, W1s], [W1s])
                if "d" in ev:
                    tt("dve", W2s[:, gl], bk[:, 128:256], mask2[:], ALU.mult, [bk, mask2], [W2s])
            if not self.dbg.get("nowdma"):
                dma("sp", sW1[:, gs], W1s[:], r=[W1s], w=[sW1])
                dma("sp", sW2[:, gs], W2s[:], r=[W2s], w=[sW2])
                dma("sp", sW3[:, gs], W3s[:], r=[W3s], w=[sW3])
            if dbgW:
                dma("sp", o3[:, gs], W2s[:], r=[W2s])
                dma("sp", o4[:, gs], W1s[:], r=[W1s])
        if "tables" in self.dbg:
            o1 = self.dout("dbg_L", [128, NG, 128])
            o2 = self.dout("dbg_R0", [128, NG, 128])
            dma("sp", o1, Lt[:].rearrange("p g s c -> p g (s c)"), r=[Lt])
            dma("sp", o2, R0[:].rearrange("p g s c -> p g (s c)"), r=[R0])


    def setup_s5b(self, L):
        g = lambda k: L[k]
        kcol, ramp = g("kcol"), g("ramp")
        a_re, a_im = g("a_re"), g("a_im")
        sPr, sPi, sQr, sQB = g("sPr"), g("sPi"), g("sQr"), g("sQB")
        op, dma, sb = self.op, self.dma, self.sb
        tt, ts, stt, act, cp = self.tt, self.ts, self.stt, self.act, self.cp
        XR, TH, DT = self.XR, self.TH, self.DT
        self.P.barrier()
        self.sb_top = self.mark
        big = 32 * 129
        tmp = [sb("cq%d" % i, [128, big], F32) for i in range(4)] + [sb("cqi", [128, big], I32)]
        mark2 = self.sb_top
        ARb = sb("ARb", [128, NG, 64], F32)
        AIb = sb("AIb", [128, NG, 64], F32)
        dma("sp", ARb[:].rearrange("p g n -> p (g n)"), a_re[0].rearrange("g n -> (g n)").partition_broadcast(128), w=[ARb])
        dma("sp", AIb[:].rearrange("p g n -> p (g n)"), a_im[0].rearrange("g n -> (g n)").partition_broadcast(128), w=[AIb])
        XP = sb("XP", [128, 32, 64], F32)
        TP = sb("TP", [128, 32, 64], F32)
        Prh = sb("Prh", [128, 32, 64], F32)
        Pih = sb("Pih", [128, 32, 64], F32)
        for h in range(2):
            gs = slice(h * 32, (h + 1) * 32)
            dtb = ins_b(DT[:, gs], 2, 64)
            tt("dve", XP[:], ARb[:, gs], dtb, ALU.mult, [ARb, DT], [XP])
            ts("dve", XP[:], XP[:], kcol[:], ALU.mult, [XP, kcol], [XP], s2=-8.0, op1=ALU.mult)
            tt("dve", TP[:], AIb[:, gs], dtb, ALU.mult, [AIb, DT], [TP])
            ts("dve", TP[:], TP[:], kcol[:], ALU.mult, [TP, kcol], [TP], s2=-8.0 * INV_2PI, op1=ALU.mult)
            self.cpow(XP[:], TP[:], Prh[:], Pih[:], [128, 32, 64], tmp, [XP, TP], [Prh, Pih])
            dma("sp", sPr[:, gs], Prh[:], r=[Prh], w=[sPr])
            dma("sp", sPi[:, gs], Pih[:], r=[Pih], w=[sPi])
        self.P.barrier()
        self.sb_top = mark2
        XQ = sb("XQ", [128, 32, 129], F32)
        TQ = sb("TQ", [128, 32, 129], F32)
        Qrh = sb("Qrh", [128, 32, 129], F32)
        Qih = sb("Qih", [128, 32, 129], F32)
        QBh = sb("QBh", [128, 32, 129], F32)
        for h in range(2):
            gs = slice(h * 32, (h + 1) * 32)
            rb = ins_b(ramp[:], 1, 32)
            stt(XQ[:], ins_b(XR[:, gs], 2, 129), 8.0, rb, ALU.mult, ALU.mult, [XR, ramp], [XQ])
            stt(TQ[:], ins_b(TH[:, gs], 2, 129), 8.0, rb, ALU.mult, ALU.mult, [TH, ramp], [TQ])
            self.cpow(XQ[:], TQ[:], Qrh[:], Qih[:], [128, 32, 129], tmp, [XQ, TQ], [Qrh, Qih])
            ts("dve", QBh[0:64], Qih[0:64], -1.0, ALU.mult, [Qih], [QBh])
            cp("dve", QBh[64:128], Qih[64:128], [Qih, QBh], [QBh])
            dma("sp", sQr[:, gs], Qrh[:], r=[Qrh], w=[sQr])
            dma("sp", sQB[:, gs], QBh[:], r=[QBh], w=[sQB])
        self.P.barrier()
        self.sb_top = self.mark

    def main(self, L):
        nc = self.nc
        g = lambda k: L[k]
        op, dma, sb = self.op, self.dma, self.sb
        tt, ts, stt, act, cp = self.tt, self.ts, self.stt, self.act, self.cp
        xp = g("xp")
        identb, trib = self.identb, self.trib
        Rr = [T(None, "R%d" % r) for r in range(8)]
        Rt = sb("Rt", [128, 8, D], F32)
        HN = sb("HN", [128, 8, D], BF16)
        HNr = [Res("HN%d" % r) for r in range(8)]
        XT = sb("XT", [128, 8, D], BF16)
        XTd = [Res("XT%d" % i) for i in range(8)]
        G = sb("G", [128, 5, D], F32)
        Dv = sb("Dv", [128, D], F32)
        dma("sp", G[:, 0:2, :], g("norm_mix").partition_broadcast(128), w=[G])
        dma("sp", Dv[:], g("ssm_d")[0].partition_broadcast(128), w=[Dv])
        zc = [sb("zc%d" % i, [128, NG], F32) for i in range(2)]
        zcs = [sb("zcs%d" % i, [128, NG], F32) for i in range(2)]
        ssq = sb("ssq", [128, 8], F32)
        rstd = sb("rstd", [128, 8], F32)
        epsT = sb("epsT", [128, 1], F32)
        op("pool", lambda e: e.memset(epsT[:], EPS), w=[epsT])
        KT = sb("KT", [64, 4, 9, 128], BF16)
        KTres = [Res("KT%d" % i) for i in range(9)]
        VP = sb("VP", [128, 9, 4, 65], BF16)
        VPres = [Res("VP%d" % i) for i in range(9)]
        op("pool", lambda e: e.memset(VP[:], 1.0), w=VPres)
        esink = sb("esink", [128, 16], F32)
        gq = sb("gq", [128, 64], F32)
        gk = sb("gk", [128, 64], F32)
        io = self.io
        dma("sp", esink[:], io["sinks"][0].partition_broadcast(128), w=[esink])
        act(esink[:], esink[:], AF.Exp, [esink], [esink])
        dma("sp", gq[:], io["q_norm"][0].partition_broadcast(128), w=[gq])
        ts("dve", gq[:], gq[:], 0.125, ALU.mult, [gq], [gq])
        dma("sp", gk[:], io["k_norm"].partition_broadcast(128), w=[gk])
        gq8 = sb("gq8", [64, 1], F32)
        gqk = sb("gqk", [64, 1], F32)
        dma("sp", gq8[:], io["q_norm"][0].rearrange("(d o) -> d o", o=1), w=[gq8])
        dma("sp", gqk[:], io["k_norm"].rearrange("(d o) -> d o", o=1), w=[gqk])
        ts("dve", gq8[:], gq8[:], 0.125, ALU.mult, [gq8], [gq8])
        tt("dve", gqk[:], gqk[:], gq8[:], ALU.mult, [gqk, gq8], [gqk])
        gcol = sb("gcol", [128, 2, 8], F32)
        g8v = XT[0:8, 0, 0:512].bitcast(F32)
        dma("sp", g8v[:, 0:128], L["kv_norm"].rearrange("(kt p) -> kt p", p=128), w=[XTd[0]])
        dma("sp", g8v[:, 128:256], L["norm_mix"][1].rearrange("(kt p) -> kt p", p=128), w=[XTd[0]])
        bkg = self.bank()
        def g8f(e, bkg=bkg):
            e.transpose(out=bkg[:, 0:8], in_=g8v[:, 0:128], identity=self.identf[0:8, 0:8])
            return e.transpose(out=bkg[:, 8:16], in_=g8v[:, 128:256], identity=self.identf[0:8, 0:8])
        op("pe", g8f, r=[XTd[0], self.identf], w=[bkg])
        cp("act", gcol[:].rearrange("p a k -> p (a k)"), bkg[:, 0:16], [bkg], [gcol])
        Hfin = sb("Hfin", [128, NG], F32)
        AinvR = sb("AinvR", [128, NG], F32)
        AinvB = sb("AinvB", [128, NG], F32)
        Ht1 = sb("Ht1", [128, NG], F32)
        Ht2 = sb("Ht2", [128, NG], F32)
        A1r, A1i = self.A1r, self.A1i
        tt("dve", Ht1[:], A1r[:], A1r[:], ALU.mult, [A1r], [Ht1])
        tt("dve", Ht2[:], A1i[:], A1i[:], ALU.mult, [A1i], [Ht2])
        tt("dve", Ht1[:], Ht1[:], Ht2[:], ALU.add, [Ht1, Ht2], [Ht1])
        op("dve", lambda e: e.reciprocal(out=Ht1[:], in_=Ht1[:]), [Ht1], [Ht1])
        tt("dve", AinvR[:], A1r[:], Ht1[:], ALU.mult, [A1r, Ht1], [AinvR])
        tt("dve", AinvB[0:64], A1i[0:64], Ht1[0:64], ALU.mult, [A1i, Ht1], [AinvB])
        stt(AinvB[64:128], A1i[64:128], -1.0, Ht1[64:128], ALU.mult, ALU.mult, [A1i, Ht1, AinvB], [AinvB])
        self.main_mark = self.sb_top
        Rres = [Res("Rres%d" % r) for r in range(8)]

        HNg = HN[:].rearrange("p r d -> p (r d)").rearrange("p (g s c) -> p g s c", g=NG, s=8)

        cx = {"M": 128, "NT": 8}

        def rmsnorm(gi, gmajor=False):
            M, NT = cx["M"], cx["NT"]
            for r in range(NT):
                act(XT[0:M, 0, :], Rt[0:M, r, :], AF.Square, [Rres[r]], [XTd[0], ssq], accum_out=ssq[0:M, r:r + 1])
            act(rstd[0:M, 0:NT], ssq[0:M, 0:NT], AF.Ln, [ssq, epsT], [rstd], scale=1.0 / D, bias=epsT[0:M])
            act(rstd[0:M, 0:NT], rstd[0:M, 0:NT], AF.Exp, [rstd], [rstd], scale=-0.5)
            for r in range(NT):
                if gi is None:
                    ts("dve", HN[0:M, r, :], Rt[0:M, r, :], rstd[0:M, r:r + 1], ALU.mult, [Rres[r], rstd], [HNr[r]])
                elif gmajor:
                    stt(HNg[:, :, r, :], Rt[:, r, :].rearrange("p (g c) -> p g c", c=16), rstd[:, r:r + 1],
                        G[:, gi, :].rearrange("p (g c) -> p g c", c=16), ALU.mult, ALU.mult, [Rres[r], rstd, G], [HNr[r]])
                else:
                    stt(HN[0:M, r, :], Rt[0:M, r, :], rstd[0:M, r:r + 1], G[0:M, gi, :], ALU.mult, ALU.mult, [Rres[r], rstd, G], [HNr[r]])

        sW1, sW2, sW3, sPr, sPi, sQr, sQB = [g(k) for k in ("sW1", "sW2", "sW3", "sPr", "sPi", "sQr", "sQB")]
        tb = []
        for i in range(2):
            tb.append(dict(
                W1=sb("W1b%d" % i, [128, 8, 256], BF16), W2=sb("W2b%d" % i, [128, 8, 128], BF16),
                W3=sb("W3b%d" % i, [128, 8, 128], BF16), Pr=sb("Prb%d" % i, [128, 8, 64], F32),
                Pi=sb("Pib%d" % i, [128, 8, 64], F32), Qr=sb("Qrb%d" % i, [128, 8, 129], F32),
                QB=sb("QBb%d" % i, [128, 8, 129], F32)))
        UTb = [sb("UTb%d" % i, [128, 8, 128], BF16) for i in range(2)]
        Zb = [sb("Zb%d" % i, [128, 8, 192], BF16) for i in range(2)]
        tA = [sb("tA%d" % i, [128, 2, 192], F32) for i in range(2)]
        tB = [sb("tB%d" % i, [128, 2, 192], F32) for i in range(2)]
        t1b = sb("t1b", [128, 8, 129], F32)
        t2b = sb("t2b", [128, 8, 129], F32)
        HTb = [sb("HTb%d" % i, [128, 8, 128], BF16) for i in range(2)]
        du = [sb("du0", [128, 8, 128], F32)] * 2
        ytmp = [sb("ytmp0", [128, 8, 128], F32)] * 2
        zsw = sb("zsw", [64, 128], F32)
        hres = [[Res("htb%d_%d" % (i, j)) for j in range(2)] for i in range(2)]
        utres = [[Res("ut%d_%d" % (i, j)) for j in range(2)] for i in range(2)]
        t1res = [Res("t1b%d" % i) for i in range(8)]
        t2res = [Res("t2b%d" % i) for i in range(8)]
        s5_top = self.sb_top
        self.sb_top = self.sb_cap - 16 * 1024 - 64
        assert self.sb_top >= s5_top, (self.sb_top, s5_top)
        GL = sb("GL", [128, 8, D], BF16)
        arena_lim = self.sb_cap - 16 * 1024 - 64
        self.sb_top = self.main_mark
        Wg4 = [sb("Wg4_%d" % i, [128, 8, 512], BF16) for i in range(4)]
        sg = [sb("sg%d" % i, [128, 512], F32) for i in range(2)]
        pr = [sb("pr%d" % i, [128, 512], F32) for i in range(2)]
        assert self.sb_top <= arena_lim
        self.sb_top = self.main_mark
        Hh = sb("Hh", [128, 32, 512], BF16)
        Hres = [Res("Hh%d" % i) for i in range(32)]
        NWB = 3
        Wu = [sb("Wu%d" % i, [128, 8, 512], BF16) for i in range(NWB)]
        Wd = [sb("Wd%d" % i, [128, 4, D], BF16) for i in range(NWB)]
        assert self.sb_top <= arena_lim
        self.sb_top = self.main_mark
        Wkv = sb("Wkv", [128, 8, 512], BF16)
        Wq = [sb("Wq%d" % i, [128, 8, 512], BF16) for i in range(2)]
        Wo = Wq
        sqf2 = [sb("sqf%d" % i, [128, D], F32) for i in range(2)]
        qss2 = [sb("qss%d" % i, [128, 16], F32) for i in range(2)]
        Kf2 = [sb("Kf%d" % i, [128, 256], F32) for i in range(2)]
        Vf2 = [sb("Vf0", [128, 256], F32)] * 2
        Kb2 = [sb("Kb%d" % i, [128, 256], BF16) for i in range(2)]
        Qb2 = [sb("Qb%d" % i, [128, D], BF16) for i in range(2)]
        sqf, qss, Kf, Vf, Kb, Qb = sqf2[0], qss2[0], Kf2[0], Vf2[0], Kb2[0], Qb2[0]
        scf4 = [sb("scf%d" % i, [128, 512], F32) for i in range(4)]
        scf = scf4
        att_shared = self.sb_top
        OTt = sb("OTt", [128, 8, D], BF16)
        OTd = [Res("OT%d" % i) for i in range(8)]
        abias = sb("abias", [128, 2, 2, 16, 128], BF16)
        QT = [sb("QT%d" % i, [64, 16, 128], BF16) for i in range(2)]
        exb = [sb("exb%d" % i, [128, 512], BF16) for i in range(4)]
        den2 = [sb("den%d" % i, [128, 4], F32) for i in range(2)]
        Otok = [sb("Otok%d" % i, [128, D], BF16) for i in range(2)]
        assert self.sb_top <= self.sb_cap, self.sb_top
        self.sb_top = att_shared
        att_top = self.sb_top
        WoS = sb("WoS", [64, 16, 512], BF16)
        sbc = sb("sbc", [128, 64], F32)
        sbn = sb("sbn", [64, 4, 256], F32)
        KTn = sb("KTn", [64, 4, 64], BF16)
        VPn = sb("VPn", [64, 4, 64], BF16)
        QTs = sb("QTs", [64, 16, 64], BF16)
        QTs2 = sb("QTs2", [64, 4, 256], BF16)
        exn = sb("exn", [64, 4, 256], BF16)
        Kc = [sb("Kc%d" % i, [128, 256], F32) for i in range(2)]
        Vc = [sb("Vc%d" % i, [128, 256], F32) for i in range(2)]
        Kcb2 = [sb("Kcb%d" % i, [128, 256], BF16) for i in range(2)]
        VPc = [sb("VPc%d" % i, [128, 4, 64], BF16) for i in range(2)]
        KTc = [sb("KTc%d" % i, [64, 4, 128], BF16) for i in range(2)]
        exc = [sb("exc%d" % i, [128, 64], BF16) for i in range(2)]
        OTs = sb("OTs", [64, 16, 64], BF16)
        dn = sb("dn", [64, 512], F32)
        onesb = sb("onesb", [128, 64], BF16)
        assert self.sb_top <= self.sb_cap, self.sb_top
        self.sb_top = self.main_mark
        stb = [dict(W1=sb("sW1b%d" % i, [128, 8, 256], BF16), W2=sb("sW2b%d" % i, [128, 8, 128], BF16),
                    W3=sb("sW3b%d" % i, [128, 8, 128], BF16)) for i in range(2)]
        HNs = sb("HNs", [16, NG, 4, 16], BF16)
        GLs = sb("GLs", [16, 4, D], BF16)
        H0k = [sb("H0k%d" % i, [16, 8, 128], F32) for i in range(2)]
        H0k2 = [sb("H0k2%d" % i, [16, 8, 128], F32) for i in range(2)]
        H0T2 = [sb("H0T%d" % i, [128, 8, 16], F32) for i in range(2)]
        H0Ts2 = [sb("H0Ts%d" % i, [128, 8, 16], F32) for i in range(2)]
        za1 = sb("za1", [128, 8, 16], F32)
        za2 = sb("za2", [128, 8, 16], F32)
        zt1 = sb("zt1", [128, 8, 16], F32)
        zt2 = sb("zt2", [128, 8, 16], F32)
        Fbs2 = [sb("Fbs%d" % i, [128, 8, 16], BF16) for i in range(2)]
        UTs2 = [sb("UTs%d" % i, [64, 8, 16], BF16) for i in range(2)]
        Hs_ = sb("Hs_", [128, 8, 16], F32)
        Hout = sb("Hout", [16, 8, 128], F32)
        yt16 = sb("yt16", [16, 4, 128], F32)
        du16 = sb("du16", [16, 4, 128], F32)
        assert self.sb_top <= arena_lim, self.sb_top
        wb = self.wb
        dma("sp", G[:, 2:4, :], g("norm_ffn").partition_broadcast(128), w=[G])
        dma("sp", G[:, 4, :], g("kv_norm").partition_broadcast(128), w=[G])

        def to_xt(src, src_res, perm=False):
            M, NT = cx["M"], cx["NT"]
            for r in range(NT):
                bk = self.bank()
                bkb = bk[:].bitcast(BF16)
                def tf(e, bkb=bkb, r=r):
                    ins = None
                    for dt_ in range(8):
                        ins = e.transpose(out=bkb[:, dt_ * 128:dt_ * 128 + M], in_=src[0:M, r, dt_ * 128:(dt_ + 1) * 128], identity=identb[0:M, 0:M])
                    return ins
                rr_ = [src_res[r]] if len(src_res) == NT and NT > 1 else list(src_res)
                op("pe", tf, r=rr_ + [identb], w=[bk])
                eng = "act" if r % 2 == 0 else "dve"
                if perm:
                    o_ = XT[:].rearrange("p t (q r l) -> p t q r l", q=8, r=8)[:, :, :, r, :]
                    i_ = bkb[:, 0:1024].rearrange("p (t q l) -> p t q l", t=8, q=8)
                    cp(eng, o_, i_, [bk], XTd)
                elif NT == 8:
                    cp(eng, XT[:, :, r * 128:(r + 1) * 128], bkb[:, 0:1024].rearrange("p (t k) -> p t k", t=8), [bk], XTd)
                else:
                    cp(eng, XT[:, :, 0:M], bkb[:, 0:1024].rearrange("p (t k) -> p t k", t=8)[:, :, 0:M], [bk], XTd)

        def glu_phase():
            M, NT = cx["M"], cx["NT"]
            self.P.barrier()
            to_xt(GL, [GLres])
            wg = wb["wglu"]
            wv = wg[:].rearrange("(kt p) f -> p kt f", p=128)
            for j in range(2):
                Wa, Wgt = Wg4[2 * (j % 2)], Wg4[2 * (j % 2) + 1]
                dma("sp", Wa[:], wv[:, :, j * 512:(j + 1) * 512], r=wg.blk, w=[Wa])
                dma("sp", Wgt[:], wv[:, :, D + j * 512:D + (j + 1) * 512], r=wg.blk, w=[Wgt])
                for r in range(NT):
                    bA, bB = self.bank(), self.bank()
                    def mm(e, bk, W, r=r):
                        ins = None
                        for kt in range(8):
                            ins = e.matmul(bk[0:M, 0:512], lhsT=XT[:, kt, r * 128:r * 128 + M], rhs=W[:, kt, :], start=(kt == 0), stop=(kt == 7))
                        return ins
                    op("pe", lambda e, bA=bA, Wa=Wa, mm=mm: mm(e, bA, Wa), r=XTd + [Wa], w=[bA])
                    op("pe", lambda e, bB=bB, Wgt=Wgt, mm=mm: mm(e, bB, Wgt), r=XTd + [Wgt], w=[bB])
                    sg_, pr_ = sg[r % 2], pr[r % 2]
                    act(sg_[0:M], bB[0:M, 0:512], AF.Sigmoid, [bB], [sg_])
                    tt("dve", pr_[0:M], bA[0:M, 0:512], sg_[0:M], ALU.mult, [bA, sg_], [pr_])
                    tt("pool", Rt[0:M, r, j * 512:(j + 1) * 512], Rt[0:M, r, j * 512:(j + 1) * 512], pr_[0:M], ALU.add, [Rres[r], pr_], [Rres[r]])

        def mlp_phase(layer):
            M, NT = cx["M"], cx["NT"]
            rmsnorm(2 + layer)
            to_xt(HN, HNr)
            self.P.barrier()
            wu, wd = wb["wup%d" % layer], wb["wdn%d" % layer]
            wuv = wu[:].rearrange("(kt p) f -> p kt f", p=128)
            wdv = wd[:].rearrange("(ft p) d -> p ft d", p=128)
            nld = [0, 0]
            nhalf = 2 if NT == 8 else 1
            tph = NT // nhalf
            ncol = 512 if NT == 8 else M
            for hf in range(nhalf):
                cols = slice(hf * 512, hf * 512 + ncol)
                for fb in range(8):
                    W = Wu[nld[0] % NWB]
                    nld[0] += 1
                    dma("sp", W[:], wuv[:, :, fb * 512:(fb + 1) * 512], r=wu.blk, w=[W])
                    for f4 in range(4):
                        ft = fb * 4 + f4
                        bk = self.bank()
                        def mm(e, bk=bk, W=W, f4=f4, cols=cols):
                            ins = None
                            for kt in range(8):
                                ins = e.matmul(bk[:, 0:ncol], lhsT=W[:, kt, f4 * 128:(f4 + 1) * 128], rhs=XT[:, kt, cols], start=(kt == 0), stop=(kt == 7))
                            return ins
                        op("pe", mm, r=XTd + [W], w=[bk])
                        act(Hh[:, ft, 0:ncol], bk[:, 0:ncol], AF.Relu, [bk], [Hres[ft]])
                        tt("pool", Hh[:, ft, 0:ncol], Hh[:, ft, 0:ncol], Hh[:, ft, 0:ncol], ALU.mult, [Hres[ft]], [Hres[ft]])
                for wbk in range(8):
                    W = Wd[nld[1] % NWB]
                    nld[1] += 1
                    dma("sp", W[:], wdv[:, wbk * 4:(wbk + 1) * 4, :], r=wd.blk[wbk * 4:(wbk + 1) * 4], w=[W])
                    for rl in range(tph):
                        def mm(e, W=W, rl=rl, wbk=wbk):
                            ins = None
                            for f4 in range(4):
                                ft = wbk * 4 + f4
                                for dh in range(2):
                                    ins = e.matmul(self.banks[rl * 2 + dh][0:M, 0:512], lhsT=Hh[:, ft, rl * 128:rl * 128 + M],
                                                   rhs=W[:, f4, dh * 512:(dh + 1) * 512], start=(ft == 0), stop=(ft == 31))
                            return ins
                        op("pe", mm, r=Hres[wbk * 4:(wbk + 1) * 4] + [W], w=[self.banks[rl * 2], self.banks[rl * 2 + 1]])
                for rl in range(tph):
                    r = hf * tph + rl
                    for dh in range(2):
                        bk = self.banks[rl * 2 + dh]
                        tt("dve", Rt[0:M, r, dh * 512:(dh + 1) * 512], bk[0:M, 0:512], Rt[0:M, r, dh * 512:(dh + 1) * 512], ALU.add, [bk, Rres[r]], [Rres[r]])
        GLres = Res("GL")
        dbgY = self.dout("dbg_Y", [2, 128, 8, D]) if "s5" in self.dbg else None
        dbgF = self.dout("dbg_F", [2, 128, NG]) if "s5" in self.dbg else None

        def headnorm(ps_list, nh, outb, par=0, outf=None):
            M = cx["M"]
            sqf, qss = sqf2[par], qss2[par]
            c0 = 0
            for bk, ncol in ps_list:
                act(sqf[0:M, c0:c0 + ncol], bk[0:M, 0:ncol], AF.Square, [bk], [sqf])
                c0 += ncol
            op("dve", lambda e: e.tensor_reduce(out=qss[0:M, 0:nh], in_=sqf[0:M, 0:nh * 64].rearrange("p (h d) -> p h d", d=64),
                                                axis=mybir.AxisListType.X, op=ALU.add), [sqf], [qss])
            act(qss[0:M, 0:nh], qss[0:M, 0:nh], AF.Ln, [qss, epsT], [qss], scale=1.0 / 64, bias=epsT[0:M])
            act(qss[0:M, 0:nh], qss[0:M, 0:nh], AF.Exp, [qss], [qss], scale=-0.5)
            c0 = 0
            for bk, ncol in ps_list:
                h0_, hn_ = c0 // 64, ncol // 64
                v3 = lambda a: a.rearrange("p (h d) -> p h d", d=64)
                tt("dve", v3(outb[0:M, c0:c0 + ncol]), v3(bk[0:M, 0:ncol]), ins_b(qss[0:M, h0_:h0_ + hn_], 2, 64), ALU.mult, [bk, qss], [outb])
                if outf is not None:
                    tt("dve", v3(outf[0:M, c0:c0 + ncol]), v3(bk[0:M, 0:ncol]), ins_b(qss[0:M, h0_:h0_ + hn_], 2, 64), ALU.mult, [bk, qss], [outf])
                    tt("pool", v3(outf[0:M, c0:c0 + ncol]), v3(outf[0:M, c0:c0 + ncol]), ins_b(gk[0:M], 1, hn_), ALU.mult, [outf, gk], [outf])
                c0 += ncol

        def scale_w(W, which):
            for kt in range(8):
                ts("dve", W[:, kt, :], W[:, kt, :], gcol[:, which, kt:kt + 1], ALU.mult, [W, gcol], [W])

        def attn_phase(b_, c_):
            self.P.barrier()
            dma("sp", abias[:], io["c_abias"], w=[abias])
            wkv, wq, wo = wb["wkv"], wb["wq"], wb["wo"]
            dma("sp", Wkv[:], wkv[:].rearrange("(kt p) f -> p kt f", p=128), r=wkv.blk, w=[Wkv])
            for j in range(2):
                dma("sp", Wq[j][:], wq[:].rearrange("(kt p) f -> p kt f", p=128)[:, :, j * 512:(j + 1) * 512], r=wq.blk, w=[Wq[j]])
            scale_w(Wkv, 0)
            scale_w(Wq[0], 1)
            scale_w(Wq[1], 1)
            rmsnorm(None)
            to_xt(HN, HNr, perm=True)
            if c_ == 1:
                cp("pool", KT[:, :, 0, :], KT[:, :, 8, :], [KTres[8]], [KTres[0]])
                cp("pool", VP[:, 0], VP[:, 8], [VPres[8]], [VPres[0]])
            kvb = {}

            def kv_proj(qb):
                bk = self.bankp("kvp", [0, 1, 2])
                def kvf(e, bk=bk, qb=qb):
                    ins = None
                    for kt in range(8):
                        ins = e.matmul(bk[:, 0:512], lhsT=XT[:, kt, qb * 128:(qb + 1) * 128], rhs=Wkv[:, kt, :], start=(kt == 0), stop=(kt == 7))
                    return ins
                op("pe", kvf, r=XTd + [Wkv], w=[bk])
                kvb[qb] = bk

            def kv_epi(qb):
                bk = kvb.pop(qb)
                Kf_, Vf_, Kb_ = Kf2[qb % 2], Vf2[qb % 2], Kb2[qb % 2]
                last = (c_ == 1 and qb == 7)
                cp("act", Vf_[:], bk[:, 256:512], [bk], [Vf_])
                cp("pool", VP[:, qb + 1, :, 0:64], Vf_[:].rearrange("p (h d) -> p h d", d=64), [Vf_], [VPres[qb + 1]])
                headnorm([(bk, 256)], 4, Kb_, qb % 2, outf=(Kf_ if last else None))
                bk2 = self.bankp("kvt", [3, 4])
                bk2b = bk2[:].bitcast(BF16)
                def ktf(e, bk2b=bk2b, Kb_=Kb_):
                    ins = None
                    for hk in range(4):
                        ins = e.transpose(out=bk2b[0:64, hk * 128:(hk + 1) * 128], in_=Kb_[:, hk * 64:(hk + 1) * 64], identity=identb[:])
                    return ins
                op("pe", ktf, r=[Kb_, identb], w=[bk2])
                act(KT[:, :, qb + 1, :], bk2b[0:64, 0:512].rearrange("p (h k) -> p h k", h=4), AF.Copy, [bk2, gqk], [KTres[qb + 1]], scale=gqk[:])
                if last:
                    kv_ = io["kb_p"][b_].rearrange("(kl r) h d -> r kl (h d)", r=8)
                    vv_ = io["vb_p"][b_].rearrange("(kl r) h d -> r kl (h d)", r=8)
                    for r in range(8):
                        dma("sp", kv_[r], Kf_[r * 16:(r + 1) * 16, :], r=[Kf_])
                        dma("sp", vv_[r], Vf_[r * 16:(r + 1) * 16, :], r=[Vf_])

            kv_proj(0)
            for qb in range(8):
                if qb + 1 < 8:
                    kv_proj(qb + 1)
                kv_epi(qb)

            def stage_a(qb):
                qt = QT[qb % 2]
                Qb_ = Qb2[qb % 2]
                bq = [self.bankp("qp", [0, 1]), self.bankp("qp", [0, 1])]
                for j in range(2):
                    def qf(e, bk=bq[j], j=j, qb=qb):
                        ins = None
                        for kt in range(8):
                            ins = e.matmul(bk[:, 0:512], lhsT=XT[:, kt, qb * 128:(qb + 1) * 128], rhs=Wq[j][:, kt, :], start=(kt == 0), stop=(kt == 7))
                        return ins
                    op("pe", qf, r=XTd + [Wq[j]], w=[bq[j]])
                headnorm([(bq[0], 512), (bq[1], 512)], 16, Qb_, qb % 2)

            def stage_a2(qb):
                qt = QT[qb % 2]
                Qb_ = Qb2[qb % 2]
                for half in range(2):
                    bk2 = self.bankp("qt", [2])
                    bk2b = bk2[:].bitcast(BF16)
                    def qtf(e, bk2b=bk2b, half=half, Qb_=Qb_):
                        ins = None
                        for hh in range(8):
                            h = half * 8 + hh
                            ins = e.transpose(out=bk2b[0:64, hh * 128:(hh + 1) * 128], in_=Qb_[:, h * 64:(h + 1) * 64], identity=identb[:])
                        return ins
                    op("pe", qtf, r=[Qb_, identb], w=[bk2])
                    cp("act" if half == 0 else "dve", qt[:, half * 8:(half + 1) * 8, :], bk2b[0:64, 0:1024].rearrange("p (h k) -> p h k", h=8), [bk2], [qt])

            def stage_b(qb):
                qt = QT[qb % 2]
                kbs = [1] if (c_ == 0 and qb == 0) else [0, 1]
                ot = Otok[qb % 2]
                sbanks = {}

                def scores(hk):
                    lst = []
                    for kb in kbs:
                        slot = qb + kb
                        bk = self.bankp("sc", [3, 4, 5, 6])
                        def scf_(e, bk=bk, hk=hk, slot=slot, kb=kb):
                            e.matmul(bk[:, 0:512], lhsT=KT[:, hk, slot, :], rhs=qt[:, 4 * hk:4 * hk + 4, :].rearrange("p h k -> p (h k)"), start=True, stop=False)
                            e.matmul(bk[:, 0:512], lhsT=identb[:], rhs=abias[:, 0, kb, 4 * hk:4 * hk + 4, :].rearrange("p h k -> p (h k)"), start=False, stop=False)
                            return e.matmul(bk[:, 0:512], lhsT=identb[:], rhs=abias[:, 1, kb, 4 * hk:4 * hk + 4, :].rearrange("p h k -> p (h k)"), start=False, stop=True)
                        op("pe", scf_, r=[KTres[slot], qt, abias, identb], w=[bk])
                        lst.append((bk, kb, slot))
                    sbanks[hk] = lst

                exd = {}

                def rest1(hk):
                    exs = []
                    for bk, kb, slot in sbanks.pop(hk):
                        ex = exb[(hk % 2) * 2 + kb]
                        act(ex[:], bk[:, 0:512], AF.Exp, [bk], [ex])
                        exs.append((ex, slot))
                    exd[hk] = exs

                def rest2(hk):
                    exs = exd.pop(hk)
                    bo = self.bankp("pv", [7])
                    def pvf(e, bo=bo, exs=exs, hk=hk):
                        ins = None
                        for hl in range(4):
                            for i, (ex, slot) in enumerate(exs):
                                ins = e.matmul(bo[:, hl * 65:(hl + 1) * 65], lhsT=ex[:, hl * 128:(hl + 1) * 128], rhs=VP[:, slot, hk, :],
                                               start=(i == 0), stop=(i == len(exs) - 1))
                        return ins
                    op("pe", pvf, r=[e_ for e_, _ in exs] + [VPres[sl] for _, sl in exs], w=[bo])
                    bov = bo[:, 0:260].rearrange("p (h c) -> p h c", c=65)
                    dn_ = den2[hk % 2]
                    tt("dve", dn_[:], bov[:, :, 64], esink[:, 4 * hk:4 * hk + 4], ALU.add, [bo, esink], [dn_])
                    op("dve", lambda e, dn_=dn_: e.reciprocal(out=dn_[:], in_=dn_[:]), [dn_], [dn_])
                    tt("dve", ot[:, hk * 256:(hk + 1) * 256].rearrange("p (h d) -> p h d", d=64), bov[:, :, 0:64], ins_b(dn_[:], 2, 64), ALU.mult, [bo, dn_], [ot])

                scores(0)
                rest1(0)
                for hk in range(4):
                    if hk + 1 < 4:
                        scores(hk + 1)
                        rest1(hk + 1)
                    rest2(hk)
                bk3 = self.bankp("qt", [2])
                bk3b = bk3[:].bitcast(BF16)
                def otf(e, bk3b=bk3b, ot=ot):
                    ins = None
                    for dt_ in range(8):
                        ins = e.transpose(out=bk3b[:, dt_ * 128:(dt_ + 1) * 128], in_=ot[:, dt_ * 128:(dt_ + 1) * 128], identity=identb[:])
                    return ins
                op("pe", otf, r=[ot, identb], w=[bk3])
                o_ = OTt[:].rearrange("p t (r k) -> p t r k", r=8)[:, :, :, qb * 16:(qb + 1) * 16]
                i_ = bk3b[:, 0:1024].rearrange("p (t r l) -> p t r l", t=8, r=8)
                cp("act", o_, i_, [bk3], OTd)

            stage_a(0)
            stage_a2(0)
            for qb in range(8):
                if qb + 1 < 8:
                    stage_a(qb + 1)
                stage_b(qb)
                if qb + 1 < 8:
                    stage_a2(qb + 1)
            for j in range(2):
                dma("sp", Wo[j][:], wo[:].rearrange("(kt p) f -> p kt f", p=128)[:, :, j * 512:(j + 1) * 512], r=wo.blk, w=[Wo[j]])
            for r in range(8):
                for j in range(2):
                    bk = self.bank()
                    def of(e, bk=bk, r=r, j=j):
                        ins = None
                        for kt in range(8):
                            ins = e.matmul(bk[:, 0:512], lhsT=OTt[:, kt, r * 128:(r + 1) * 128], rhs=Wo[j][:, kt, :], start=(kt == 0), stop=(kt == 7))
                        return ins
                    op("pe", of, r=OTd + [Wo[j]], w=[bk])
                    tt("dve", Rt[:, r, j * 512:(j + 1) * 512], bk[:, 0:512], Rt[:, r, j * 512:(j + 1) * 512], ALU.add, [bk, Rres[r]], [Rres[r]])

        identf = self.identf
        A1r = self.A1r

        def sample_s5():
            self.P.barrier()
            for t in range(4):
                dma("sp", Rt[t * 16:(t + 1) * 16, 0, :], io["xs"][:, t, :], w=[Rres[0]])
            rmsnorm(0)
            for t in range(4):
                dma("sp", HNs[:, :, t, :], HN[t * 16:(t + 1) * 16, 0, :].rearrange("p (g c) -> p g c", c=16), r=[HNr[0]], w=[HNs])
            def ss1(gb):
                gsl = slice(gb * 8, (gb + 1) * 8)
                csl = slice(gb * 128, (gb + 1) * 128)
                t_ = stb[gb % 2]
                H0T, H0Ts, Fbs, UTs = H0T2[gb % 2], H0Ts2[gb % 2], Fbs2[gb % 2], UTs2[gb % 2]
                for key, src in (("W1", sW1), ("W2", sW2), ("W3", sW3)):
                    dma("sp", t_[key][:], src[:, gsl], r=[src], w=[t_[key]])
                h0, h02 = H0k[gb % 2], H0k2[gb % 2]
                dma("sp", h0[:, :, 0:64], io["st_re"][:, gsl, :], w=[h0])
                dma("sp", h0[:, :, 64:128], io["st_im"][:, gsl, :], w=[h0])
                dma("sp", h02[:, :, 0:64], io["st_im"][:, gsl, :], w=[h02])
                dma("sp", h02[:, :, 64:128], io["st_re"][:, gsl, :], w=[h02])
                for src, dst, eng in ((h0, H0T, "act"), (h02, H0Ts, "dve")):
                    bk = self.bank()
                    def hf_(e, bk=bk, src=src):
                        ins = None
                        for gl in range(8):
                            ins = e.transpose(out=bk[:, gl * 16:(gl + 1) * 16], in_=src[:, gl, :], identity=identf[0:16, 0:16])
                        return ins
                    op("pe", hf_, r=[src, identf], w=[bk])
                    cp(eng, dst[:].rearrange("p g s -> p (g s)"), bk[:, 0:128], [bk], [dst])
                tt("dve", za1[:], H0T[:], ins_b(A1r[:, gsl], 2, 16), ALU.mult, [H0T, A1r], [za1])
                tt("dve", za2[:], H0Ts[:], ins_b(self.A1B[:, gsl], 2, 16), ALU.mult, [H0Ts, self.A1B], [za2])
                tt("pool", Fbs[:], za1[:], za2[:], ALU.add, [za1, za2], [Fbs])
                bk = self.bank()
                bkb = bk[:].bitcast(BF16)
                def uf_(e, bkb=bkb, gb=gb):
                    ins = None
                    for gl in range(8):
                        ins = e.transpose(out=bkb[0:64, gl * 16:(gl + 1) * 16], in_=HNs[:, gb * 8 + gl].rearrange("p t c -> p (t c)"), identity=identb[0:16, 0:16])
                    return ins
                op("pe", uf_, r=[HNs, identb], w=[bk])
                cp("act", UTs[:].rearrange("p g s -> p (g s)"), bkb[0:64, 0:128], [bk], [UTs])
            def ss2(gb):
                gsl = slice(gb * 8, (gb + 1) * 8)
                csl = slice(gb * 128, (gb + 1) * 128)
                t_ = stb[gb % 2]
                H0T, H0Ts, Fbs, UTs = H0T2[gb % 2], H0Ts2[gb % 2], Fbs2[gb % 2], UTs2[gb % 2]
                bk = self.bank()
                def yf_(e, bk=bk, t_=t_):
                    ins = None
                    for gl in range(8):
                        e.matmul(bk[0:16, gl * 64:(gl + 1) * 64], lhsT=Fbs[:, gl, :], rhs=t_["W3"][:, gl, 0:64], start=True, stop=False)
                        ins = e.matmul(bk[0:16, gl * 64:(gl + 1) * 64], lhsT=UTs[:, gl, :], rhs=t_["W2"][0:64, gl, 0:64], start=False, stop=True)
                    return ins
                op("pe", yf_, r=[Fbs, UTs, t_["W3"], t_["W2"]], w=[bk])
                tt("pool", du16[:].rearrange("p t (g c) -> p g t c", g=8), HNs[:, gsl, :, :],
                   ins_b(Dv[0:16, csl].rearrange("p (g c) -> p g c", g=8), 2, 4), ALU.mult, [HNs, Dv], [du16])
                tt("dve", yt16[:].rearrange("p t (g c) -> p g t c", g=8), bk[0:16, 0:512].rearrange("p (g t c) -> p g t c", g=8, t=4),
                   du16[:].rearrange("p t (g c) -> p g t c", g=8), ALU.add, [bk, du16], [yt16])
                act(GLs[:, :, csl], yt16[:], AF.Gelu_apprx_tanh, [yt16], [GLs])
                bx, bxs = self.bank(), self.bank()
                for bk_, c0 in ((bx, 0), (bxs, 64)):
                    def xf_(e, bk_=bk_, c0=c0, t_=t_):
                        ins = None
                        for gl in range(8):
                            ins = e.matmul(bk_[:, gl * 16:(gl + 1) * 16], lhsT=t_["W1"][0:64, gl, c0:c0 + 128], rhs=UTs[:, gl, :], start=True, stop=True)
                        return ins
                    op("pe", xf_, r=[t_["W1"], UTs], w=[bk_])
                bc = lambda t__: ins_b(t__[:, gsl], 2, 16)
                v3 = lambda a: a.rearrange("p (g s) -> p g s", g=8)
                tt("dve", zt1[:], v3(bx[:, 0:128]), bc(self.A3r), ALU.mult, [bx, self.A3r], [zt1])
                tt("dve", zt2[:], v3(bxs[:, 0:128]), bc(self.A3B), ALU.mult, [bxs, self.A3B], [zt2])
                tt("pool", zt1[:], zt1[:], zt2[:], ALU.add, [zt1, zt2], [zt1])
                tt("dve", zt2[:], H0T[:], bc(self.A4r), ALU.mult, [H0T, self.A4r], [zt2])
                tt("pool", zt1[:], zt1[:], zt2[:], ALU.add, [zt1, zt2], [zt1])
                tt("dve", zt2[:], H0Ts[:], bc(self.A4B), ALU.mult, [H0Ts, self.A4B], [zt2])
                tt("pool", Hs_[:], zt1[:], zt2[:], ALU.add, [zt1, zt2], [Hs_])
                for q in range(2):
                    bk = self.bank()
                    def tf_(e, bk=bk, q=q):
                        ins = None
                        for gq in range(4):
                            ins = e.transpose(out=bk[0:16, gq * 128:(gq + 1) * 128], in_=Hs_[:, 4 * q + gq, :], identity=identf[:])
                        return ins
                    op("pe", tf_, r=[Hs_, identf], w=[bk])
                    cp("act", Hout[:, 4 * q:4 * q + 4, :].rearrange("p g n -> p (g n)"), bk[0:16, 0:512], [bk], [Hout])
                dma("sp", io["sre_s"][:, gsl, :], Hout[:, :, 0:64], r=[Hout])
                dma("sp", io["sim_s"][:, gsl, :], Hout[:, :, 64:128], r=[Hout])
            ss1(0)
            for gb in range(8):
                if gb + 1 < 8:
                    ss1(gb + 1)
                ss2(gb)
            for t in range(4):
                dma("sp", GL[t * 16:(t + 1) * 16, 0, :], GLs[:, t, :], r=[GLs], w=[GLres])

        def sample_attn():
            M = 64
            self.P.barrier()
            self.bank_pool = [0, 1, 2, 3]
            wkv, wq, wo = wb["wkv"], wb["wq"], wb["wo"]
            dma("sp", Wkv[:], wkv[:].rearrange("(kt p) f -> p kt f", p=128), r=wkv.blk, w=[Wkv])
            for j in range(2):
                dma("sp", Wq[j][:], wq[:].rearrange("(kt p) f -> p kt f", p=128)[:, :, j * 512:(j + 1) * 512], r=wq.blk, w=[Wq[j]])
            dma("sp", sbc[:], io["c_sbc"].rearrange("p h t -> p (h t)"), w=[sbc])
            dma("sp", sbn[:], io["c_sbn"].rearrange("p k b l t -> p k (b l t)"), w=[sbn])
            op("pool", lambda e: e.memset(onesb[:], 1.0), w=[onesb])
            scale_w(Wkv, 0)
            scale_w(Wq[0], 1)
            scale_w(Wq[1], 1)
            rmsnorm(None)
            to_xt(HN, HNr)
            bk = self.bank()
            def kvf(e, bk=bk):
                ins = None
                for kt in range(8):
                    ins = e.matmul(bk[0:M, 0:512], lhsT=XT[:, kt, 0:M], rhs=Wkv[:, kt, :], start=(kt == 0), stop=(kt == 7))
                return ins
            op("pe", kvf, r=XTd + [Wkv], w=[bk])
            cp("act", Vf[0:M], bk[0:M, 256:512], [bk], [Vf])
            cp("pool", VPn[:], Vf[0:M].rearrange("p (h d) -> p h d", d=64), [Vf], [VPn])
            headnorm([(bk, 256)], 4, Kb, 0, outf=Kf)
            dma("sp", io["kb_s"][:, 0:124], io["ck"][:, 4:128])
            dma("sp", io["vb_s"][:, 0:124], io["cv"][:, 4:128])
            for t in range(4):
                dma("sp", io["kb_s"][:, 124 + t].rearrange("b h d -> b (h d)"), Kf[t * 16:(t + 1) * 16, :], r=[Kf])
                dma("sp", io["vb_s"][:, 124 + t].rearrange("b h d -> b (h d)"), Vf[t * 16:(t + 1) * 16, :], r=[Vf])
            bk2 = self.bank()
            bk2b = bk2[:].bitcast(BF16)
            def ktf(e, bk2b=bk2b):
                ins = None
                for hk in range(4):
                    ins = e.transpose(out=bk2b[0:64, hk * 64:(hk + 1) * 64], in_=Kb[0:M, hk * 64:(hk + 1) * 64], identity=identb[0:M, 0:M])
                return ins
            op("pe", ktf, r=[Kb, identb], w=[bk2])
            act(KTn[:].rearrange("p h k -> p (h k)"), bk2b[0:64, 0:256], AF.Copy, [bk2, gqk], [KTn], scale=gqk[:])
            bq = [self.bank(), self.bank()]
            for j in range(2):
                def qf(e, bk=bq[j], j=j):
                    ins = None
                    for kt in range(8):
                        ins = e.matmul(bk[0:M, 0:512], lhsT=XT[:, kt, 0:M], rhs=Wq[j][:, kt, :], start=(kt == 0), stop=(kt == 7))
                    return ins
                op("pe", qf, r=XTd + [Wq[j]], w=[bq[j]])
            headnorm([(bq[0], 512), (bq[1], 512)], 16, Qb, 0)
            bk2 = self.bank()
            bk2b = bk2[:].bitcast(BF16)
            def qtf(e, bk2b=bk2b):
                ins = None
                for h in range(16):
                    ins = e.transpose(out=bk2b[0:64, h * 64:(h + 1) * 64], in_=Qb[0:M, h * 64:(h + 1) * 64], identity=identb[0:M, 0:M])
                return ins
            op("pe", qtf, r=[Qb, identb], w=[bk2])
            cp("act", QTs[:].rearrange("p h k -> p (h k)"), bk2b[0:64, 0:1024], [bk2], [QTs])
            for hk in range(4):
                cp("pool", QTs2[:, hk, :].rearrange("p (b l t) -> p b l t", b=16, l=4),
                   QTs[:, 4 * hk:4 * hk + 4, :].rearrange("p l (t b) -> p b l t", b=16), [QTs], [QTs2])
            for hp in range(2):
                bk = self.bank()
                def snf(e, bk=bk, hp=hp):
                    ins = None
                    for q in range(2):
                        hk = 2 * hp + q
                        ins = e.matmul(bk[0:64, q * 256:(q + 1) * 256], lhsT=KTn[:, hk, :], rhs=QTs2[:, hk, :], start=True, stop=True)
                    return ins
                op("pe", snf, r=[KTn, QTs2], w=[bk])
                tt("dve", dn[:, :], bk[0:64, 0:512], sbn[:, 2 * hp:2 * hp + 2, :].rearrange("p k c -> p (k c)"), ALU.add, [bk, sbn], [dn])
                act(exn[:, 2 * hp:2 * hp + 2, :].rearrange("p k c -> p (k c)"), dn[:, :], AF.Exp, [dn], [exn])
            NUM = [self.banks[4], self.banks[5]]
            DEN = [self.banks[6], self.banks[7]]
            def sp_stage(b):
                kc, vc, vpc, ktc, ex_ = Kc[b % 2], Vc[b % 2], VPc[b % 2], KTc[b % 2], exc[b % 2]
                kcb_ = Kcb2[b % 2]
                dma("sp", kc[:], io["ck"][b].rearrange("j h d -> j (h d)"), w=[kc])
                dma("sp", vc[:], io["cv"][b].rearrange("j h d -> j (h d)"), w=[vc])
                cp("pool", kcb_[:], kc[:], [kc], [kcb_])
                cp("pool", vpc[:].rearrange("p h d -> p (h d)"), vc[:], [vc], [vpc])
                bk2 = self.bank()
                bk2b = bk2[:].bitcast(BF16)
                def kcf(e, bk2b=bk2b, kcb_=kcb_):
                    ins = None
                    for hk in range(4):
                        ins = e.transpose(out=bk2b[0:64, hk * 128:(hk + 1) * 128], in_=kcb_[:, hk * 64:(hk + 1) * 64], identity=identb[:])
                    return ins
                op("pe", kcf, r=[kcb_, identb], w=[bk2])
                act(ktc[:].rearrange("p h k -> p (h k)"), bk2b[0:64, 0:512], AF.Copy, [bk2, gq8], [ktc], scale=gq8[:])
                bk = self.bank()
                def scf_(e, bk=bk, ktc=ktc, b=b):
                    ins = None
                    for hk in range(4):
                        ins = e.matmul(bk[:, hk * 16:(hk + 1) * 16], lhsT=ktc[:, hk, :], rhs=QTs2[:, hk, b * 16:(b + 1) * 16], start=True, stop=True)
                    return ins
                op("pe", scf_, r=[ktc, QTs2], w=[bk])
                sc_ = scf[b % 2]
                tt("dve", sc_[:, 0:64], bk[:, 0:64], sbc[:], ALU.add, [bk, sbc], [sc_])
                act(ex_[:], sc_[:, 0:64], AF.Exp, [sc_], [ex_])

            def sv_stage(b):
                vpc, ex_ = VPc[b % 2], exc[b % 2]
                nb_, db_ = NUM[b // 8], DEN[b // 8]
                def pvf(e, nb_=nb_, db_=db_, vpc=vpc, ex_=ex_, b=b):
                    ins = None
                    for hk in range(4):
                        c0 = (b % 8) * 64 + hk * 16
                        e.matmul(nb_[0:64, c0:c0 + 16], lhsT=vpc[:, hk, :], rhs=ex_[:, hk * 16:(hk + 1) * 16], start=True, stop=False)
                        e.matmul(nb_[0:64, c0:c0 + 16], lhsT=VPn[:, hk, :], rhs=exn[:, hk, b * 16:(b + 1) * 16], start=False, stop=True)
                        e.matmul(db_[0:64, c0:c0 + 16], lhsT=onesb[:, 0:64], rhs=ex_[:, hk * 16:(hk + 1) * 16], start=True, stop=False)
                        ins = e.matmul(db_[0:64, c0:c0 + 16], lhsT=onesb[0:64, 0:64], rhs=exn[:, hk, b * 16:(b + 1) * 16], start=False, stop=True)
                    return ins
                op("pe", pvf, r=[vpc, ex_, VPn, exn, onesb], w=[nb_, db_])

            sp_stage(0)
            for b in range(16):
                if b + 1 < 16:
                    sp_stage(b + 1)
                sv_stage(b)
            for half in range(2):
                nb_, db_ = NUM[half], DEN[half]
                dn4 = dn[:, :].rearrange("p (b h t) -> p b h t", b=8, h=16)
                es4 = ins_b(ins_b(esink[0:64, 0:16], 1, 8), 3, 4)
                tt("dve", dn4, db_[0:64, 0:512].rearrange("p (b h t) -> p b h t", b=8, h=16), es4, ALU.add, [db_, esink], [dn])
                op("dve", lambda e: e.reciprocal(out=dn[:, :], in_=dn[:, :]), [dn], [dn])
                o4 = OTs[:].rearrange("p h (t b) -> p b h t", b=16)[:, half * 8:(half + 1) * 8]
                tt("dve", o4, nb_[0:64, 0:512].rearrange("p (b h t) -> p b h t", b=8, h=16), dn4, ALU.mult, [nb_, dn], [OTs])
            wov = wo[:].rearrange("(h p) f -> p h f", p=64)
            for j in range(2):
                dma("sp", WoS[:], wov[:, :, j * 512:(j + 1) * 512], r=wo.blk, w=[WoS])
                bk = self.bank()
                def of(e, bk=bk):
                    ins = None
                    for h in range(16):
                        ins = e.matmul(bk[0:M, 0:512], lhsT=OTs[:, h, :], rhs=WoS[:, h, :], start=(h == 0), stop=(h == 15))
                    return ins
                op("pe", of, r=[OTs, WoS], w=[bk])
                tt("dve", Rt[0:M, 0, j * 512:(j + 1) * 512], bk[0:M, 0:512], Rt[0:M, 0, j * 512:(j + 1) * 512], ALU.add, [bk, Rres[0]], [Rres[0]])
            self.bank_pool = None

        nchunks = self.dbg.get("nchunks", 4)
        if "endprobe" in self.dbg:
            self.dbgE = self.dout("dbg_E", [128, 64])
        if "l0" in self.dbg:
            self.dbgR = self.dout("dbg_R", [2, 128, 8, D])
        for ch in range(nchunks):
            b_, c_ = ch // 2, ch % 2
            xv = xp[b_, c_ * 1024:(c_ + 1) * 1024, :].rearrange("(k r) d -> k r d", r=8)
            for r in range(8):
                dma("sp", Rt[:, r, :], xv[:, r, :], w=[Rres[r]])
            rmsnorm(0, gmajor=True)
            zin, zsin = zc[ch % 2], zcs[ch % 2]
            zout, zsout = zc[(ch + 1) % 2], zcs[(ch + 1) % 2]
            if c_ == 0:
                op("pool", lambda e, z=zin: e.memset(z[:], 0.0), w=[zin])
                op("pool", lambda e, z=zsin: e.memset(z[:], 0.0), w=[zsin])
            def s1(gb):
                gsl = slice(gb * 8, (gb + 1) * 8)
                t_ = tb[gb % 2]
                for key, src in (("W1", sW1), ("W2", sW2), ("W3", sW3), ("Pr", sPr), ("Pi", sPi), ("Qr", sQr), ("QB", sQB)):
                    dma("sp", t_[key][:], src[:, gsl], r=[src], w=[t_[key]])
                ut, zb, htb = UTb[gb % 2], Zb[gb % 2], HTb[gb % 2]
                bk = self.bankp("s5ut", [0])
                bkb = bk[:].bitcast(BF16)
                def utf(e, bkb=bkb, gb=gb):
                    ins = None
                    for gl in range(8):
                        gi = gb * 8 + gl
                        ins = e.transpose(out=bkb[:, gl * 128:(gl + 1) * 128], in_=HNg[:, gi].rearrange("p s c -> p (s c)"), identity=identb[:])
                    return ins
                op("pe", utf, r=HNr + [identb], w=[bk])
                cp("act", ut[:, 0:4, :].rearrange("p g k -> p (g k)"), bkb[:, 0:512], [bk], [utres[gb % 2][0]])
                cp("dve", ut[:, 4:8, :].rearrange("p g k -> p (g k)"), bkb[:, 512:1024], [bk], [utres[gb % 2][1]])
                for j in range(4):
                    bk = self.bankp("s5x", [1, 2])
                    def xf(e, bk=bk, j=j, ut=ut, t_=t_):
                        ins = None
                        for q in range(2):
                            gl = 2 * j + q
                            ins = e.matmul(bk[:, q * 256:(q + 1) * 256], lhsT=ut[:, gl, :], rhs=t_["W1"][:, gl, :], start=True, stop=True)
                        return ins
                    op("pe", xf, r=[utres[gb % 2][j // 2], t_["W1"]], w=[bk])
                    Xv = bk[:].rearrange("p (g c) -> p g c", g=2)
                    ta, tb_ = tA[j % 2], tB[j % 2]
                    x0 = Xv[:, :, 0:192].rearrange("p g (a n) -> p g a n", a=3)
                    x1 = Xv[:, :, 64:256].rearrange("p g (a n) -> p g a n", a=3)
                    prb = ins_b(t_["Pr"][:, 2 * j:2 * j + 2, :], 2, 3)
                    pib = ins_b(t_["Pi"][:, 2 * j:2 * j + 2, :], 2, 3)
                    ta4 = ta[:].rearrange("p g (a n) -> p g a n", a=3)
                    tb4 = tb_[:].rearrange("p g (a n) -> p g a n", a=3)
                    tt("dve", ta4, x0, prb, ALU.mult, [bk, t_["Pr"]], [ta])
                    tt("dve", tb4, x1, pib, ALU.mult, [bk, t_["Pi"]], [tb_])
                    z4 = zb[:, 2 * j:2 * j + 2, :].rearrange("p g (a n) -> p g a n", a=3)
                    def sl(v, lo, step, cnt):
                        return bass.AP(v.tensor, v.offset + lo * v.ap[2][0], [list(v.ap[0]), list(v.ap[1]), [step * v.ap[2][0], cnt], list(v.ap[3])])
                    tt("pool", sl(z4, 0, 2, 2), sl(ta4, 0, 2, 2), sl(tb4, 0, 2, 2), ALU.subtract, [ta, tb_], [zb])
                    tt("pool", sl(z4, 1, 1, 1), sl(ta4, 1, 1, 1), sl(tb4, 1, 1, 1), ALU.add, [ta, tb_, zb], [zb])
            def s23(gb):
                gsl = slice(gb * 8, (gb + 1) * 8)
                t_ = tb[gb % 2]
                ut, zb, htb = UTb[gb % 2], Zb[gb % 2], HTb[gb % 2]
                for gl in range(8):
                    gi = gb * 8 + gl
                    bk = self.bankp("s5c", [3, 4, 5])
                    def cf(e, bk=bk, gl=gl, zb=zb):
                        e.matmul(bk[:, 0:129], lhsT=zb[:, gl, 0:128], rhs=trib[:, 0:129], start=True, stop=True)
                        return e.matmul(bk[:, 256:385], lhsT=zb[:, gl, 64:192], rhs=trib[:, 0:129], start=True, stop=True)
                    op("pe", cf, r=[zb, trib], w=[bk])
                    stt(t1b[:, gl, :], bk[:, 0:129], zin[:, gi:gi + 1], t_["Qr"][:, gl, :], ALU.add, ALU.mult, [bk, zin, t_["Qr"]], [t1res[gl]])
                    stt(t2b[:, gl, :], bk[:, 256:385], zsin[:, gi:gi + 1], t_["QB"][:, gl, :], ALU.add, ALU.mult, [bk, zsin, t_["QB"]], [t2res[gl]])
                tt("dve", htb[:, 0:4, :], t1b[:, 0:4, 0:128], t2b[:, 0:4, 0:128], ALU.add, t1res[0:4] + t2res[0:4], [hres[gb % 2][0]])
                tt("pool", htb[:, 4:8, :], t1b[:, 4:8, 0:128], t2b[:, 4:8, 0:128], ALU.add, t1res[4:8] + t2res[4:8], [hres[gb % 2][1]])
                tt("pool", zout[:, gsl], t1b[:, :, 128], t2b[:, :, 128], ALU.add, t1res + t2res, [zout])
                du_, yt = du[gb % 2], ytmp[gb % 2]
                csl = slice(gb * 128, (gb + 1) * 128)
                tt("pool", du_[:].rearrange("p t (g c) -> p g t c", g=8), HNg[:, gsl, :, :],
                   ins_b(Dv[:, csl].rearrange("p (g c) -> p g c", g=8), 2, 8), ALU.mult, HNr + [Dv], [du_])
                for q in range(2):
                    bk = self.bankp("s5y", [6, 7])
                    def yf(e, bk=bk, q=q, htb=htb, ut=ut, t_=t_):
                        ins = None
                        for gq in range(4):
                            gl = 4 * q + gq
                            e.matmul(bk[:, gq * 128:(gq + 1) * 128], lhsT=htb[:, gl, :], rhs=t_["W3"][:, gl, :], start=True, stop=False)
                            ins = e.matmul(bk[:, gq * 128:(gq + 1) * 128], lhsT=ut[:, gl, :], rhs=t_["W2"][:, gl, :], start=False, stop=True)
                        return ins
                    op("pe", yf, r=[hres[gb % 2][q], utres[gb % 2][q], t_["W3"], t_["W2"]], w=[bk])
                    bv = bk[:].rearrange("p (g t c) -> p g t c", g=4, t=8)
                    yv = yt[:, :, q * 64:(q + 1) * 64].rearrange("p t (g c) -> p g t c", g=4)
                    dv = du_[:, :, q * 64:(q + 1) * 64].rearrange("p t (g c) -> p g t c", g=4)
                    tt("dve", yv, bv, dv, ALU.add, [bk, du_], [yt])
                if dbgY is not None and ch < 2:
                    dma("sp", dbgY[ch, :, :, csl], yt[:], r=[yt])
                act(GL[:, :, csl], yt[:], AF.Gelu_apprx_tanh, [yt], [GLres])
            s1(0)
            for gb in range(8):
                if gb + 1 < 8:
                    s1(gb + 1)
                s23(gb)
            bkT = self.bankp("s5ut", [0])
            op("pe", lambda e, bkT=bkT, zout=zout: e.transpose(out=bkT[0:64, 0:128], in_=zout[:], identity=identf[:]), r=[zout, identf], w=[bkT])
            cp("act", zsw[0:64, 0:64], bkT[0:64, 64:128], [bkT], [zsw])
            cp("act", zsw[0:64, 64:128], bkT[0:64, 0:64], [bkT, zsw], [zsw])
            bkT2 = self.bankp("s5x", [1, 2])
            op("pe", lambda e, bkT2=bkT2: e.transpose(out=bkT2[:, 0:64], in_=zsw[0:64, :], identity=identf[0:64, 0:64]), r=[zsw, identf], w=[bkT2])
            cp("act", zsout[:], bkT2[:, 0:64], [bkT2], [zsout])
            if c_ == 1:
                tt("dve", Hfin[:], zout[:], AinvR[:], ALU.mult, [zout, AinvR], [Hfin])
                tt("dve", Ht1[:], zsout[:], AinvB[:], ALU.mult, [zsout, AinvB], [Ht1])
                tt("dve", Hfin[:], Hfin[:], Ht1[:], ALU.add, [Hfin, Ht1], [Hfin])
                bk = self.bank()
                op("pe", lambda e, bk=bk: e.transpose(out=bk[0:64, 0:128], in_=Hfin[:], identity=self.identf[:]), r=[Hfin, self.identf], w=[bk])
                cp("act", Ht2[0:64, :], bk[0:64, 0:64], [bk], [Ht2])
                cp("act", Ht1[0:64, :], bk[0:64, 64:128], [bk, Ht1], [Ht1])
                dma("sp", io["sre_p"][b_], Ht2[0:64, :], r=[Ht2])
                dma("sp", io["sim_p"][b_], Ht1[0:64, :], r=[Ht1])
            if dbgF is not None and ch < 2:
                dma("sp", dbgF[ch], zout[:], r=[zout])
            stop = self.dbg.get("stop")
            if "endprobe" in self.dbg and ch == nchunks - 1 and stop is not None:
                self.P.barrier()
                dma("sp", self.dbgE[:], Rt[:, 0, 0:64], r=[Rres[0]])
            if stop == "s5":
                continue
            glu_phase()
            if stop == "glu":
                continue
            mlp_phase(0)
            if "l0" in self.dbg and ch < 2:
                for r in range(8):
                    dma("sp", self.dbgR[ch, :, r, :], Rt[:, r, :], r=[Rres[r]])
            if stop == "l0":
                self.P.barrier()
                continue
            attn_phase(b_, c_)
            mlp_phase(1)
            yv = io["yp"][b_, c_ * 1024:(c_ + 1) * 1024, :].rearrange("(k r) d -> k r d", r=8)
            for r in range(8):
                dma("sp", yv[:, r, :], Rt[:, r, :], r=[Rres[r]])
            self.P.barrier()
        if self.dbg.get("sample", True):
            cx["M"], cx["NT"] = 64, 1
            sample_s5()
            glu_phase()
            mlp_phase(0)
            sample_attn()
            mlp_phase(1)
            for t in range(4):
                dma("sp", io["ys"][:, t, :], Rt[t * 16:(t + 1) * 16, 0, :], r=[Rres[0]])

def make_consts():
    c = {}
    c["c_ident"] = np.eye(128, dtype=np.float32)
    j = np.arange(128)[:, None]
    k = np.arange(129)[None, :]
    c["c_tri"] = (j < k).astype(np.float32)
    p = np.arange(128)
    s_ = p // 16
    c["c_mask2"] = (s_[None, :] >= s_[:, None]).astype(np.float32)
    c["c_kcol"] = np.arange(128, dtype=np.float32)[:, None]
    c["c_ramp"] = np.tile(np.arange(129, dtype=np.float32)[None, :], (128, 1))
    tok = 8 * (p % 16) + (p // 16)
    slopes = 2.0 ** (-8.0 * np.arange(1, 17, dtype=np.float64) / 16)
    ab = np.zeros((128, 2, 16, 128), np.float32)
    for kb in range(2):
        dist = tok[None, :] - tok[:, None] + (128 if kb == 0 else 0)
        valid = (dist >= 0) & (dist < 128)
        for h in range(16):
            ab[:, kb, h, :] = np.where(valid, -slopes[h] * dist, -30000.0)
    import ml_dtypes
    ab_hi = ab.astype(ml_dtypes.bfloat16)
    ab_lo = (ab - ab_hi.astype(np.float32)).astype(ml_dtypes.bfloat16)
    c["c_abias"] = np.stack([ab_hi, ab_lo], axis=1)
    sbc = np.zeros((128, 16, 4), np.float32)
    jj = np.arange(128)
    for t in range(4):
        dist = t + 128 - jj
        valid = (dist >= 0) & (dist < 128)
        for h in range(16):
            sbc[:, h, t] = np.where(valid, -slopes[h] * dist, -30000.0)
    c["c_sbc"] = sbc
    sbn = np.full((64, 4, 16, 4, 4), -30000.0, np.float32)
    for tp in range(4):
        for sp_ in range(16):
            for hk in range(4):
                for hl in range(4):
                    for t in range(tp, 4):
                        sbn[tp * 16 + sp_, hk, sp_, hl, t] = -slopes[4 * hk + hl] * (t - tp)
    c["c_sbn"] = sbn
    return c


def build_program(dbg=None):
    b = Builder(dbg)
    b.P.emit(b.nc)
    return b


_CACHE = {}


def kernel(**inputs):
    if "b" not in _CACHE:
        _CACHE["b"] = build_program()
    b = _CACHE["b"]
    consts = make_consts()
    f32 = lambda a: np.ascontiguousarray(np.asarray(a, dtype=np.float32))
    in_maps = []
    for c in range(NCORES):
        m = dict(consts)
        for k, v in inputs.items():
            m[k] = f32(v)
        m["xp"] = f32(inputs["x_prompt"][2 * c:2 * c + 2])
        m["xs"] = f32(inputs["x_sample"][16 * c:16 * c + 16])
        m["st_re"] = f32(inputs["state_ssm_re"][0, 16 * c:16 * c + 16])
        m["st_im"] = f32(inputs["state_ssm_im"][0, 16 * c:16 * c + 16])
        m["ck"] = f32(inputs["cache_k"][16 * c:16 * c + 16])
        m["cv"] = f32(inputs["cache_v"][16 * c:16 * c + 16])
        in_maps.append({k: v for k, v in m.items() if k in b.dram})
    res = run_bass_kernel_spmd(b.nc, in_maps, core_ids=list(range(NCORES)))
    rs = res.results

    def cat(name, shape):
        if name in rs[0]:
            return np.concatenate([np.asarray(r[name], dtype=np.float32) for r in rs], axis=0)
        return np.zeros(shape, np.float32)
    y_prompt = cat("yp", (16, 2048, D))
    y_sample = cat("ys", (128, 4, D))
    re_p = cat("sre_p", (16, NG, 64))[None]
    im_p = cat("sim_p", (16, NG, 64))[None]
    k_p = cat("kb_p", (16, 128, 4, 64))
    v_p = cat("vb_p", (16, 128, 4, 64))
    re_s = cat("sre_s", (128, NG, 64))[None]
    im_s = cat("sim_s", (128, NG, 64))[None]
    k_s = cat("kb_s", (128, 128, 4, 64))
    v_s = cat("vb_s", (128, 128, 4, 64))
    return (y_prompt, y_sample, re_p, im_p, k_p, v_p, re_s, im_s, k_s, v_s)
```
